# Optimizing a Trainium2 kernel written in Bass

```python
import math
import jax, jax.numpy as jnp
from jax import lax
import numpy as np

D_MODEL = 1024
BATCH = 4
SEQ = 8192
DEPTH = 4
DEC_BATCH = 32
DEC_SEQ = 2048
PAST_LEN = 128

GRID_W = 64
HEAD_DIM = 64
N_Q_HEADS = 8
N_KV_HEADS = 2
Q_PER_KV = N_Q_HEADS // N_KV_HEADS
Q_WIDTH = N_Q_HEADS * HEAD_DIM
KV_WIDTH = N_KV_HEADS * HEAD_DIM
SG_GROUPS = 8
SG_GROUP_DIM = 64
SG_WIDTH = SG_GROUPS * SG_GROUP_DIM
CHUNK = 128
Q_BLOCK = 128
D_FF = 2048
CONV_W = 3
ROPE_THETA = 10000.0
EPS = 1e-6
IN_WIDTH = Q_WIDTH + 2 * KV_WIDTH + 2 * SG_WIDTH + 2 * D_MODEL

kernel_name = "hybrid_gqa_gmlp_convffn_encoder"


def rmsnorm(x, g):
    xf = x.astype(jnp.float32)
    y = xf * lax.rsqrt(jnp.mean(xf * xf, axis=-1, keepdims=True) + EPS)
    return (y * g.astype(jnp.float32)).astype(x.dtype)


def axial_rope_tables(seq_len, dtype):
    t = jnp.arange(seq_len)
    row = (t // GRID_W).astype(jnp.float32)
    col = (t % GRID_W).astype(jnp.float32)
    half = HEAD_DIM // 2
    freq = ROPE_THETA ** (-jnp.arange(0, half, 2, dtype=jnp.float32) / half)
    ang_r = row[:, None] * freq[None, :]
    ang_c = col[:, None] * freq[None, :]
    return (jnp.cos(ang_r).astype(dtype), jnp.sin(ang_r).astype(dtype),
            jnp.cos(ang_c).astype(dtype), jnp.sin(ang_c).astype(dtype))


def _rot_half(x, cos, sin):
    x1, x2 = jnp.split(x, 2, axis=-1)
    c = cos[:, None, :]
    s = sin[:, None, :]
    return jnp.concatenate([x1 * c - x2 * s, x1 * s + x2 * c], axis=-1)


def apply_axial_rope(x, tabs):
    cr, sr, cc, sc = tabs
    xr, xc = jnp.split(x, 2, axis=-1)
    return jnp.concatenate([_rot_half(xr, cr, sr), _rot_half(xc, cc, sc)], axis=-1)


def gqa_attention(q, k, v):
    b, s, _, d = q.shape
    nb = s // Q_BLOCK
    scale = 1.0 / math.sqrt(HEAD_DIM)
    qb = q.reshape(b, nb, Q_BLOCK, N_KV_HEADS, Q_PER_KV, d).transpose(1, 0, 2, 3, 4, 5)

    def block(qblk):
        sc = jnp.einsum('bqgrd,bkgd->bgrqk', qblk, k).astype(jnp.float32) * scale
        p = jax.nn.softmax(sc, axis=-1).astype(v.dtype)
        return jnp.einsum('bgrqk,bkgd->bqgrd', p, v)

    o = lax.map(block, qb)
    return o.transpose(1, 0, 2, 3, 4, 5).reshape(b, s, Q_WIDTH)


def spatial_gating(u, vs, sg_norm_g, sg_w, sg_b):
    b, s, _ = u.shape
    n = s // CHUNK
    vs = rmsnorm(vs, sg_norm_g).reshape(b, n, CHUNK, SG_GROUPS, SG_GROUP_DIM)
    sp = jnp.einsum('gpq,bnqgc->bnpgc', sg_w, vs) + jnp.swapaxes(sg_b, 0, 1)[:, :, None]
    return (u.reshape(b, n, CHUNK, SG_GROUPS, SG_GROUP_DIM) * sp).reshape(b, s, SG_WIDTH)


def token_mixer(h, tabs, w_in, q_norm_g, k_norm_g, sg_norm_g, sg_w, sg_b,
                w_branch_a, w_branch_b, w_mix_out):
    b, s, _ = h.shape
    z = h @ w_in
    q, k, v, u, vs, ga, gb = jnp.split(z, np.cumsum(
        [Q_WIDTH, KV_WIDTH, KV_WIDTH, SG_WIDTH, SG_WIDTH, D_MODEL]).tolist(), axis=-1)
    q = rmsnorm(q.reshape(b, s, N_Q_HEADS, HEAD_DIM), q_norm_g)
    k = rmsnorm(k.reshape(b, s, N_KV_HEADS, HEAD_DIM), k_norm_g)
    v = v.reshape(b, s, N_KV_HEADS, HEAD_DIM)
    q = apply_axial_rope(q, tabs)
    k = apply_axial_rope(k, tabs)
    ya = gqa_attention(q, k, v) @ w_branch_a
    yb = spatial_gating(jax.nn.gelu(u), jax.nn.gelu(vs), sg_norm_g, sg_w, sg_b) @ w_branch_b
    m = jax.nn.sigmoid(ga) * ya + jax.nn.sigmoid(gb) * yb
    return m @ w_mix_out


def conv_ffn(h, w_up, conv_w, conv_b, w_down):
    a = h @ w_up
    s = a.shape[1]
    ap = jnp.pad(a, ((0, 0), (CONV_W // 2, CONV_W // 2), (0, 0)))
    c = ap[:, 0:s] * conv_w[0] + ap[:, 1:s + 1] * conv_w[1] + ap[:, 2:s + 2] * conv_w[2] + conv_b
    g, val = jnp.split(c, 2, axis=-1)
    return (jax.nn.gelu(g) * val) @ w_down


def trunk(x, attn_norm_g, w_in, q_norm_g, k_norm_g, sg_norm_g, sg_w, sg_b,
          w_branch_a, w_branch_b, w_mix_out, ffn_norm_g, w_up, conv_w, conv_b,
          w_down, final_norm_g):
    tabs = axial_rope_tables(x.shape[1], x.dtype)
    for l in range(DEPTH):
        x = x + token_mixer(rmsnorm(x, attn_norm_g[l]), tabs, w_in[l], q_norm_g[l],
                            k_norm_g[l], sg_norm_g[l], sg_w[l], sg_b[l],
                            w_branch_a[l], w_branch_b[l], w_mix_out[l])
        x = x + conv_ffn(rmsnorm(x, ffn_norm_g[l]), w_up[l], conv_w[l], conv_b[l], w_down[l])
    return rmsnorm(x, final_norm_g)


def setup_inputs(seed: int = 0) -> dict:
    key = jax.random.key(seed)
    ks = jax.random.split(key, 20)
    f32 = jnp.float32

    def nrm(k, shape, fan_in):
        return jax.random.normal(k, shape, f32) * (fan_in ** -0.5)

    def gain(k, shape):
        return 1.0 + 0.02 * jax.random.normal(k, shape, f32)

    return {
        "x_prompt": jax.random.normal(ks[0], (BATCH, SEQ, D_MODEL), f32),
        "x_sample": jax.random.normal(ks[1], (DEC_BATCH, DEC_SEQ, D_MODEL), f32),
        "attn_norm_g": gain(ks[2], (DEPTH, D_MODEL)),
        "w_in": nrm(ks[3], (DEPTH, D_MODEL, IN_WIDTH), D_MODEL),
        "q_norm_g": gain(ks[4], (DEPTH, HEAD_DIM)),
        "k_norm_g": gain(ks[5], (DEPTH, HEAD_DIM)),
        "sg_norm_g": gain(ks[6], (DEPTH, SG_WIDTH)),
        "sg_w": nrm(ks[7], (DEPTH, SG_GROUPS, CHUNK, CHUNK), CHUNK),
        "sg_b": gain(ks[8], (DEPTH, SG_GROUPS, CHUNK)),
        "w_branch_a": nrm(ks[9], (DEPTH, Q_WIDTH, D_MODEL), Q_WIDTH),
        "w_branch_b": nrm(ks[10], (DEPTH, SG_WIDTH, D_MODEL), SG_WIDTH),
        "w_mix_out": nrm(ks[11], (DEPTH, D_MODEL, D_MODEL), D_MODEL),
        "ffn_norm_g": gain(ks[12], (DEPTH, D_MODEL)),
        "w_up": nrm(ks[13], (DEPTH, D_MODEL, 2 * D_FF), D_MODEL),
        "conv_w": nrm(ks[14], (DEPTH, CONV_W, 2 * D_FF), CONV_W),
        "conv_b": 0.02 * jax.random.normal(ks[15], (DEPTH, 2 * D_FF), f32),
        "w_down": nrm(ks[16], (DEPTH, D_FF, D_MODEL), D_FF),
        "final_norm_g": gain(ks[17], (D_MODEL,)),
    }


def reference(x_prompt, x_sample, attn_norm_g, w_in, q_norm_g, k_norm_g, sg_norm_g,
              sg_w, sg_b, w_branch_a, w_branch_b, w_mix_out, ffn_norm_g, w_up, conv_w,
              conv_b, w_down, final_norm_g):
    y_prompt = trunk(x_prompt, attn_norm_g, w_in, q_norm_g, k_norm_g, sg_norm_g, sg_w, sg_b,
                     w_branch_a, w_branch_b, w_mix_out, ffn_norm_g, w_up, conv_w, conv_b,
                     w_down, final_norm_g)
    y_sample = trunk(x_sample, attn_norm_g, w_in, q_norm_g, k_norm_g, sg_norm_g, sg_w, sg_b,
                     w_branch_a, w_branch_b, w_mix_out, ffn_norm_g, w_up, conv_w, conv_b,
                     w_down, final_norm_g)
    return (y_prompt, y_sample)
```

```python
import contextlib
import math
import types
import numpy as np
import concourse.bass as bass
import concourse.mybir as mybir
from concourse.bass_utils import run_bass_kernel_spmd

F32 = mybir.dt.float32
BF16 = mybir.dt.bfloat16
AF = mybir.ActivationFunctionType
ALU = mybir.AluOpType

ENGS = ("pe", "act", "dve", "pool", "sp")
SEM_EPOCH = 30000

D = 1024
KC = 8
HD = 64
EPS = 1e-6
NG_MAIN = 23
G_Q, G_U, G_VS, G_GA0, G_GB0, G_WA, G_WB, G_GA1, G_GB1, G_MIX, G_UP, G_DN = 0, 1, 2, 3, 4, 5, 6, 7, 8, 9, 11, 19
PP_L = 8 + 8 + 1 + 1 + 128


class Prog:
    def __init__(self, nc):
        self.nc = nc
        self.ops = []
        self.last_w = {}
        self.readers = {}
        self.eng_count = {e: 0 for e in ENGS}

    @staticmethod
    def _freeze(fn):
        if fn.__closure__ is None:
            return fn
        cells = []
        for c in fn.__closure__:
            try:
                cells.append(types.CellType(c.cell_contents))
            except ValueError:
                cells.append(c)
        return types.FunctionType(fn.__code__, fn.__globals__, fn.__name__, fn.__defaults__, tuple(cells))

    def _key(self, oid):
        o = self.ops[oid]
        return ("dma", o["dma"]) if o["dma"] is not None else ("eng", o["eng"])

    def op(self, eng, fn, reads=(), writes=(), dma=None):
        oid = len(self.ops)
        deps = {}

        def add(d):
            k = self._key(d)
            if deps.get(k, -1) < d:
                deps[k] = d
        for r in reads:
            if r in self.last_w:
                add(self.last_w[r])
        for w in writes:
            if w in self.last_w:
                add(self.last_w[w])
            for rd in self.readers.get(w, {}).values():
                add(rd)
        for w in writes:
            self.last_w[w] = oid
            self.readers[w] = {}
        self.ops.append(dict(eng=eng, fn=self._freeze(fn), deps=set(deps.values()), dma=dma, lidx=self.eng_count[eng]))
        k = self._key(oid)
        for r in reads:
            self.readers.setdefault(r, {})[k] = oid
        self.eng_count[eng] += 1
        return oid

    @staticmethod
    def _skippable(do, o):
        if do["dma"] is not None or o["dma"] is not None or do["eng"] != o["eng"]:
            return False
        if o["eng"] == "pe":
            return True
        return o["lidx"] - do["lidx"] >= 3

    def emit(self):
        nc = self.nc
        ops = self.ops
        need_inc = [False] * len(ops)
        for o in ops:
            for d in o["deps"]:
                if not self._skippable(ops[d], o):
                    need_inc[d] = True
        dma_keys = []
        for i, o in enumerate(ops):
            if o["dma"] is not None:
                need_inc[i] = True
                if o["dma"] not in dma_keys:
                    dma_keys.append(o["dma"])
        n_inc_eng = {e: 0 for e in ENGS}
        for i, o in enumerate(ops):
            if o["dma"] is None and need_inc[i]:
                n_inc_eng[o["eng"]] += 1
        sem_names = []
        for e in ENGS:
            for k in range((n_inc_eng[e] + SEM_EPOCH - 1) // SEM_EPOCH):
                sem_names.append(("eng", e, k))
        for k in dma_keys:
            sem_names.append(("dma", k))
        self.n_sems = len(sem_names)
        stack = contextlib.ExitStack()
        sems = {}
        for i, sn in enumerate(sem_names):
            sems[sn] = stack.enter_context(nc.semaphore("s%d" % i))
        cnt_eng = {e: 0 for e in ENGS}
        cnt_dma = {k: 0 for k in dma_keys}
        for i, o in enumerate(ops):
            if not need_inc[i]:
                o["sem"] = None
            elif o["dma"] is not None:
                cnt_dma[o["dma"]] += 16
                o["sem"] = (("dma", o["dma"]), cnt_dma[o["dma"]])
            else:
                n = cnt_eng[o["eng"]]
                cnt_eng[o["eng"]] += 1
                o["sem"] = (("eng", o["eng"], n // SEM_EPOCH), n % SEM_EPOCH + 1)
        per_eng = {e: [] for e in ENGS}
        for i, o in enumerate(ops):
            per_eng[o["eng"]].append(i)
        final_dma = dict(cnt_dma)

        def run_engine(e, eh):
            waited = {}
            maxep = {}
            for i in per_eng[e]:
                o = ops[i]
                w = {}
                for d in o["deps"]:
                    do = ops[d]
                    if do["sem"] is None or self._skippable(do, o):
                        continue
                    sn, v = do["sem"]
                    if waited.get(sn, 0) >= v:
                        continue
                    if sn[0] == "eng" and maxep.get(sn[1], -1) > sn[2]:
                        continue
                    if w.get(sn, 0) < v:
                        w[sn] = v
                for sn, v in w.items():
                    eh.wait_ge(sems[sn], v)
                    waited[sn] = v
                    if sn[0] == "eng":
                        maxep[sn[1]] = max(maxep.get(sn[1], -1), sn[2])
                ins = o["fn"](eh)
                if o["sem"] is not None:
                    sn, v = o["sem"]
                    ins.then_inc(sems[sn], 16 if o["dma"] is not None else 1)
            if e == "sp":
                for k, v in final_dma.items():
                    if v > 0:
                        eh.wait_ge(sems[("dma", k)], v)

        with nc.Block() as block:
            @block.tensor
            def _(eh):
                run_engine("pe", eh)

            @block.scalar
            def _(eh):
                run_engine("act", eh)

            @block.vector
            def _(eh):
                run_engine("dve", eh)

            @block.gpsimd
            def _(eh):
                run_engine("pool", eh)

            @block.sync
            def _(eh):
                run_engine("sp", eh)
        stack.close()


def ffn_widths(unit):
    n = -(-unit // 510)
    base = unit // n
    rem = unit - base * n
    return [base + (1 if i < rem else 0) for i in range(n)]


class Cfg:
    def __init__(self, L=4, unit=2048, a_units=4, n_b=2, nring=6):
        self.L = L
        self.unit = unit
        self.a_units = a_units
        self.n_b = n_b
        self.nring = nring
        self.seqs = [(0, a_units)]
        off = a_units * unit
        for _ in range(n_b):
            self.seqs.append((off, 1))
            off += unit
        self.ntok = off
        self.maxkeys = a_units * unit
        self.npp = L * PP_L + 8 + 1 + 16


def build(cfg):
    nc = bass.Bass("TRN2", target_bir_lowering=False)
    L, NT, UNIT = cfg.L, cfg.ntok, cfg.unit
    TQ = 512
    NKB = cfg.maxkeys // 128

    def din(name, shape, dt=F32):
        return nc.dram_tensor(name, shape, dt, kind="ExternalInput").ap()

    x_d = din("x", [NT, D])
    wmain_d = din("wmain", [L * NG_MAIN, 128, 4096])
    wkv_d = din("wkv", [L, 128, 2048])
    sgw_d = din("sgw", [L, 128, 1024])
    sgg_d = din("sgg", [L, 128, 512])
    sgb_d = din("sgb", [L, 128, 512])
    pp_d = din("pp", [128, cfg.npp])
    cos_d = din("ropec", [128, NT])
    sin_d = din("ropes", [128, NT])
    ident_d = din("ident", [128, 128])
    rotm_d = din("rotm", [128, 128])
    y_d = nc.dram_tensor("y", [NT, D], F32, kind="ExternalOutput").ap()

    def dint(name, shape, dt):
        return nc.dram_tensor(name, shape, dt, kind="Internal").ap()

    wmain_b = dint("wmain_b", [L * NG_MAIN, 128, 4096], BF16)
    wkv_b = dint("wkv_b", [L, 128, 2048], BF16)
    sgw_b = dint("sgw_b", [L, 128, 1024], BF16)
    xT = [dint("xTa", [KC, 128, NT], F32), dint("xTb", [KC, 128, NT], F32)]
    xTv = [t.rearrange("k p t -> p k t") for t in xT]

    st = contextlib.ExitStack()
    with st:
        def sb(name, shape, dt):
            return st.enter_context(nc.sbuf_tensor("sb_" + name, shape, dt))

        ring = [sb("ring%d" % i, [128, 4096], BF16) for i in range(cfg.nring)]
        wkv_s = sb("wkv_s", [128, 2048], BF16)
        sgw_s = sb("sgw_s", [128, 1024], BF16)
        sgg_s = sb("sgg_s", [128, 512], F32)
        sgb_s = sb("sgb_s", [128, 512], F32)
        pp = sb("pp", [128, cfg.npp], F32)
        ident = sb("ident", [128, 128], F32)
        rot_f = sb("rot_f", [128, 128], F32)
        rot_b = sb("rot_b", [128, 128], BF16)
        ones_b = sb("ones_b", [128, 128], BF16)
        blk_b = sb("blk_b", [128, 128], BF16)
        KT = sb("KT", [128, max(cfg.maxkeys, 8192)], BF16)
        actT = KT[:, 0:8192].rearrange("p (j n) -> p j n", j=16)
        VA = sb("VA", [128, NKB, 192], BF16)
        xt = [sb("xt%d" % i, [128, KC, 514], F32) for i in range(2)]
        hT = sb("hT", [128, KC, 514], BF16)
        sqT = sb("sqT", [128, KC, 514], BF16)
        rstd = sb("rstd", [128, 514], F32)
        cosb = [sb("cos%d" % i, [128, 512], F32) for i in range(2)]
        sinb = [sb("sin%d" % i, [128, 512], F32) for i in range(2)]
        bbuf = sb("bbuf", [128, 16, 512], BF16)
        qT = bbuf[:, 0:4, :]
        attnT = bbuf[:, 4:8, :]
        uspT = bbuf[:, 8:12, :]
        vsn = bbuf[:, 12:16, :]
        fbuf = sb("fbuf", [128, 2048], F32)
        u_g = fbuf[:].rearrange("p (c n) -> p c n", c=4)
        ytok = fbuf[:].rearrange("p (b d) -> p b d", b=2)
        NTMP = 6
        tmp = [sb("tmp%d" % i, [128, 512], F32) for i in range(NTMP)]
        tb16 = [sb("tb16_%d" % i, [128, 512], BF16) for i in range(2)]
        NPT = 4
        pT = [sb("pT%d" % i, [128, 1024], BF16) for i in range(NPT)]
        small = sb("small", [128, 16], F32)
        psall = st.enter_context(nc.psum_tensor("psall", [128, 8 * 512], F32))
        psb = [psall[:, i * 512:(i + 1) * 512] for i in range(8)]

        p = Prog(nc)
        state = dict(tmp=0, pt=0, tb=0, wseq=0, wload=0)

        def T():
            i = state["tmp"] % NTMP
            state["tmp"] += 1
            return tmp[i], "tmp%d" % i

        def TB():
            i = state["tb"] % 2
            state["tb"] += 1
            return tb16[i], "tb16_%d" % i

        wlist = []

        def ring_load(seq_idx):
            if seq_idx >= len(wlist):
                return
            slot = seq_idx % cfg.nring
            gi = wlist[seq_idx]
            p.op("sp", lambda e, slot=slot, gi=gi: e.dma_start(out=ring[slot][:], in_=wmain_b[gi]),
                 reads=[("wready", gi // NG_MAIN)], writes=[("ring", slot)], dma=("ring", slot))

        class WG:
            def __init__(self):
                self.idx = state["wseq"]
                state["wseq"] += 1
                self.slot = self.idx % cfg.nring
                self.t = ring[self.slot]
                self.res = ("ring", self.slot)

            def done(self):
                ring_load(self.idx + cfg.nring)

        def sched_groups():
            out = []
            for l in range(L):
                for (soff, nu) in cfg.seqs:
                    ntile = nu * UNIT // TQ
                    for _ in range(ntile):
                        out += [l * NG_MAIN + g for g in range(0, G_UP)]
                    for _u in range(nu):
                        for _w in ffn_widths(UNIT):
                            out += [l * NG_MAIN + g for g in range(G_UP, NG_MAIN)]
            return out

        wlist.extend(sched_groups())

        p.op("sp", lambda e: e.dma_start(out=pp[:], in_=pp_d), writes=["pp"], dma="c0")
        p.op("sp", lambda e: e.dma_start(out=ident[:], in_=ident_d), writes=["ident"], dma="c1")
        p.op("sp", lambda e: e.dma_start(out=rot_f[:], in_=rotm_d), writes=["rot_f"], dma="c2")
        p.op("dve", lambda e: e.tensor_copy(out=rot_b[:], in_=rot_f[:]), reads=["rot_f"], writes=["rot_b"])
        p.op("dve", lambda e: e.memset(ones_b[:], 1.0), writes=["ones_b"])
        p.op("dve", lambda e: e.memset(blk_b[:], 0.0), writes=["blk_b"])
        p.op("dve", lambda e: e.memset(blk_b[0:64, 0:64], 1.0), reads=["blk_b"], writes=["blk_b0"])
        p.op("dve", lambda e: e.memset(blk_b[64:128, 64:128], 1.0), reads=["blk_b"], writes=["blk_b1"])
        p.op("pool", lambda e: e.memset(VA[:, :, 64:128], 1.0), writes=["VA1"])
        BLK = ["blk_b", "blk_b0", "blk_b1"]
        cast_q = []

        def cast_ops(l):
            ops_ = []
            ops_.append(lambda: p.op("pool", lambda e: e.dma_start(out=wkv_b[l], in_=wkv_d[l]), writes=[("wcast", l, -1)], dma=("cast", l)))
            ops_.append(lambda: p.op("pool", lambda e: e.dma_start(out=sgw_b[l], in_=sgw_d[l]), writes=[("wcast", l, -2)], dma=("cast", l)))
            for g in range(NG_MAIN):
                gi = l * NG_MAIN + g
                ops_.append(lambda gi=gi, g=g: p.op("pool", lambda e: e.dma_start(out=wmain_b[gi], in_=wmain_d[gi]),
                                                    writes=[("wcast", l, g)], dma=("cast", l)))
            ops_.append(lambda: p.op("pool", lambda e: e.memset(small[:, 12 + l % 4:13 + l % 4], 0.0),
                                     reads=[("wcast", l, g) for g in range(-2, NG_MAIN)], writes=[("wready", l)]))
            return ops_

        for f in cast_ops(0):
            f()
        for l in range(1, L):
            cast_q.extend(cast_ops(l))

        def drain_casts(n):
            for _ in range(min(n, len(cast_q))):
                cast_q.pop(0)()

        xin = x_d.rearrange("(n p) d -> p n d", p=128)
        HS = 256
        for t0 in range(0, NT, HS):
            bi = (t0 // HS) % 2
            p.op("sp", lambda e, t0=t0: e.dma_start(out=ytok, in_=xin[:, t0 // 128:t0 // 128 + 2, :]),
                 writes=["ytok"], dma="xin")
            for kc in range(KC):
                bank = 1 + (kc % 4)
                for tb in range(2):
                    p.op("pe", lambda e, kc=kc, tb=tb, bank=bank: e.transpose(
                        out=psb[bank][:, tb * 128:(tb + 1) * 128], in_=ytok[:, tb, kc * 128:(kc + 1) * 128],
                        identity=ident[:]), reads=["ytok", "ident"], writes=[("ps", bank)])
                if kc % 2 == 0:
                    p.op("dve", lambda e, kc=kc, bank=bank, bi=bi: e.tensor_copy(out=xt[bi][:, kc, 0:HS], in_=psb[bank][:, 0:HS]),
                         reads=[("ps", bank)], writes=[("xt", bi, kc)])
                else:
                    p.op("act", lambda e, kc=kc, bank=bank, bi=bi: e.copy(out=xt[bi][:, kc, 0:HS], in_=psb[bank][:, 0:HS]),
                         reads=[("ps", bank)], writes=[("xt", bi, kc)])
            p.op("sp", lambda e, t0=t0, bi=bi: e.dma_start(out=xTv[0][:, :, t0:t0 + HS], in_=xt[bi][:, :, 0:HS]),
                 reads=[("xt", bi, kc) for kc in range(KC)], writes=[("xT", 0, t0 // TQ)], dma=("xst", bi))

        for i in range(cfg.nring):
            ring_load(i)
        state["wload"] = cfg.nring

        def pcol(i):
            return pp[:, i:i + 1]

        def load_x(src, c0, ncol, bi, dst0=0):
            p.op("pool", lambda e: e.dma_start(out=xt[bi][:, :, dst0:dst0 + ncol], in_=xTv[src][:, :, c0:c0 + ncol]),
                 reads=[("xT", src, t) for t in range(c0 // TQ, (c0 + ncol - 1) // TQ + 1)],
                 writes=[("xt", bi, kc) for kc in range(KC)], dma=("xld", bi))

        XT = lambda bi: [("xt", bi, kc) for kc in range(KC)]
        HT = [("hT", kc) for kc in range(KC)]

        sched = []
        for l in range(L):
            src = l % 2
            for si, (soff, nu) in enumerate(cfg.seqs):
                ntile = nu * UNIT // TQ
                for j in range(ntile):
                    sched.append(dict(kind="kv", l=l, si=si, j=j, src=src, lo=soff + j * TQ, ncol=TQ, dst0=0, rope=soff + j * TQ,
                                      ntile=ntile, nu=nu, soff=soff))
                for j in range(ntile):
                    sched.append(dict(kind="mx", l=l, si=si, j=j, src=src, lo=soff + j * TQ, ncol=TQ, dst0=0, rope=soff + j * TQ,
                                      ntile=ntile, nu=nu, soff=soff))
                widths = ffn_widths(UNIT)
                for u in range(nu):
                    woff = 0
                    for wi, W in enumerate(widths):
                        c0 = soff + u * UNIT + woff
                        woff += W
                        left_edge = wi == 0
                        right_edge = wi == len(widths) - 1
                        seq_left = left_edge and u == 0
                        seq_right = right_edge and u == nu - 1
                        lo = c0 - 1 + (1 if seq_left else 0)
                        hi = c0 + W + 1 - (1 if seq_right else 0)
                        sched.append(dict(kind="ffn", l=l, si=si, src=src, lo=lo, ncol=hi - lo, dst0=(1 if seq_left else 0), rope=None,
                                          c0=c0, W=W, left_edge=left_edge, right_edge=right_edge, seq_left=seq_left, seq_right=seq_right))
        for t0 in range(0, NT, TQ):
            sched.append(dict(kind="fin", l=L, src=L % 2, lo=t0, ncol=TQ, dst0=0, rope=None))
        last_kv = {}
        last_mx = {}
        for i, t in enumerate(sched):
            if t["kind"] == "kv":
                last_kv[t["l"]] = i
            if t["kind"] == "mx":
                last_mx[t["l"]] = i
        loaded = [False] * len(sched)
        prologued = [False] * len(sched)

        def emit_load(i):
            if i >= len(sched) or loaded[i]:
                return
            loaded[i] = True
            t = sched[i]
            bi = i % 2
            if t["kind"] == "ffn":
                NW = t["W"] + 2
                if t["seq_left"]:
                    p.op("pool", lambda e: e.memset(xt[bi][:, :, 0:1], 0.0), writes=XT(bi))
                if t["seq_right"]:
                    p.op("pool", lambda e: e.memset(xt[bi][:, :, NW - 1:NW], 0.0), writes=XT(bi))
            load_x(t["src"], t["lo"], t["ncol"], bi, dst0=t["dst0"])
            if t["rope"] is not None:
                c0 = t["rope"]
                p.op("pool", lambda e: e.dma_start(out=cosb[bi][:], in_=cos_d[:, c0:c0 + TQ]), writes=[("cos", bi)], dma=("cos", bi))
                p.op("pool", lambda e: e.dma_start(out=sinb[bi][:], in_=sin_d[:, c0:c0 + TQ]), writes=[("sin", bi)], dma=("sin", bi))

        def layer_consts_kv(l):
            if l < L:
                p.op("sp", lambda e: e.dma_start(out=wkv_s[:], in_=wkv_b[l]), reads=[("wready", l)], writes=["wkv_s"], dma="lw0")

        def layer_consts_mx(l):
            if l < L:
                p.op("sp", lambda e: e.dma_start(out=sgw_s[:], in_=sgw_b[l]), reads=[("wready", l)], writes=["sgw_s"], dma="lw1")
                p.op("sp", lambda e: e.dma_start(out=sgg_s[:], in_=sgg_d[l]), writes=["sgg_s"], dma="lw2")
                p.op("sp", lambda e: e.dma_start(out=sgb_s[:], in_=sgb_d[l]), writes=["sgb_s"], dma="lw3")

        def prologue(i, alt=False):
            if i >= len(sched) or prologued[i]:
                return

            def sq_ap(kc, n):
                return bbuf[:, kc, 0:n] if alt else sqT[:, kc, 0:n]

            def sq_res(kc):
                if not alt:
                    return [("sqT", kc)]
                return [("qT", kc)] if kc < 4 else [("attnT", kc - 4, 0), ("attnT", kc - 4, 1)]
            prologued[i] = True
            t = sched[i]
            bi = i % 2
            l = t["l"]
            ncol = t["ncol"] + t["dst0"] + (1 if t.get("seq_right") else 0) if t["kind"] == "ffn" else TQ
            if t["kind"] == "ffn":
                ncol = t["W"] + 2
                gbase = l * PP_L + 8
                if t["left_edge"] and not t["seq_left"]:
                    p.op("dve", lambda e: e.tensor_scalar(out=xt[bi][:, :, 0:1], in0=xt[bi][:, :, 0:1],
                                                          scalar1=pcol(L * PP_L + 8), scalar2=None, op0=ALU.mult),
                         reads=XT(bi) + ["pp"], writes=XT(bi))
                if t["right_edge"] and not t["seq_right"]:
                    p.op("dve", lambda e: e.tensor_scalar(out=xt[bi][:, :, ncol - 1:ncol], in0=xt[bi][:, :, ncol - 1:ncol],
                                                          scalar1=pcol(L * PP_L + 8), scalar2=None, op0=ALU.mult),
                         reads=XT(bi) + ["pp"], writes=XT(bi))
            elif t["kind"] == "fin":
                gbase = L * PP_L
            else:
                gbase = l * PP_L
            for kc in range(KC):
                p.op("act", lambda e, kc=kc: e.activation(out=sq_ap(kc, ncol), in_=xt[bi][:, kc, 0:ncol], func=AF.Square),
                     reads=[("xt", bi, kc)], writes=sq_res(kc))
            for kc in range(KC):
                p.op("pe", lambda e, kc=kc: e.matmul(psb[0][:, 0:ncol], lhsT=ones_b[:], rhs=sq_ap(kc, ncol),
                                                     start=(kc == 0), stop=(kc == KC - 1)),
                     reads=sq_res(kc) + ["ones_b"], writes=[("ps", 0)])
            p.op("act", lambda e: e.activation(out=rstd[:, 0:ncol], in_=psb[0][:, 0:ncol], func=AF.Ln, bias=EPS, scale=1.0 / D),
                 reads=[("ps", 0)], writes=["rstd"])
            p.op("act", lambda e: e.activation(out=rstd[:, 0:ncol], in_=rstd[:, 0:ncol], func=AF.Exp, scale=-0.5),
                 reads=["rstd"], writes=["rstd"])
            if t["kind"] == "fin":
                for kc in range(KC):
                    p.op("dve", lambda e, kc=kc: e.scalar_tensor_tensor(
                        out=xt[bi][:, kc, 0:TQ], in0=xt[bi][:, kc, 0:TQ], scalar=pcol(gbase + kc), in1=rstd[:, 0:TQ],
                        op0=ALU.mult, op1=ALU.mult), reads=[("xt", bi, kc), "rstd", "pp"], writes=[("xt", bi, kc)])
            else:
                for kc in range(KC):
                    p.op("dve", lambda e, kc=kc: e.scalar_tensor_tensor(
                        out=hT[:, kc, 0:ncol], in0=xt[bi][:, kc, 0:ncol], scalar=pcol(gbase + kc), in1=rstd[:, 0:ncol],
                        op0=ALU.mult, op1=ALU.mult), reads=[("xt", bi, kc), "rstd", "pp"], writes=[("hT", kc)])

        wkv3 = wkv_s[:].rearrange("p (k n) -> p k n", k=KC)
        sgw3 = sgw_s[:].rearrange("p (g n) -> p g n", g=8)
        sgb3 = sgb_s[:].rearrange("p (c n) -> p c n", c=4)

        def head_chain_a(src_bank, gcol):
            sq, sqr = TB()
            p.op("act", lambda e: e.activation(out=sq[:], in_=psb[src_bank][:], func=AF.Square),
                 reads=[("ps", src_bank)], writes=[sqr])
            p.op("pe", lambda e: e.matmul(psb[6][:], lhsT=blk_b[:], rhs=sq[:], start=True, stop=True),
                 reads=[sqr] + BLK, writes=[("ps", 6)])
            t1, t1r = T()
            p.op("act", lambda e: e.activation(out=t1[:], in_=psb[6][:], func=AF.Ln, bias=EPS, scale=1.0 / HD),
                 reads=[("ps", 6)], writes=[t1r])
            p.op("act", lambda e: e.activation(out=t1[:], in_=t1[:], func=AF.Exp, scale=-0.5), reads=[t1r], writes=[t1r])
            kn, knr = TB()
            p.op("dve", lambda e: e.scalar_tensor_tensor(out=kn[:], in0=psb[src_bank][:], scalar=pcol(gcol), in1=t1[:],
                                                         op0=ALU.mult, op1=ALU.mult),
                 reads=[("ps", src_bank), t1r, "pp"], writes=[knr])
            return kn, knr

        def head_chain_b(kn, knr, cs, out_ap, out_res):
            p.op("pe", lambda e: e.matmul(psb[7][:], lhsT=rot_b[:], rhs=kn[:], start=True, stop=True),
                 reads=[knr, "rot_b"], writes=[("ps", 7)])
            t2, t2r = T()
            p.op("dve", lambda e: e.tensor_tensor(out=t2[:], in0=kn[:], in1=cosb[cs][:], op=ALU.mult),
                 reads=[knr, ("cos", cs)], writes=[t2r])
            t3, t3r = T()
            p.op("dve", lambda e: e.tensor_tensor(out=t3[:], in0=psb[7][:], in1=sinb[cs][:], op=ALU.mult),
                 reads=[("ps", 7), ("sin", cs)], writes=[t3r])
            p.op("dve", lambda e: e.tensor_tensor(out=out_ap, in0=t2[:], in1=t3[:], op=ALU.add),
                 reads=[t2r, t3r], writes=[out_res])

        def proj_fm(wg, mc, bank):
            w3 = wg.t[:].rearrange("p (k n) -> p k n", k=KC)
            for kc in range(KC):
                p.op("pe", lambda e, kc=kc: e.matmul(psb[bank][:], lhsT=w3[:, kc, mc * 128:(mc + 1) * 128],
                                                     rhs=hT[:, kc, 0:TQ], start=(kc == 0), stop=(kc == KC - 1)),
                     reads=[("hT", kc), wg.res], writes=[("ps", bank)])

        def kv_tile(i):
            t = sched[i]
            bi = i % 2
            l, j = t["l"], t["j"]
            ppl = l * PP_L
            for kc in range(KC):
                p.op("pe", lambda e, kc=kc: e.matmul(psb[1][:], lhsT=wkv3[:, kc, 0:128], rhs=hT[:, kc, 0:TQ],
                                                     start=(kc == 0), stop=(kc == KC - 1)),
                     reads=[("hT", kc), "wkv_s"], writes=[("ps", 1)])
            for tb in range(4):
                for kc in range(KC):
                    p.op("pe", lambda e, kc=kc, tb=tb: e.matmul(
                        psb[2][:, tb * 128:(tb + 1) * 128], lhsT=hT[:, kc, tb * 128:(tb + 1) * 128],
                        rhs=wkv3[:, kc, 128:256], start=(kc == 0), stop=(kc == KC - 1)),
                        reads=[("hT", kc), "wkv_s"], writes=[("ps", 2)])
            kn, knr = head_chain_a(1, ppl + 17)
            if i + 1 < len(sched) and sched[i + 1]["kind"] == "kv" and loaded[i + 1]:
                prologue(i + 1)
            head_chain_b(kn, knr, bi, KT[:, j * TQ:(j + 1) * TQ], ("KT", j))
            kb0 = j * 4
            v4 = psb[2][:].rearrange("p (b g d) -> p b g d", b=4, g=2)
            p.op("dve", lambda e: e.tensor_copy(out=VA[:, kb0:kb0 + 4, 0:64], in_=v4[:, :, 0, :]),
                 reads=[("ps", 2)], writes=[("VA0", j)])
            p.op("dve", lambda e: e.tensor_copy(out=VA[:, kb0:kb0 + 4, 128:192], in_=v4[:, :, 1, :]),
                 reads=[("ps", 2)], writes=[("VA2", j)])
            if i == last_kv[l]:
                layer_consts_kv(l + 1)

        def mx_tile(i):
            t = sched[i]
            bi = i % 2
            l, j, nu, soff = t["l"], t["j"], t["nu"], t["soff"]
            src = t["src"]
            c0 = t["lo"]
            ppl = l * PP_L
            nkb = nu * UNIT // 128
            uq = j // (UNIT // TQ)
            wq = WG()
            wu = WG()
            wv = WG()
            ubank = [5, 6, 7, 0]
            w3 = wv.t[:].rearrange("p (k n) -> p k n", k=KC)
            for tb in range(4):
                for kc in range(KC):
                    p.op("pe", lambda e, kc=kc, tb=tb: e.matmul(
                        psb[1 + tb][:], lhsT=hT[:, kc, tb * 128:(tb + 1) * 128], rhs=w3[:, kc, :],
                        start=(kc == 0), stop=(kc == KC - 1)),
                        reads=[("hT", kc), wv.res], writes=[("ps", 1 + tb)])
            wv.done()
            for c in range(4):
                proj_fm(wu, c, ubank[c])
            wu.done()
            vsg = []
            for tb in range(4):
                t1, t1r = T()
                vsg.append((t1, t1r))
                p.op("act", lambda e, tb=tb, t1=t1: e.activation(out=t1[:], in_=psb[1 + tb][:], func=AF.Gelu_apprx_tanh),
                     reads=[("ps", 1 + tb)], writes=[t1r])
            for c in range(4):
                p.op("act", lambda e, c=c: e.activation(out=u_g[:, c, :], in_=psb[ubank[c]][:], func=AF.Gelu_apprx_tanh),
                     reads=[("ps", ubank[c])], writes=[("u_g", c)])
            for c in range(4):
                proj_fm(wq, c, 1 + c)
            wq.done()
            for tb in range(4):
                t1, t1r = vsg[tb]
                junk, junkr = TB()
                p.op("act", lambda e, t1=t1, junk=junk, tb=tb: e.activation(out=junk[:], in_=t1[:], func=AF.Square,
                                                                             accum_out=small[:, tb:tb + 1]),
                     reads=[t1r], writes=[junkr, ("small", tb)])
            for tb in range(4):
                p.op("act", lambda e, tb=tb: e.activation(out=small[:, 4 + tb:5 + tb], in_=small[:, tb:tb + 1], func=AF.Ln,
                                                          bias=EPS, scale=1.0 / 512),
                     reads=[("small", tb)], writes=[("small", 4 + tb)])
            for tb in range(4):
                p.op("act", lambda e, tb=tb: e.activation(out=small[:, 8 + tb:9 + tb], in_=small[:, 4 + tb:5 + tb], func=AF.Exp,
                                                          scale=-0.5),
                     reads=[("small", 4 + tb)], writes=[("small", 8 + tb)])
            for tb in range(4):
                t1, t1r = vsg[tb]
                p.op("dve", lambda e, tb=tb, t1=t1: e.scalar_tensor_tensor(
                    out=vsn[:, tb, :], in0=t1[:], scalar=small[:, 8 + tb:9 + tb], in1=sgg_s[:],
                    op0=ALU.mult, op1=ALU.mult), reads=[t1r, ("small", 8 + tb), "sgg_s"], writes=[("vsn", tb)])
            qsq = []
            for c in range(4):
                p.op("act", lambda e, c=c: e.activation(out=attnT[:, c, :], in_=psb[1 + c][:], func=AF.Square),
                     reads=[("ps", 1 + c)], writes=[("attnT", c, 0), ("attnT", c, 1)])
            rs = []
            for c in range(4):
                kb_ = 5 + (c % 2)
                p.op("pe", lambda e, c=c, kb_=kb_: e.matmul(psb[kb_][:], lhsT=blk_b[:], rhs=attnT[:, c, :], start=True, stop=True),
                     reads=[("attnT", c, 0), ("attnT", c, 1)] + BLK, writes=[("ps", kb_)])
                t1, t1r = T()
                rs.append((t1, t1r))
                p.op("act", lambda e, t1=t1, kb_=kb_: e.activation(out=t1[:], in_=psb[kb_][:], func=AF.Ln, bias=EPS, scale=1.0 / HD),
                     reads=[("ps", kb_)], writes=[t1r])
            for c in range(4):
                t1, t1r = rs[c]
                p.op("act", lambda e, t1=t1: e.activation(out=t1[:], in_=t1[:], func=AF.Exp, scale=-0.5), reads=[t1r], writes=[t1r])
            for c in range(4):
                t1, t1r = rs[c]
                p.op("dve", lambda e, c=c, t1=t1: e.scalar_tensor_tensor(out=attnT[:, c, :], in0=psb[1 + c][:], scalar=pcol(ppl + 16),
                                                                         in1=t1[:], op0=ALU.mult, op1=ALU.mult),
                     reads=[("ps", 1 + c), t1r, "pp"], writes=[("attnT", c, 0), ("attnT", c, 1)])
            for c in range(4):
                rb = 7 if c % 2 == 0 else 0
                p.op("pe", lambda e, c=c, rb=rb: e.matmul(psb[rb][:], lhsT=rot_b[:], rhs=attnT[:, c, :], start=True, stop=True),
                     reads=[("attnT", c, 0), ("attnT", c, 1), "rot_b"], writes=[("ps", rb)])
                t2, t2r = T()
                p.op("dve", lambda e, c=c, t2=t2: e.tensor_tensor(out=t2[:], in0=attnT[:, c, :], in1=cosb[bi][:], op=ALU.mult),
                     reads=[("attnT", c, 0), ("attnT", c, 1), ("cos", bi)], writes=[t2r])
                t3, t3r = T()
                p.op("dve", lambda e, rb=rb, t3=t3: e.tensor_tensor(out=t3[:], in0=psb[rb][:], in1=sinb[bi][:], op=ALU.mult),
                     reads=[("ps", rb), ("sin", bi)], writes=[t3r])
                p.op("dve", lambda e, c=c, t2=t2, t3=t3: e.tensor_tensor(out=qT[:, c, :], in0=t2[:], in1=t3[:], op=ALU.add),
                     reads=[t2r, t3r], writes=[("qT", c)])
            for tb in range(4):
                for c in range(4):
                    bank = 1 + c
                    for gg in range(2):
                        g = 2 * c + gg
                        p.op("pe", lambda e, tb=tb, g=g, gg=gg, bank=bank: e.matmul(
                            psb[bank][gg * 64:(gg + 1) * 64, tb * 128:(tb + 1) * 128],
                            lhsT=vsn[:, tb, g * 64:(g + 1) * 64], rhs=sgw3[:, g, :], start=True, stop=True,
                            tile_position=(0, gg * 64)),
                            reads=[("vsn", tb), "sgw_s"], writes=[("ps", bank)])
            for c in range(4):
                bank = 1 + c
                t1, t1r = T()
                p.op("dve", lambda e, c=c, bank=bank, t1=t1: e.tensor_tensor(
                    out=t1[:].rearrange("p (b n) -> p b n", b=4), in0=psb[bank][:].rearrange("p (b n) -> p b n", b=4),
                    in1=sgb3[:, c:c + 1, :].broadcast_to([128, 4, 128]), op=ALU.add),
                    reads=[("ps", bank), "sgb_s"], writes=[t1r])
                p.op("dve", lambda e, c=c, t1=t1: e.tensor_tensor(out=uspT[:, c, :], in0=t1[:], in1=u_g[:, c, :], op=ALU.mult),
                     reads=[t1r, ("u_g", c)], writes=[("uspT", c)])
            for c in range(4):
                ob = 4 if c % 2 == 0 else 6

                def S(kb, c=c):
                    sb0 = 2 * (kb % 2)
                    for hh in range(2):
                        r0 = hh * 64
                        p.op("pe", lambda e: e.matmul(psb[sb0 + hh][:], lhsT=KT[r0:r0 + 64, kb * 128:(kb + 1) * 128],
                                                      rhs=qT[r0:r0 + 64, c, :], start=True, stop=True),
                             reads=[("KT", kb // 4), ("qT", c)], writes=[("ps", sb0 + hh)])

                def E(kb):
                    sb0 = 2 * (kb % 2)
                    pi = state["pt"] % NPT
                    state["pt"] += 1
                    uk = (kb * 128) // UNIT
                    bias = pcol(L * PP_L + 9 + uq * 4 + uk) if nu > 1 else 0.0
                    p.op("act", lambda e: e.activation(out=pT[pi][:], in_=psall[:, sb0 * 512:(sb0 + 2) * 512], func=AF.Exp,
                                                       bias=bias, scale=0.125),
                         reads=[("ps", sb0), ("ps", sb0 + 1), "pp"], writes=[("pT", pi)])
                    return pi

                def PV(kb, pi, ob=ob):
                    for hh in range(2):
                        vcol = 64 * hh
                        p.op("pe", lambda e: e.matmul(psb[ob + hh][:], lhsT=VA[:, kb, vcol:vcol + 128],
                                                      rhs=pT[pi][:, hh * 512:(hh + 1) * 512],
                                                      start=(kb == 0), stop=(kb == nkb - 1)),
                             reads=[("pT", pi), ("VA0", kb // 4), ("VA2", kb // 4), "VA1"], writes=[("ps", ob + hh)])

                S(0)
                if nkb > 1:
                    S(1)
                for kb in range(nkb):
                    pi = E(kb)
                    if kb + 2 < nkb:
                        S(kb + 2)
                    PV(kb, pi)
                for hh in range(2):
                    r0 = hh * 64
                    d0 = 64 - r0
                    obank = ob + hh
                    t1, t1r = T()
                    p.op("dve", lambda e: e.reciprocal(out=t1[d0:d0 + 64, :], in_=psb[obank][d0:d0 + 64, :]),
                         reads=[("ps", obank)], writes=[t1r])
                    p.op("dve", lambda e: e.tensor_tensor(
                        out=attnT[r0:r0 + 64, c, :], in0=psb[obank][r0:r0 + 64, :], in1=t1[d0:d0 + 64, :], op=ALU.mult),
                        reads=[("ps", obank), t1r], writes=[("attnT", c, hh)])
            wga0 = WG()
            wgb0 = WG()
            wa = WG()
            wb = WG()
            wa3 = wa.t[:].rearrange("p (k n) -> p k n", k=4)
            wb3 = wb.t[:].rearrange("p (k n) -> p k n", k=4)
            wga, wgb = wga0, wgb0
            for m in range(8):
                if m == 4:
                    wga0.done()
                    wgb0.done()
                    wga = WG()
                    wgb = WG()
                banks = [1, 2, 3, 4] if m % 2 == 0 else [5, 6, 7, 0]
                proj_fm(wga, m % 4, banks[0])
                proj_fm(wgb, m % 4, banks[1])
                for kc in range(4):
                    p.op("pe", lambda e, kc=kc, m=m, b=banks[2]: e.matmul(
                        psb[b][:], lhsT=wa3[:, kc, m * 128:(m + 1) * 128], rhs=attnT[:, kc, :],
                        start=(kc == 0), stop=(kc == 3)),
                        reads=[("attnT", kc, 0), ("attnT", kc, 1), wa.res], writes=[("ps", banks[2])])
                for kc in range(4):
                    p.op("pe", lambda e, kc=kc, m=m, b=banks[3]: e.matmul(
                        psb[b][:], lhsT=wb3[:, kc, m * 128:(m + 1) * 128], rhs=uspT[:, kc, :],
                        start=(kc == 0), stop=(kc == 3)),
                        reads=[("uspT", kc), wb.res], writes=[("ps", banks[3])])
                sa, sar = T()
                p.op("act", lambda e, sa=sa, b=banks[0]: e.activation(out=sa[:], in_=psb[b][:], func=AF.Sigmoid),
                     reads=[("ps", banks[0])], writes=[sar])
                sg_, sgr = T()
                p.op("act", lambda e, sg_=sg_, b=banks[1]: e.activation(out=sg_[:], in_=psb[b][:], func=AF.Sigmoid),
                     reads=[("ps", banks[1])], writes=[sgr])
                p.op("dve", lambda e, sa=sa, b=banks[2]: e.tensor_tensor(out=sa[:], in0=sa[:], in1=psb[b][:], op=ALU.mult),
                     reads=[sar, ("ps", banks[2])], writes=[sar])
                p.op("dve", lambda e, sg_=sg_, b=banks[3]: e.tensor_tensor(out=sg_[:], in0=sg_[:], in1=psb[b][:], op=ALU.mult),
                     reads=[sgr, ("ps", banks[3])], writes=[sgr])
                p.op("dve", lambda e, sa=sa, sg_=sg_, m=m: e.tensor_tensor(out=sqT[:, m, 0:TQ], in0=sa[:], in1=sg_[:], op=ALU.add),
                     reads=[sar, sgr], writes=[("sqT", m)])
            wga.done()
            wgb.done()
            wa.done()
            wb.done()
            for half in range(2):
                if half == 1 and i + 1 < len(sched) and sched[i + 1]["kind"] in ("mx", "ffn") and loaded[i + 1]:
                    prologue(i + 1, alt=True)
                wg = WG()
                w3 = wg.t[:].rearrange("p (k n) -> p k n", k=KC)
                for mm in range(4):
                    m = half * 4 + mm
                    bank = 1 + (m % 2)
                    for kc in range(KC):
                        p.op("pe", lambda e, kc=kc, mm=mm, bank=bank: e.matmul(
                            psb[bank][:], lhsT=w3[:, kc, mm * 128:(mm + 1) * 128], rhs=sqT[:, kc, 0:TQ],
                            start=(kc == 0), stop=(kc == KC - 1)),
                            reads=[("sqT", kc), wg.res], writes=[("ps", bank)])
                    p.op("dve", lambda e, m=m, bank=bank: e.tensor_tensor(
                        out=xt[bi][:, m, 0:TQ], in0=xt[bi][:, m, 0:TQ], in1=psb[bank][:], op=ALU.add),
                        reads=[("xt", bi, m), ("ps", bank)], writes=[("xt", bi, m)])
                wg.done()
            p.op("pool", lambda e: e.dma_start(out=xTv[src][:, :, c0:c0 + TQ], in_=xt[bi][:, :, 0:TQ]),
                 reads=XT(bi), writes=[("xT", src, c0 // TQ)], dma=("xst", bi))
            if i == last_mx[l]:
                layer_consts_mx(l + 1)

        def ffn_tile(i):
            t = sched[i]
            bi = i % 2
            l, W, c0 = t["l"], t["W"], t["c0"]
            dst = 1 - t["src"]
            ppl = l * PP_L
            NW = W + 2
            for gi in range(8):
                wg = WG()
                w3 = wg.t[:].rearrange("p (k n) -> p k n", k=KC)
                for jj in range(2):
                    jf = 2 * gi + jj
                    gb_, vb_ = (1, 2) if jf % 2 == 0 else (3, 4)
                    for (bank, col) in ((gb_, jj), (vb_, 2 + jj)):
                        for kc in range(KC):
                            p.op("pe", lambda e, kc=kc, bank=bank, col=col: e.matmul(
                                psb[bank][:, 0:NW], lhsT=w3[:, kc, col * 128:(col + 1) * 128], rhs=hT[:, kc, 0:NW],
                                start=(kc == 0), stop=(kc == KC - 1)),
                                reads=[("hT", kc), wg.res], writes=[("ps", bank)])
                    res = []
                    for (bank, ch) in ((gb_, jf), (vb_, 16 + jf)):
                        cb = ppl + 18 + ch * 4
                        t1, t1r = T()
                        p.op("act", lambda e, t1=t1, bank=bank, cb=cb: e.activation(
                            out=t1[:, 0:W], in_=psb[bank][:, 1:W + 1], func=AF.Identity, bias=pcol(cb + 3), scale=pcol(cb + 1)),
                            reads=[("ps", bank), "pp"], writes=[t1r])
                        p.op("dve", lambda e, t1=t1, bank=bank, cb=cb: e.scalar_tensor_tensor(
                            out=t1[:, 0:W], in0=psb[bank][:, 0:W], scalar=pcol(cb + 0), in1=t1[:, 0:W],
                            op0=ALU.mult, op1=ALU.add), reads=[("ps", bank), t1r, "pp"], writes=[t1r])
                        p.op("dve", lambda e, t1=t1, bank=bank, cb=cb: e.scalar_tensor_tensor(
                            out=t1[:, 0:W], in0=psb[bank][:, 2:W + 2], scalar=pcol(cb + 2), in1=t1[:, 0:W],
                            op0=ALU.mult, op1=ALU.add), reads=[("ps", bank), t1r, "pp"], writes=[t1r])
                        res.append((t1, t1r))
                    (tg, tgr), (tv, tvr) = res
                    p.op("act", lambda e, tg=tg: e.activation(out=tg[:, 0:W], in_=tg[:, 0:W], func=AF.Gelu_apprx_tanh),
                         reads=[tgr], writes=[tgr])
                    p.op("dve", lambda e, tg=tg, tv=tv, jf=jf: e.tensor_tensor(
                        out=actT[:, jf, 0:W], in0=tg[:, 0:W], in1=tv[:, 0:W], op=ALU.mult),
                        reads=[tgr, tvr], writes=[("actT", jf)])
                wg.done()
            for gi in range(4):
                if gi == 1 and i + 1 < len(sched) and sched[i + 1]["kind"] == "ffn" and loaded[i + 1]:
                    prologue(i + 1)
                wg = WG()
                w3 = wg.t[:].rearrange("p (k n) -> p k n", k=16)
                for mm in range(2):
                    m = gi * 2 + mm
                    bank = 5 + (m % 2)
                    for kc in range(16):
                        p.op("pe", lambda e, kc=kc, mm=mm, bank=bank: e.matmul(
                            psb[bank][:, 0:W], lhsT=w3[:, kc, mm * 128:(mm + 1) * 128], rhs=actT[:, kc, 0:W],
                            start=(kc == 0), stop=(kc == 15)),
                            reads=[("actT", kc), wg.res], writes=[("ps", bank)])
                    p.op("dve", lambda e, m=m, bank=bank: e.tensor_tensor(
                        out=xt[bi][:, m, 1:W + 1], in0=xt[bi][:, m, 1:W + 1], in1=psb[bank][:, 0:W], op=ALU.add),
                        reads=[("xt", bi, m), ("ps", bank)], writes=[("xt", bi, m)])
                wg.done()
            tl = list(range(c0 // TQ, (c0 + W - 1) // TQ + 1))
            p.op("pool", lambda e: e.dma_start(out=xTv[dst][:, :, c0:c0 + W], in_=xt[bi][:, :, 1:W + 1]),
                 reads=XT(bi), writes=[("xT", dst, tt) for tt in tl], dma=("xst", bi))

        yout = y_d.rearrange("(n p) d -> p n d", p=128)

        def fin_tile(i):
            t = sched[i]
            bi = i % 2
            t0 = t["lo"]
            for th in range(2):
                for tbl in range(2):
                    tb = th * 2 + tbl
                    for half in range(2):
                        bank = 1 + ((tbl * 2 + half) % 4)
                        for k4 in range(4):
                            kc = half * 4 + k4
                            p.op("pe", lambda e, kc=kc, k4=k4, tb=tb, bank=bank: e.transpose(
                                out=psb[bank][:, k4 * 128:(k4 + 1) * 128], in_=xt[bi][:, kc, tb * 128:(tb + 1) * 128], identity=ident[:]),
                                reads=[("xt", bi, kc), "ident"], writes=[("ps", bank)])
                        if half == 0:
                            p.op("dve", lambda e, tbl=tbl, bank=bank: e.tensor_copy(out=ytok[:, tbl, 0:512], in_=psb[bank][:]),
                                 reads=[("ps", bank)], writes=[("ytok", tbl, 0)])
                        else:
                            p.op("act", lambda e, tbl=tbl, bank=bank: e.copy(out=ytok[:, tbl, 512:1024], in_=psb[bank][:]),
                                 reads=[("ps", bank)], writes=[("ytok", tbl, 1)])
                r0 = t0 // 128 + th * 2
                p.op("pool", lambda e, r0=r0: e.dma_start(out=yout[:, r0:r0 + 2, :], in_=ytok),
                     reads=[("ytok", tbl, h) for tbl in range(2) for h in range(2)], dma="yst")

        layer_consts_kv(0)
        layer_consts_mx(0)
        def wr_range(t):
            if t["kind"] == "mx":
                return (t["src"], t["lo"], t["lo"] + TQ)
            if t["kind"] == "ffn":
                return (1 - t["src"], t["c0"], t["c0"] + t["W"])
            return None

        def can_prefetch(i):
            if i + 1 >= len(sched):
                return False
            w = wr_range(sched[i])
            n = sched[i + 1]
            return not (w is not None and w[0] == n["src"] and w[1] < n["lo"] + n["ncol"] and n["lo"] < w[2])

        for i, t in enumerate(sched):
            drain_casts(2)
            emit_load(i)
            if can_prefetch(i):
                emit_load(i + 1)
            prologue(i)
            {"kv": kv_tile, "mx": mx_tile, "ffn": ffn_tile, "fin": fin_tile}[t["kind"]](i)

        assert state["wseq"] == len(wlist), (state["wseq"], len(wlist))
        p.emit()
    return nc


def rope_tables(positions):
    half = HD // 2
    freq = (np.float32(10000.0) ** (-np.arange(0, half, 2, dtype=np.float32) / np.float32(half))).astype(np.float32)
    pos = np.asarray(positions)
    row = (pos // 64).astype(np.float32)
    col = (pos % 64).astype(np.float32)
    cos = np.zeros((128, len(pos)), np.float32)
    sin = np.zeros((128, len(pos)), np.float32)
    for pidx in range(128):
        d = pidx % 64
        axis = row if d < 32 else col
        j = d % 32
        i = j % 16
        ang = (axis * freq[i]).astype(np.float32)
        cos[pidx] = np.cos(ang)
        s = np.sin(ang)
        sin[pidx] = -s if j < 16 else s
    return cos, sin


def rot_matrix():
    r = np.zeros((128, 128), np.float32)
    for pidx in range(128):
        partner = pidx + 16 if (pidx % 32) < 16 else pidx - 16
        r[partner, pidx] = 1.0
    return r


def img(w):
    K, N = w.shape
    return np.ascontiguousarray(w.reshape(K // 128, 128, N).transpose(1, 0, 2)).reshape(128, -1)


def prep_weights(inp, cfg):
    L = cfg.L
    wmain = np.empty((L * NG_MAIN, 128, 4096), np.float32)
    wkv = np.empty((L, 128, 2048), np.float32)
    sgw = np.empty((L, 128, 1024), np.float32)
    sgg = np.empty((L, 128, 512), np.float32)
    sgb = np.empty((L, 128, 512), np.float32)
    pp = np.zeros((128, cfg.npp), np.float32)
    qperm = np.concatenate([np.concatenate([np.arange(c * 64, c * 64 + 64), np.arange((4 + c) * 64, (4 + c) * 64 + 64)])
                            for c in range(4)])
    for l in range(L):
        w_in = np.asarray(inp["w_in"][l])
        b = l * NG_MAIN
        wmain[b + G_Q] = img(w_in[:, 0:512][:, qperm])
        wkv[l] = img(w_in[:, 512:768])
        wmain[b + G_U] = img(w_in[:, 768:1280])
        wmain[b + G_VS] = img(w_in[:, 1280:1792])
        wmain[b + G_GA0] = img(w_in[:, 1792:2304])
        wmain[b + G_GA1] = img(w_in[:, 2304:2816])
        wmain[b + G_GB0] = img(w_in[:, 2816:3328])
        wmain[b + G_GB1] = img(w_in[:, 3328:3840])
        wmain[b + G_WA] = img(np.asarray(inp["w_branch_a"][l])[qperm, :])
        wmain[b + G_WB] = img(np.asarray(inp["w_branch_b"][l]))
        wm = np.asarray(inp["w_mix_out"][l])
        wmain[b + G_MIX] = img(wm[:, 0:512])
        wmain[b + G_MIX + 1] = img(wm[:, 512:1024])
        wu = np.asarray(inp["w_up"][l])
        for i in range(8):
            cols = np.concatenate([np.arange(2 * i * 128, (2 * i + 2) * 128), 2048 + np.arange(2 * i * 128, (2 * i + 2) * 128)])
            wmain[b + G_UP + i] = img(wu[:, cols])
        wd = np.asarray(inp["w_down"][l])
        for i in range(4):
            wmain[b + G_DN + i] = img(wd[:, i * 256:(i + 1) * 256])
        sw = np.asarray(inp["sg_w"][l])
        sgw[l] = np.ascontiguousarray(sw.transpose(2, 0, 1)).reshape(128, 1024)
        sgg[l] = np.broadcast_to(np.asarray(inp["sg_norm_g"][l])[None, :], (128, 512))
        sbias = np.asarray(inp["sg_b"][l])
        t = np.empty((128, 4, 128), np.float32)
        for c in range(4):
            t[0:64, c, :] = sbias[2 * c][None, :]
            t[64:128, c, :] = sbias[2 * c + 1][None, :]
        sgb[l] = t.reshape(128, 512)
        o = l * PP_L
        pp[:, o:o + 8] = np.asarray(inp["attn_norm_g"][l]).reshape(8, 128).T
        pp[:, o + 8:o + 16] = np.asarray(inp["ffn_norm_g"][l]).reshape(8, 128).T
        pp[:, o + 16] = np.tile(np.asarray(inp["q_norm_g"][l]), 2)
        pp[:, o + 17] = np.tile(np.asarray(inp["k_norm_g"][l]), 2)
        cw = np.asarray(inp["conv_w"][l])
        cb = np.asarray(inp["conv_b"][l])
        cv = np.stack([cw[0], cw[1], cw[2], cb], axis=-1).reshape(32, 128, 4).transpose(1, 0, 2)
        pp[:, o + 18:o + 18 + 128] = cv.reshape(128, 128)
    o = L * PP_L
    pp[:, o:o + 8] = np.asarray(inp["final_norm_g"]).reshape(8, 128).T
    return dict(wmain=wmain, wkv=wkv, sgw=sgw, sgg=sgg, sgb=sgb), pp


def core_inputs(shared, pp_base, cfg, xa_list, xb_list, a_is_one_seq):
    L = cfg.L
    x = np.concatenate([np.asarray(a) for a in xa_list] + [np.asarray(b) for b in xb_list], axis=0).astype(np.float32)
    pos = []
    if a_is_one_seq:
        pos.append(np.arange(cfg.a_units * cfg.unit))
    else:
        for _ in range(cfg.a_units):
            pos.append(np.arange(cfg.unit))
    for _ in range(cfg.n_b):
        pos.append(np.arange(cfg.unit))
    cos, sin = rope_tables(np.concatenate(pos))
    pp = pp_base.copy()
    o = L * PP_L
    pp[:, o + 8] = 1.0 if a_is_one_seq else 0.0
    for uq in range(4):
        for uk in range(4):
            pp[:, o + 9 + uq * 4 + uk] = 0.0 if (a_is_one_seq or uq == uk) else -30000.0
    m = dict(shared)
    m.update(x=np.ascontiguousarray(x), pp=pp, ropec=cos, ropes=sin,
             ident=np.eye(128, dtype=np.float32), rotm=rot_matrix())
    return m


_NC_CACHE = {}


def kernel(**inputs):
    cfg = Cfg()
    shared, pp_base = prep_weights(inputs, cfg)
    xp = np.asarray(inputs["x_prompt"])
    xs = np.asarray(inputs["x_sample"])
    in_maps = []
    for c in range(8):
        if c < 4:
            in_maps.append(core_inputs(shared, pp_base, cfg, [xp[c]], [xs[2 * c], xs[2 * c + 1]], True))
        else:
            b = 8 + 6 * (c - 4)
            in_maps.append(core_inputs(shared, pp_base, cfg, [xs[b + i] for i in range(4)], [xs[b + 4], xs[b + 5]], False))
    nc = build(cfg)
    res = run_bass_kernel_spmd(nc, in_maps, core_ids=list(range(8)))
    yp = np.empty(xp.shape, np.float32)
    ys = np.empty(xs.shape, np.float32)
    U = cfg.unit
    for c in range(8):
        y = np.asarray(res.results[c]["y"])
        if c < 4:
            yp[c] = y[0:4 * U]
            ys[2 * c] = y[4 * U:5 * U]
            ys[2 * c + 1] = y[5 * U:6 * U]
        else:
            b = 8 + 6 * (c - 4)
            for i in range(6):
                ys[b + i] = y[i * U:(i + 1) * U]
    return (yp, ys)
```

```python
import contextlib
import math
import types
import numpy as np
import concourse.bass as bass
import concourse.mybir as mybir
from concourse.bass_utils import run_bass_kernel_spmd

F32 = mybir.dt.float32
BF16 = mybir.dt.bfloat16
AF = mybir.ActivationFunctionType
ALU = mybir.AluOpType

ENGS = ("pe", "act", "dve", "pool", "sp")
SEM_EPOCH = 30000

D = 1024
KC = 8
HD = 64
EPS = 1e-6
NG_MAIN = 23
G_Q, G_U, G_VS, G_GA0, G_GB0, G_WA, G_WB, G_GA1, G_GB1, G_MIX, G_UP, G_DN = 0, 1, 2, 3, 4, 5, 6, 7, 8, 9, 11, 19
PP_L = 8 + 8 + 1 + 1 + 128


class Prog:
    def __init__(self, nc):
        self.nc = nc
        self.ops = []
        self.last_w = {}
        self.readers = {}
        self.eng_count = {e: 0 for e in ENGS}

    @staticmethod
    def _freeze(fn):
        if fn.__closure__ is None:
            return fn
        cells = []
        for c in fn.__closure__:
            try:
                cells.append(types.CellType(c.cell_contents))
            except ValueError:
                cells.append(c)
        return types.FunctionType(fn.__code__, fn.__globals__, fn.__name__, fn.__defaults__, tuple(cells))

    def _key(self, oid):
        o = self.ops[oid]
        return ("dma", o["dma"]) if o["dma"] is not None else ("eng", o["eng"])

    def op(self, eng, fn, reads=(), writes=(), dma=None):
        oid = len(self.ops)
        deps = {}

        def add(d):
            k = self._key(d)
            if deps.get(k, -1) < d:
                deps[k] = d
        for r in reads:
            if r in self.last_w:
                add(self.last_w[r])
        for w in writes:
            if w in self.last_w:
                add(self.last_w[w])
            for rd in self.readers.get(w, {}).values():
                add(rd)
        for w in writes:
            self.last_w[w] = oid
            self.readers[w] = {}
        self.ops.append(dict(eng=eng, fn=self._freeze(fn), deps=set(deps.values()), dma=dma, lidx=self.eng_count[eng]))
        k = self._key(oid)
        for r in reads:
            self.readers.setdefault(r, {})[k] = oid
        self.eng_count[eng] += 1
        return oid

    @staticmethod
    def _skippable(do, o):
        if do["dma"] is not None or o["dma"] is not None or do["eng"] != o["eng"]:
            return False
        if o["eng"] == "pe":
            return True
        return o["lidx"] - do["lidx"] >= 3

    def emit(self):
        nc = self.nc
        ops = self.ops
        need_inc = [False] * len(ops)
        for o in ops:
            for d in o["deps"]:
                if not self._skippable(ops[d], o):
                    need_inc[d] = True
        dma_keys = []
        for i, o in enumerate(ops):
            if o["dma"] is not None:
                need_inc[i] = True
                if o["dma"] not in dma_keys:
                    dma_keys.append(o["dma"])
        n_inc_eng = {e: 0 for e in ENGS}
        for i, o in enumerate(ops):
            if o["dma"] is None and need_inc[i]:
                n_inc_eng[o["eng"]] += 1
        sem_names = []
        for e in ENGS:
            for k in range((n_inc_eng[e] + SEM_EPOCH - 1) // SEM_EPOCH):
                sem_names.append(("eng", e, k))
        for k in dma_keys:
            sem_names.append(("dma", k))
        self.n_sems = len(sem_names)
        stack = contextlib.ExitStack()
        sems = {}
        for i, sn in enumerate(sem_names):
            sems[sn] = stack.enter_context(nc.semaphore("s%d" % i))
        cnt_eng = {e: 0 for e in ENGS}
        cnt_dma = {k: 0 for k in dma_keys}
        for i, o in enumerate(ops):
            if not need_inc[i]:
                o["sem"] = None
            elif o["dma"] is not None:
                cnt_dma[o["dma"]] += 16
                o["sem"] = (("dma", o["dma"]), cnt_dma[o["dma"]])
            else:
                n = cnt_eng[o["eng"]]
                cnt_eng[o["eng"]] += 1
                o["sem"] = (("eng", o["eng"], n // SEM_EPOCH), n % SEM_EPOCH + 1)
        per_eng = {e: [] for e in ENGS}
        for i, o in enumerate(ops):
            per_eng[o["eng"]].append(i)
        final_dma = dict(cnt_dma)

        def run_engine(e, eh):
            waited = {}
            maxep = {}
            for i in per_eng[e]:
                o = ops[i]
                w = {}
                for d in o["deps"]:
                    do = ops[d]
                    if do["sem"] is None or self._skippable(do, o):
                        continue
                    sn, v = do["sem"]
                    if waited.get(sn, 0) >= v:
                        continue
                    if sn[0] == "eng" and maxep.get(sn[1], -1) > sn[2]:
                        continue
                    if w.get(sn, 0) < v:
                        w[sn] = v
                for sn, v in w.items():
                    eh.wait_ge(sems[sn], v)
                    waited[sn] = v
                    if sn[0] == "eng":
                        maxep[sn[1]] = max(maxep.get(sn[1], -1), sn[2])
                ins = o["fn"](eh)
                if o["sem"] is not None:
                    sn, v = o["sem"]
                    ins.then_inc(sems[sn], 16 if o["dma"] is not None else 1)
            if e == "sp":
                for k, v in final_dma.items():
                    if v > 0:
                        eh.wait_ge(sems[("dma", k)], v)

        with nc.Block() as block:
            @block.tensor
            def _(eh):
                run_engine("pe", eh)

            @block.scalar
            def _(eh):
                run_engine("act", eh)

            @block.vector
            def _(eh):
                run_engine("dve", eh)

            @block.gpsimd
            def _(eh):
                run_engine("pool", eh)

            @block.sync
            def _(eh):
                run_engine("sp", eh)
        stack.close()


def ffn_widths(unit):
    n = -(-unit // 510)
    base = unit // n
    rem = unit - base * n
    return [base + (1 if i < rem else 0) for i in range(n)]


class Cfg:
    def __init__(self, L=4, unit=2048, a_units=4, n_b=2, nring=6):
        self.L = L
        self.unit = unit
        self.a_units = a_units
        self.n_b = n_b
        self.nring = nring
        self.seqs = [(0, a_units)]
        off = a_units * unit
        for _ in range(n_b):
            self.seqs.append((off, 1))
            off += unit
        self.ntok = off
        self.maxkeys = a_units * unit
        self.npp = L * PP_L + 8 + 1 + 16


def build(cfg):
    nc = bass.Bass("TRN2", target_bir_lowering=False)
    L, NT, UNIT = cfg.L, cfg.ntok, cfg.unit
    TQ = 512
    NKB = cfg.maxkeys // 128

    def din(name, shape, dt=F32):
        return nc.dram_tensor(name, shape, dt, kind="ExternalInput").ap()

    x_d = din("x", [NT, D])
    wmain_d = din("wmain", [L * NG_MAIN, 128, 4096])
    wkv_d = din("wkv", [L, 128, 2048])
    sgw_d = din("sgw", [L, 128, 1024])
    sgg_d = din("sgg", [L, 128, 512])
    sgb_d = din("sgb", [L, 128, 512])
    pp_d = din("pp", [128, cfg.npp])
    cos_d = din("ropec", [128, NT])
    sin_d = din("ropes", [128, NT])
    ident_d = din("ident", [128, 128])
    rotm_d = din("rotm", [128, 128])
    y_d = nc.dram_tensor("y", [NT, D], F32, kind="ExternalOutput").ap()

    def dint(name, shape, dt):
        return nc.dram_tensor(name, shape, dt, kind="Internal").ap()

    wmain_b = dint("wmain_b", [L * NG_MAIN, 128, 4096], BF16)
    wkv_b = dint("wkv_b", [L, 128, 2048], BF16)
    sgw_b = dint("sgw_b", [L, 128, 1024], BF16)
    xT = [dint("xTa", [KC, 128, NT], F32), dint("xTb", [KC, 128, NT], F32)]
    xTv = [t.rearrange("k p t -> p k t") for t in xT]

    st = contextlib.ExitStack()
    with st:
        def sb(name, shape, dt):
            return st.enter_context(nc.sbuf_tensor("sb_" + name, shape, dt))

        ring = [sb("ring%d" % i, [128, 4096], BF16) for i in range(cfg.nring)]
        wkv_s = sb("wkv_s", [128, 2048], BF16)
        sgw_s = sb("sgw_s", [128, 1024], BF16)
        sgg_s = sb("sgg_s", [128, 512], F32)
        sgb_s = sb("sgb_s", [128, 512], F32)
        pp = sb("pp", [128, cfg.npp], F32)
        ident = sb("ident", [128, 128], F32)
        rot_f = sb("rot_f", [128, 128], F32)
        rot_b = sb("rot_b", [128, 128], BF16)
        ones_b = sb("ones_b", [128, 128], BF16)
        blk_b = sb("blk_b", [128, 128], BF16)
        KT = sb("KT", [128, max(cfg.maxkeys, 8192)], BF16)
        actT = KT[:, 0:8192].rearrange("p (j n) -> p j n", j=16)
        VA = sb("VA", [128, NKB, 192], BF16)
        xt = [sb("xt%d" % i, [128, KC, 514], F32) for i in range(2)]
        hT = sb("hT", [128, KC, 514], BF16)
        sqT = sb("sqT", [128, KC, 514], BF16)
        rstd = sb("rstd", [128, 514], F32)
        cosb = [sb("cos%d" % i, [128, 512], F32) for i in range(2)]
        sinb = [sb("sin%d" % i, [128, 512], F32) for i in range(2)]
        bbuf = sb("bbuf", [128, 16, 512], BF16)
        qT = bbuf[:, 0:4, :]
        attnT = bbuf[:, 4:8, :]
        uspT = bbuf[:, 8:12, :]
        vsn = bbuf[:, 12:16, :]
        fbuf = sb("fbuf", [128, 2048], F32)
        u_g = fbuf[:].rearrange("p (c n) -> p c n", c=4)
        ytok = fbuf[:].rearrange("p (b d) -> p b d", b=2)
        NTMP = 6
        tmp = [sb("tmp%d" % i, [128, 512], F32) for i in range(NTMP)]
        tb16 = [sb("tb16_%d" % i, [128, 512], BF16) for i in range(2)]
        NPT = 4
        pT = [sb("pT%d" % i, [128, 1024], BF16) for i in range(NPT)]
        small = sb("small", [128, 16], F32)
        psall = st.enter_context(nc.psum_tensor("psall", [128, 8 * 512], F32))
        psb = [psall[:, i * 512:(i + 1) * 512] for i in range(8)]

        p = Prog(nc)
        state = dict(tmp=0, pt=0, tb=0, wseq=0, wload=0)

        def T():
            i = state["tmp"] % NTMP
            state["tmp"] += 1
            return tmp[i], "tmp%d" % i

        def TB():
            i = state["tb"] % 2
            state["tb"] += 1
            return tb16[i], "tb16_%d" % i

        wlist = []

        def ring_load(seq_idx):
            if seq_idx >= len(wlist):
                return
            slot = seq_idx % cfg.nring
            gi = wlist[seq_idx]
            p.op("sp", lambda e, slot=slot, gi=gi: e.dma_start(out=ring[slot][:], in_=wmain_b[gi]),
                 reads=[("wready", gi // NG_MAIN)], writes=[("ring", slot)], dma=("ring", slot))

        class WG:
            def __init__(self):
                self.idx = state["wseq"]
                state["wseq"] += 1
                self.slot = self.idx % cfg.nring
                self.t = ring[self.slot]
                self.res = ("ring", self.slot)

            def done(self):
                ring_load(self.idx + cfg.nring)

        def sched_groups():
            out = []
            for l in range(L):
                for (soff, nu) in cfg.seqs:
                    ntile = nu * UNIT // TQ
                    for _ in range(ntile):
                        out += [l * NG_MAIN + g for g in range(0, G_UP)]
                    for _u in range(nu):
                        for _w in ffn_widths(UNIT):
                            out += [l * NG_MAIN + g for g in range(G_UP, NG_MAIN)]
            return out

        wlist.extend(sched_groups())

        p.op("sp", lambda e: e.dma_start(out=pp[:], in_=pp_d), writes=["pp"], dma="c0")
        p.op("sp", lambda e: e.dma_start(out=ident[:], in_=ident_d), writes=["ident"], dma="c1")
        p.op("sp", lambda e: e.dma_start(out=rot_f[:], in_=rotm_d), writes=["rot_f"], dma="c2")
        p.op("dve", lambda e: e.tensor_copy(out=rot_b[:], in_=rot_f[:]), reads=["rot_f"], writes=["rot_b"])
        p.op("dve", lambda e: e.memset(ones_b[:], 1.0), writes=["ones_b"])
        p.op("dve", lambda e: e.memset(blk_b[:], 0.0), writes=["blk_b"])
        p.op("dve", lambda e: e.memset(blk_b[0:64, 0:64], 1.0), reads=["blk_b"], writes=["blk_b0"])
        p.op("dve", lambda e: e.memset(blk_b[64:128, 64:128], 1.0), reads=["blk_b"], writes=["blk_b1"])
        p.op("pool", lambda e: e.memset(VA[:, :, 64:128], 1.0), writes=["VA1"])
        BLK = ["blk_b", "blk_b0", "blk_b1"]
        cast_q = []

        def cast_ops(l):
            ops_ = []
            ops_.append(lambda: p.op("pool", lambda e: e.dma_start(out=wkv_b[l], in_=wkv_d[l]), writes=[("wcast", l, -1)], dma=("cast", l)))
            ops_.append(lambda: p.op("pool", lambda e: e.dma_start(out=sgw_b[l], in_=sgw_d[l]), writes=[("wcast", l, -2)], dma=("cast", l)))
            for g in range(NG_MAIN):
                gi = l * NG_MAIN + g
                ops_.append(lambda gi=gi, g=g: p.op("pool", lambda e: e.dma_start(out=wmain_b[gi], in_=wmain_d[gi]),
                                                    writes=[("wcast", l, g)], dma=("cast", l)))
            ops_.append(lambda: p.op("pool", lambda e: e.memset(small[:, 12 + l % 4:13 + l % 4], 0.0),
                                     reads=[("wcast", l, g) for g in range(-2, NG_MAIN)], writes=[("wready", l)]))
            return ops_

        for f in cast_ops(0):
            f()
        for l in range(1, L):
            cast_q.extend(cast_ops(l))

        def drain_casts(n):
            for _ in range(min(n, len(cast_q))):
                cast_q.pop(0)()

        xin = x_d.rearrange("(n p) d -> p n d", p=128)
        HS = 256
        for t0 in range(0, NT, HS):
            bi = (t0 // HS) % 2
            p.op("sp", lambda e, t0=t0: e.dma_start(out=ytok, in_=xin[:, t0 // 128:t0 // 128 + 2, :]),
                 writes=["ytok"], dma="xin")
            for kc in range(KC):
                bank = 1 + (kc % 4)
                for tb in range(2):
                    p.op("pe", lambda e, kc=kc, tb=tb, bank=bank: e.transpose(
                        out=psb[bank][:, tb * 128:(tb + 1) * 128], in_=ytok[:, tb, kc * 128:(kc + 1) * 128],
                        identity=ident[:]), reads=["ytok", "ident"], writes=[("ps", bank)])
                if kc % 2 == 0:
                    p.op("dve", lambda e, kc=kc, bank=bank, bi=bi: e.tensor_copy(out=xt[bi][:, kc, 0:HS], in_=psb[bank][:, 0:HS]),
                         reads=[("ps", bank)], writes=[("xt", bi, kc)])
                else:
                    p.op("act", lambda e, kc=kc, bank=bank, bi=bi: e.copy(out=xt[bi][:, kc, 0:HS], in_=psb[bank][:, 0:HS]),
                         reads=[("ps", bank)], writes=[("xt", bi, kc)])
            p.op("sp", lambda e, t0=t0, bi=bi: e.dma_start(out=xTv[0][:, :, t0:t0 + HS], in_=xt[bi][:, :, 0:HS]),
                 reads=[("xt", bi, kc) for kc in range(KC)], writes=[("xT", 0, t0 // TQ)], dma=("xst", bi))

        for i in range(cfg.nring):
            ring_load(i)
        state["wload"] = cfg.nring

        def pcol(i):
            return pp[:, i:i + 1]

        def load_x(src, c0, ncol, bi, dst0=0):
            p.op("pool", lambda e: e.dma_start(out=xt[bi][:, :, dst0:dst0 + ncol], in_=xTv[src][:, :, c0:c0 + ncol]),
                 reads=[("xT", src, t) for t in range(c0 // TQ, (c0 + ncol - 1) // TQ + 1)],
                 writes=[("xt", bi, kc) for kc in range(KC)], dma=("xld", bi))

        XT = lambda bi: [("xt", bi, kc) for kc in range(KC)]
        HT = [("hT", kc) for kc in range(KC)]

        sched = []
        for l in range(L):
            src = l % 2
            for si, (soff, nu) in enumerate(cfg.seqs):
                ntile = nu * UNIT // TQ
                for j in range(ntile):
                    sched.append(dict(kind="kv", l=l, si=si, j=j, src=src, lo=soff + j * TQ, ncol=TQ, dst0=0, rope=soff + j * TQ,
                                      ntile=ntile, nu=nu, soff=soff))
                for j in range(ntile):
                    sched.append(dict(kind="mx", l=l, si=si, j=j, src=src, lo=soff + j * TQ, ncol=TQ, dst0=0, rope=soff + j * TQ,
                                      ntile=ntile, nu=nu, soff=soff))
                widths = ffn_widths(UNIT)
                for u in range(nu):
                    woff = 0
                    for wi, W in enumerate(widths):
                        c0 = soff + u * UNIT + woff
                        woff += W
                        left_edge = wi == 0
                        right_edge = wi == len(widths) - 1
                        seq_left = left_edge and u == 0
                        seq_right = right_edge and u == nu - 1
                        lo = c0 - 1 + (1 if seq_left else 0)
                        hi = c0 + W + 1 - (1 if seq_right else 0)
                        sched.append(dict(kind="ffn", l=l, si=si, src=src, lo=lo, ncol=hi - lo, dst0=(1 if seq_left else 0), rope=None,
                                          c0=c0, W=W, left_edge=left_edge, right_edge=right_edge, seq_left=seq_left, seq_right=seq_right))
        for t0 in range(0, NT, TQ):
            sched.append(dict(kind="fin", l=L, src=L % 2, lo=t0, ncol=TQ, dst0=0, rope=None))
        last_kv = {}
        last_mx = {}
        for i, t in enumerate(sched):
            if t["kind"] == "kv":
                last_kv[t["l"]] = i
            if t["kind"] == "mx":
                last_mx[t["l"]] = i
        loaded = [False] * len(sched)
        prologued = [False] * len(sched)

        def emit_load(i):
            if i >= len(sched) or loaded[i]:
                return
            loaded[i] = True
            t = sched[i]
            bi = i % 2
            if t["kind"] == "ffn":
                NW = t["W"] + 2
                if t["seq_left"]:
                    p.op("pool", lambda e: e.memset(xt[bi][:, :, 0:1], 0.0), writes=XT(bi))
                if t["seq_right"]:
                    p.op("pool", lambda e: e.memset(xt[bi][:, :, NW - 1:NW], 0.0), writes=XT(bi))
            load_x(t["src"], t["lo"], t["ncol"], bi, dst0=t["dst0"])
            if t["rope"] is not None:
                c0 = t["rope"]
                p.op("pool", lambda e: e.dma_start(out=cosb[bi][:], in_=cos_d[:, c0:c0 + TQ]), writes=[("cos", bi)], dma=("cos", bi))
                p.op("pool", lambda e: e.dma_start(out=sinb[bi][:], in_=sin_d[:, c0:c0 + TQ]), writes=[("sin", bi)], dma=("sin", bi))

        def layer_consts_kv(l):
            if l < L:
                p.op("sp", lambda e: e.dma_start(out=wkv_s[:], in_=wkv_b[l]), reads=[("wready", l)], writes=["wkv_s"], dma="lw0")

        def layer_consts_mx(l):
            if l < L:
                p.op("sp", lambda e: e.dma_start(out=sgw_s[:], in_=sgw_b[l]), reads=[("wready", l)], writes=["sgw_s"], dma="lw1")
                p.op("sp", lambda e: e.dma_start(out=sgg_s[:], in_=sgg_d[l]), writes=["sgg_s"], dma="lw2")
                p.op("sp", lambda e: e.dma_start(out=sgb_s[:], in_=sgb_d[l]), writes=["sgb_s"], dma="lw3")

        ht_done = [False] * len(sched)

        def prologue_ht(i):
            if i >= len(sched) or ht_done[i] or not prologued[i]:
                return
            ht_done[i] = True
            t = sched[i]
            bi = i % 2
            if t["kind"] == "ffn":
                ncol = t["W"] + 2
                gbase = t["l"] * PP_L + 8
            else:
                ncol = TQ
                gbase = t["l"] * PP_L
            for kc in range(KC):
                p.op("dve", lambda e, kc=kc: e.scalar_tensor_tensor(
                    out=hT[:, kc, 0:ncol], in0=xt[bi][:, kc, 0:ncol], scalar=pcol(gbase + kc), in1=rstd[:, 0:ncol],
                    op0=ALU.mult, op1=ALU.mult), reads=[("xt", bi, kc), "rstd", "pp"], writes=[("hT", kc)])

        def prologue(i, alt=False, defer_ht=False):
            if i >= len(sched) or prologued[i]:
                return

            def sq_ap(kc, n):
                return bbuf[:, kc, 0:n] if alt else sqT[:, kc, 0:n]

            def sq_res(kc):
                if not alt:
                    return [("sqT", kc)]
                return [("qT", kc)] if kc < 4 else [("attnT", kc - 4, 0), ("attnT", kc - 4, 1)]
            prologued[i] = True
            t = sched[i]
            bi = i % 2
            l = t["l"]
            ncol = t["ncol"] + t["dst0"] + (1 if t.get("seq_right") else 0) if t["kind"] == "ffn" else TQ
            if t["kind"] == "ffn":
                ncol = t["W"] + 2
                gbase = l * PP_L + 8
                if t["left_edge"] and not t["seq_left"]:
                    p.op("dve", lambda e: e.tensor_scalar(out=xt[bi][:, :, 0:1], in0=xt[bi][:, :, 0:1],
                                                          scalar1=pcol(L * PP_L + 8), scalar2=None, op0=ALU.mult),
                         reads=XT(bi) + ["pp"], writes=XT(bi))
                if t["right_edge"] and not t["seq_right"]:
                    p.op("dve", lambda e: e.tensor_scalar(out=xt[bi][:, :, ncol - 1:ncol], in0=xt[bi][:, :, ncol - 1:ncol],
                                                          scalar1=pcol(L * PP_L + 8), scalar2=None, op0=ALU.mult),
                         reads=XT(bi) + ["pp"], writes=XT(bi))
            elif t["kind"] == "fin":
                gbase = L * PP_L
            else:
                gbase = l * PP_L
            for kc in range(KC):
                p.op("act", lambda e, kc=kc: e.activation(out=sq_ap(kc, ncol), in_=xt[bi][:, kc, 0:ncol], func=AF.Square),
                     reads=[("xt", bi, kc)], writes=sq_res(kc))
            for kc in range(KC):
                p.op("pe", lambda e, kc=kc: e.matmul(psb[0][:, 0:ncol], lhsT=ones_b[:], rhs=sq_ap(kc, ncol),
                                                     start=(kc == 0), stop=(kc == KC - 1)),
                     reads=sq_res(kc) + ["ones_b"], writes=[("ps", 0)])
            p.op("act", lambda e: e.activation(out=rstd[:, 0:ncol], in_=psb[0][:, 0:ncol], func=AF.Ln, bias=EPS, scale=1.0 / D),
                 reads=[("ps", 0)], writes=["rstd"])
            p.op("act", lambda e: e.activation(out=rstd[:, 0:ncol], in_=rstd[:, 0:ncol], func=AF.Exp, scale=-0.5),
                 reads=["rstd"], writes=["rstd"])
            if t["kind"] == "fin":
                for kc in range(KC):
                    p.op("dve", lambda e, kc=kc: e.scalar_tensor_tensor(
                        out=xt[bi][:, kc, 0:TQ], in0=xt[bi][:, kc, 0:TQ], scalar=pcol(gbase + kc), in1=rstd[:, 0:TQ],
                        op0=ALU.mult, op1=ALU.mult), reads=[("xt", bi, kc), "rstd", "pp"], writes=[("xt", bi, kc)])
            elif not defer_ht:
                prologue_ht(i)

        wkv3 = wkv_s[:].rearrange("p (k n) -> p k n", k=KC)
        sgw3 = sgw_s[:].rearrange("p (g n) -> p g n", g=8)
        sgb3 = sgb_s[:].rearrange("p (c n) -> p c n", c=4)

        def head_chain_a(src_bank, gcol):
            sq, sqr = TB()
            p.op("act", lambda e: e.activation(out=sq[:], in_=psb[src_bank][:], func=AF.Square),
                 reads=[("ps", src_bank)], writes=[sqr])
            p.op("pe", lambda e: e.matmul(psb[6][:], lhsT=blk_b[:], rhs=sq[:], start=True, stop=True),
                 reads=[sqr] + BLK, writes=[("ps", 6)])
            t1, t1r = T()
            p.op("act", lambda e: e.activation(out=t1[:], in_=psb[6][:], func=AF.Ln, bias=EPS, scale=1.0 / HD),
                 reads=[("ps", 6)], writes=[t1r])
            p.op("act", lambda e: e.activation(out=t1[:], in_=t1[:], func=AF.Exp, scale=-0.5), reads=[t1r], writes=[t1r])
            kn, knr = TB()
            p.op("dve", lambda e: e.scalar_tensor_tensor(out=kn[:], in0=psb[src_bank][:], scalar=pcol(gcol), in1=t1[:],
                                                         op0=ALU.mult, op1=ALU.mult),
                 reads=[("ps", src_bank), t1r, "pp"], writes=[knr])
            return kn, knr

        def head_chain_b(kn, knr, cs, out_ap, out_res):
            p.op("pe", lambda e: e.matmul(psb[7][:], lhsT=rot_b[:], rhs=kn[:], start=True, stop=True),
                 reads=[knr, "rot_b"], writes=[("ps", 7)])
            t2, t2r = T()
            p.op("dve", lambda e: e.tensor_tensor(out=t2[:], in0=kn[:], in1=cosb[cs][:], op=ALU.mult),
                 reads=[knr, ("cos", cs)], writes=[t2r])
            t3, t3r = T()
            p.op("dve", lambda e: e.tensor_tensor(out=t3[:], in0=psb[7][:], in1=sinb[cs][:], op=ALU.mult),
                 reads=[("ps", 7), ("sin", cs)], writes=[t3r])
            p.op("dve", lambda e: e.tensor_tensor(out=out_ap, in0=t2[:], in1=t3[:], op=ALU.add),
                 reads=[t2r, t3r], writes=[out_res])

        def proj_fm(wg, mc, bank):
            w3 = wg.t[:].rearrange("p (k n) -> p k n", k=KC)
            for kc in range(KC):
                p.op("pe", lambda e, kc=kc: e.matmul(psb[bank][:], lhsT=w3[:, kc, mc * 128:(mc + 1) * 128],
                                                     rhs=hT[:, kc, 0:TQ], start=(kc == 0), stop=(kc == KC - 1)),
                     reads=[("hT", kc), wg.res], writes=[("ps", bank)])

        def kv_tile(i):
            t = sched[i]
            bi = i % 2
            l, j = t["l"], t["j"]
            ppl = l * PP_L
            for kc in range(KC):
                p.op("pe", lambda e, kc=kc: e.matmul(psb[1][:], lhsT=wkv3[:, kc, 0:128], rhs=hT[:, kc, 0:TQ],
                                                     start=(kc == 0), stop=(kc == KC - 1)),
                     reads=[("hT", kc), "wkv_s"], writes=[("ps", 1)])
            for tb in range(4):
                for kc in range(KC):
                    p.op("pe", lambda e, kc=kc, tb=tb: e.matmul(
                        psb[2][:, tb * 128:(tb + 1) * 128], lhsT=hT[:, kc, tb * 128:(tb + 1) * 128],
                        rhs=wkv3[:, kc, 128:256], start=(kc == 0), stop=(kc == KC - 1)),
                        reads=[("hT", kc), "wkv_s"], writes=[("ps", 2)])
            kn, knr = head_chain_a(1, ppl + 17)
            if i + 1 < len(sched) and sched[i + 1]["kind"] == "kv" and loaded[i + 1]:
                prologue(i + 1)
            head_chain_b(kn, knr, bi, KT[:, j * TQ:(j + 1) * TQ], ("KT", j))
            kb0 = j * 4
            v4 = psb[2][:].rearrange("p (b g d) -> p b g d", b=4, g=2)
            p.op("dve", lambda e: e.tensor_copy(out=VA[:, kb0:kb0 + 4, 0:64], in_=v4[:, :, 0, :]),
                 reads=[("ps", 2)], writes=[("VA0", j)])
            p.op("dve", lambda e: e.tensor_copy(out=VA[:, kb0:kb0 + 4, 128:192], in_=v4[:, :, 1, :]),
                 reads=[("ps", 2)], writes=[("VA2", j)])
            if i == last_kv[l]:
                layer_consts_kv(l + 1)

        def mx_tile(i):
            t = sched[i]
            bi = i % 2
            l, j, nu, soff = t["l"], t["j"], t["nu"], t["soff"]
            src = t["src"]
            c0 = t["lo"]
            ppl = l * PP_L
            nkb = nu * UNIT // 128
            uq = j // (UNIT // TQ)
            wq = WG()
            wu = WG()
            wv = WG()
            ubank = [5, 6, 7, 0]
            w3 = wv.t[:].rearrange("p (k n) -> p k n", k=KC)
            for tb in range(4):
                for kc in range(KC):
                    p.op("pe", lambda e, kc=kc, tb=tb: e.matmul(
                        psb[1 + tb][:], lhsT=hT[:, kc, tb * 128:(tb + 1) * 128], rhs=w3[:, kc, :],
                        start=(kc == 0), stop=(kc == KC - 1)),
                        reads=[("hT", kc), wv.res], writes=[("ps", 1 + tb)])
            wv.done()
            for c in range(4):
                proj_fm(wu, c, ubank[c])
            wu.done()
            vsg = []
            for tb in range(4):
                t1, t1r = T()
                vsg.append((t1, t1r))
                p.op("act", lambda e, tb=tb, t1=t1: e.activation(out=t1[:], in_=psb[1 + tb][:], func=AF.Gelu_apprx_tanh),
                     reads=[("ps", 1 + tb)], writes=[t1r])
            for c in range(4):
                p.op("act", lambda e, c=c: e.activation(out=u_g[:, c, :], in_=psb[ubank[c]][:], func=AF.Gelu_apprx_tanh),
                     reads=[("ps", ubank[c])], writes=[("u_g", c)])
            for c in range(4):
                proj_fm(wq, c, 1 + c)
            wq.done()
            for tb in range(4):
                t1, t1r = vsg[tb]
                junk, junkr = TB()
                p.op("act", lambda e, t1=t1, junk=junk, tb=tb: e.activation(out=junk[:], in_=t1[:], func=AF.Square,
                                                                             accum_out=small[:, tb:tb + 1]),
                     reads=[t1r], writes=[junkr, ("small", tb)])
            for tb in range(4):
                p.op("act", lambda e, tb=tb: e.activation(out=small[:, 4 + tb:5 + tb], in_=small[:, tb:tb + 1], func=AF.Ln,
                                                          bias=EPS, scale=1.0 / 512),
                     reads=[("small", tb)], writes=[("small", 4 + tb)])
            for tb in range(4):
                p.op("act", lambda e, tb=tb: e.activation(out=small[:, 8 + tb:9 + tb], in_=small[:, 4 + tb:5 + tb], func=AF.Exp,
                                                          scale=-0.5),
                     reads=[("small", 4 + tb)], writes=[("small", 8 + tb)])
            for tb in range(4):
                t1, t1r = vsg[tb]
                p.op("dve", lambda e, tb=tb, t1=t1: e.scalar_tensor_tensor(
                    out=vsn[:, tb, :], in0=t1[:], scalar=small[:, 8 + tb:9 + tb], in1=sgg_s[:],
                    op0=ALU.mult, op1=ALU.mult), reads=[t1r, ("small", 8 + tb), "sgg_s"], writes=[("vsn", tb)])
            qsq = []
            for c in range(4):
                p.op("act", lambda e, c=c: e.activation(out=attnT[:, c, :], in_=psb[1 + c][:], func=AF.Square),
                     reads=[("ps", 1 + c)], writes=[("attnT", c, 0), ("attnT", c, 1)])
            rs = []
            for c in range(4):
                kb_ = 5 + (c % 2)
                p.op("pe", lambda e, c=c, kb_=kb_: e.matmul(psb[kb_][:], lhsT=blk_b[:], rhs=attnT[:, c, :], start=True, stop=True),
                     reads=[("attnT", c, 0), ("attnT", c, 1)] + BLK, writes=[("ps", kb_)])
                t1, t1r = T()
                rs.append((t1, t1r))
                p.op("act", lambda e, t1=t1, kb_=kb_: e.activation(out=t1[:], in_=psb[kb_][:], func=AF.Ln, bias=EPS, scale=1.0 / HD),
                     reads=[("ps", kb_)], writes=[t1r])
            for c in range(4):
                t1, t1r = rs[c]
                p.op("act", lambda e, t1=t1: e.activation(out=t1[:], in_=t1[:], func=AF.Exp, scale=-0.5), reads=[t1r], writes=[t1r])
            for c in range(4):
                t1, t1r = rs[c]
                p.op("dve", lambda e, c=c, t1=t1: e.scalar_tensor_tensor(out=attnT[:, c, :], in0=psb[1 + c][:], scalar=pcol(ppl + 16),
                                                                         in1=t1[:], op0=ALU.mult, op1=ALU.mult),
                     reads=[("ps", 1 + c), t1r, "pp"], writes=[("attnT", c, 0), ("attnT", c, 1)])
            for c in range(4):
                rb = 7 if c % 2 == 0 else 0
                p.op("pe", lambda e, c=c, rb=rb: e.matmul(psb[rb][:], lhsT=rot_b[:], rhs=attnT[:, c, :], start=True, stop=True),
                     reads=[("attnT", c, 0), ("attnT", c, 1), "rot_b"], writes=[("ps", rb)])
                t2, t2r = T()
                p.op("dve", lambda e, c=c, t2=t2: e.tensor_tensor(out=t2[:], in0=attnT[:, c, :], in1=cosb[bi][:], op=ALU.mult),
                     reads=[("attnT", c, 0), ("attnT", c, 1), ("cos", bi)], writes=[t2r])
                t3, t3r = T()
                p.op("dve", lambda e, rb=rb, t3=t3: e.tensor_tensor(out=t3[:], in0=psb[rb][:], in1=sinb[bi][:], op=ALU.mult),
                     reads=[("ps", rb), ("sin", bi)], writes=[t3r])
                p.op("dve", lambda e, c=c, t2=t2, t3=t3: e.tensor_tensor(out=qT[:, c, :], in0=t2[:], in1=t3[:], op=ALU.add),
                     reads=[t2r, t3r], writes=[("qT", c)])
            for tb in range(4):
                for c in range(4):
                    bank = 1 + c
                    for gg in range(2):
                        g = 2 * c + gg
                        p.op("pe", lambda e, tb=tb, g=g, gg=gg, bank=bank: e.matmul(
                            psb[bank][gg * 64:(gg + 1) * 64, tb * 128:(tb + 1) * 128],
                            lhsT=vsn[:, tb, g * 64:(g + 1) * 64], rhs=sgw3[:, g, :], start=True, stop=True,
                            tile_position=(0, gg * 64)),
                            reads=[("vsn", tb), "sgw_s"], writes=[("ps", bank)])
            for c in range(4):
                bank = 1 + c
                t1, t1r = T()
                p.op("dve", lambda e, c=c, bank=bank, t1=t1: e.tensor_tensor(
                    out=t1[:].rearrange("p (b n) -> p b n", b=4), in0=psb[bank][:].rearrange("p (b n) -> p b n", b=4),
                    in1=sgb3[:, c:c + 1, :].broadcast_to([128, 4, 128]), op=ALU.add),
                    reads=[("ps", bank), "sgb_s"], writes=[t1r])
                p.op("dve", lambda e, c=c, t1=t1: e.tensor_tensor(out=uspT[:, c, :], in0=t1[:], in1=u_g[:, c, :], op=ALU.mult),
                     reads=[t1r, ("u_g", c)], writes=[("uspT", c)])
            for c in range(4):
                ob = 4 if c % 2 == 0 else 6

                def S(kb, c=c):
                    sb0 = 2 * (kb % 2)
                    for hh in range(2):
                        r0 = hh * 64
                        p.op("pe", lambda e: e.matmul(psb[sb0 + hh][:], lhsT=KT[r0:r0 + 64, kb * 128:(kb + 1) * 128],
                                                      rhs=qT[r0:r0 + 64, c, :], start=True, stop=True),
                             reads=[("KT", kb // 4), ("qT", c)], writes=[("ps", sb0 + hh)])

                def E(kb):
                    sb0 = 2 * (kb % 2)
                    pi = state["pt"] % NPT
                    state["pt"] += 1
                    uk = (kb * 128) // UNIT
                    bias = pcol(L * PP_L + 9 + uq * 4 + uk) if nu > 1 else 0.0
                    p.op("act", lambda e: e.activation(out=pT[pi][:], in_=psall[:, sb0 * 512:(sb0 + 2) * 512], func=AF.Exp,
                                                       bias=bias, scale=0.125),
                         reads=[("ps", sb0), ("ps", sb0 + 1), "pp"], writes=[("pT", pi)])
                    return pi

                def PV(kb, pi, ob=ob):
                    for hh in range(2):
                        vcol = 64 * hh
                        p.op("pe", lambda e: e.matmul(psb[ob + hh][:], lhsT=VA[:, kb, vcol:vcol + 128],
                                                      rhs=pT[pi][:, hh * 512:(hh + 1) * 512],
                                                      start=(kb == 0), stop=(kb == nkb - 1)),
                             reads=[("pT", pi), ("VA0", kb // 4), ("VA2", kb // 4), "VA1"], writes=[("ps", ob + hh)])

                S(0)
                if nkb > 1:
                    S(1)
                for kb in range(nkb):
                    pi = E(kb)
                    if kb + 2 < nkb:
                        S(kb + 2)
                    PV(kb, pi)
                for hh in range(2):
                    r0 = hh * 64
                    d0 = 64 - r0
                    obank = ob + hh
                    t1, t1r = T()
                    if c == 3:
                        p.op("act", lambda e: e.activation(out=t1[d0:d0 + 64, :], in_=psb[obank][d0:d0 + 64, :], func=AF.Ln),
                             reads=[("ps", obank)], writes=[t1r])
                        p.op("act", lambda e: e.activation(out=t1[d0:d0 + 64, :], in_=t1[d0:d0 + 64, :], func=AF.Exp, scale=-1.0),
                             reads=[t1r], writes=[t1r])
                    else:
                        p.op("dve", lambda e: e.reciprocal(out=t1[d0:d0 + 64, :], in_=psb[obank][d0:d0 + 64, :]),
                             reads=[("ps", obank)], writes=[t1r])
                    p.op("dve", lambda e: e.tensor_tensor(
                        out=attnT[r0:r0 + 64, c, :], in0=psb[obank][r0:r0 + 64, :], in1=t1[d0:d0 + 64, :], op=ALU.mult),
                        reads=[("ps", obank), t1r], writes=[("attnT", c, hh)])
            wga0 = WG()
            wgb0 = WG()
            wa = WG()
            wb = WG()
            wa3 = wa.t[:].rearrange("p (k n) -> p k n", k=4)
            wb3 = wb.t[:].rearrange("p (k n) -> p k n", k=4)
            wga, wgb = wga0, wgb0
            for m in range(8):
                if m == 4:
                    wga0.done()
                    wgb0.done()
                    wga = WG()
                    wgb = WG()
                banks = [1, 2, 3, 4] if m % 2 == 0 else [5, 6, 7, 0]
                proj_fm(wga, m % 4, banks[0])
                proj_fm(wgb, m % 4, banks[1])
                for kc in range(4):
                    p.op("pe", lambda e, kc=kc, m=m, b=banks[2]: e.matmul(
                        psb[b][:], lhsT=wa3[:, kc, m * 128:(m + 1) * 128], rhs=attnT[:, kc, :],
                        start=(kc == 0), stop=(kc == 3)),
                        reads=[("attnT", kc, 0), ("attnT", kc, 1), wa.res], writes=[("ps", banks[2])])
                for kc in range(4):
                    p.op("pe", lambda e, kc=kc, m=m, b=banks[3]: e.matmul(
                        psb[b][:], lhsT=wb3[:, kc, m * 128:(m + 1) * 128], rhs=uspT[:, kc, :],
                        start=(kc == 0), stop=(kc == 3)),
                        reads=[("uspT", kc), wb.res], writes=[("ps", banks[3])])
                sa, sar = T()
                p.op("act", lambda e, sa=sa, b=banks[0]: e.activation(out=sa[:], in_=psb[b][:], func=AF.Sigmoid),
                     reads=[("ps", banks[0])], writes=[sar])
                sg_, sgr = T()
                p.op("act", lambda e, sg_=sg_, b=banks[1]: e.activation(out=sg_[:], in_=psb[b][:], func=AF.Sigmoid),
                     reads=[("ps", banks[1])], writes=[sgr])
                p.op("dve", lambda e, sa=sa, b=banks[2]: e.tensor_tensor(out=sa[:], in0=sa[:], in1=psb[b][:], op=ALU.mult),
                     reads=[sar, ("ps", banks[2])], writes=[sar])
                p.op("dve", lambda e, sg_=sg_, b=banks[3]: e.tensor_tensor(out=sg_[:], in0=sg_[:], in1=psb[b][:], op=ALU.mult),
                     reads=[sgr, ("ps", banks[3])], writes=[sgr])
                p.op("dve", lambda e, sa=sa, sg_=sg_, m=m: e.tensor_tensor(out=sqT[:, m, 0:TQ], in0=sa[:], in1=sg_[:], op=ALU.add),
                     reads=[sar, sgr], writes=[("sqT", m)])
            wga.done()
            wgb.done()
            wa.done()
            wb.done()
            for half in range(2):
                if half == 1 and i + 1 < len(sched) and sched[i + 1]["kind"] in ("mx", "ffn") and loaded[i + 1]:
                    prologue(i + 1, alt=True, defer_ht=True)
                wg = WG()
                w3 = wg.t[:].rearrange("p (k n) -> p k n", k=KC)
                for mm in range(4):
                    m = half * 4 + mm
                    bank = 1 + (m % 2)
                    for kc in range(KC):
                        p.op("pe", lambda e, kc=kc, mm=mm, bank=bank: e.matmul(
                            psb[bank][:], lhsT=w3[:, kc, mm * 128:(mm + 1) * 128], rhs=sqT[:, kc, 0:TQ],
                            start=(kc == 0), stop=(kc == KC - 1)),
                            reads=[("sqT", kc), wg.res], writes=[("ps", bank)])
                    p.op("dve", lambda e, m=m, bank=bank: e.tensor_tensor(
                        out=xt[bi][:, m, 0:TQ], in0=xt[bi][:, m, 0:TQ], in1=psb[bank][:], op=ALU.add),
                        reads=[("xt", bi, m), ("ps", bank)], writes=[("xt", bi, m)])
                wg.done()
            p.op("pool", lambda e: e.dma_start(out=xTv[src][:, :, c0:c0 + TQ], in_=xt[bi][:, :, 0:TQ]),
                 reads=XT(bi), writes=[("xT", src, c0 // TQ)], dma=("xst", bi))
            prologue_ht(i + 1)
            if i == last_mx[l]:
                layer_consts_mx(l + 1)

        def ffn_tile(i):
            t = sched[i]
            bi = i % 2
            l, W, c0 = t["l"], t["W"], t["c0"]
            dst = 1 - t["src"]
            ppl = l * PP_L
            NW = W + 2
            for gi in range(8):
                wg = WG()
                w3 = wg.t[:].rearrange("p (k n) -> p k n", k=KC)
                for jj in range(2):
                    jf = 2 * gi + jj
                    gb_, vb_ = (1, 2) if jf % 2 == 0 else (3, 4)
                    for (bank, col) in ((gb_, jj), (vb_, 2 + jj)):
                        for kc in range(KC):
                            p.op("pe", lambda e, kc=kc, bank=bank, col=col: e.matmul(
                                psb[bank][:, 0:NW], lhsT=w3[:, kc, col * 128:(col + 1) * 128], rhs=hT[:, kc, 0:NW],
                                start=(kc == 0), stop=(kc == KC - 1)),
                                reads=[("hT", kc), wg.res], writes=[("ps", bank)])
                    res = []
                    for (bank, ch) in ((gb_, jf), (vb_, 16 + jf)):
                        cb = ppl + 18 + ch * 4
                        t1, t1r = T()
                        p.op("act", lambda e, t1=t1, bank=bank, cb=cb: e.activation(
                            out=t1[:, 0:W], in_=psb[bank][:, 1:W + 1], func=AF.Identity, bias=pcol(cb + 3), scale=pcol(cb + 1)),
                            reads=[("ps", bank), "pp"], writes=[t1r])
                        p.op("dve", lambda e, t1=t1, bank=bank, cb=cb: e.scalar_tensor_tensor(
                            out=t1[:, 0:W], in0=psb[bank][:, 0:W], scalar=pcol(cb + 0), in1=t1[:, 0:W],
                            op0=ALU.mult, op1=ALU.add), reads=[("ps", bank), t1r, "pp"], writes=[t1r])
                        p.op("dve", lambda e, t1=t1, bank=bank, cb=cb: e.scalar_tensor_tensor(
                            out=t1[:, 0:W], in0=psb[bank][:, 2:W + 2], scalar=pcol(cb + 2), in1=t1[:, 0:W],
                            op0=ALU.mult, op1=ALU.add), reads=[("ps", bank), t1r, "pp"], writes=[t1r])
                        res.append((t1, t1r))
                    (tg, tgr), (tv, tvr) = res
                    p.op("act", lambda e, tg=tg: e.activation(out=tg[:, 0:W], in_=tg[:, 0:W], func=AF.Gelu_apprx_tanh),
                         reads=[tgr], writes=[tgr])
                    p.op("dve", lambda e, tg=tg, tv=tv, jf=jf: e.tensor_tensor(
                        out=actT[:, jf, 0:W], in0=tg[:, 0:W], in1=tv[:, 0:W], op=ALU.mult),
                        reads=[tgr, tvr], writes=[("actT", jf)])
                wg.done()
            for gi in range(4):
                if gi == 1 and i + 1 < len(sched) and sched[i + 1]["kind"] == "ffn" and loaded[i + 1]:
                    prologue(i + 1, defer_ht=True)
                wg = WG()
                w3 = wg.t[:].rearrange("p (k n) -> p k n", k=16)
                for mm in range(2):
                    m = gi * 2 + mm
                    bank = 5 + (m % 2)
                    for kc in range(16):
                        p.op("pe", lambda e, kc=kc, mm=mm, bank=bank: e.matmul(
                            psb[bank][:, 0:W], lhsT=w3[:, kc, mm * 128:(mm + 1) * 128], rhs=actT[:, kc, 0:W],
                            start=(kc == 0), stop=(kc == 15)),
                            reads=[("actT", kc), wg.res], writes=[("ps", bank)])
                    p.op("dve", lambda e, m=m, bank=bank: e.tensor_tensor(
                        out=xt[bi][:, m, 1:W + 1], in0=xt[bi][:, m, 1:W + 1], in1=psb[bank][:, 0:W], op=ALU.add),
                        reads=[("xt", bi, m), ("ps", bank)], writes=[("xt", bi, m)])
                wg.done()
            tl = list(range(c0 // TQ, (c0 + W - 1) // TQ + 1))
            p.op("pool", lambda e: e.dma_start(out=xTv[dst][:, :, c0:c0 + W], in_=xt[bi][:, :, 1:W + 1]),
                 reads=XT(bi), writes=[("xT", dst, tt) for tt in tl], dma=("xst", bi))
            prologue_ht(i + 1)

        yout = y_d.rearrange("(n p) d -> p n d", p=128)

        def fin_tile(i):
            t = sched[i]
            bi = i % 2
            t0 = t["lo"]
            for th in range(2):
                for tbl in range(2):
                    tb = th * 2 + tbl
                    for half in range(2):
                        bank = 1 + ((tbl * 2 + half) % 4)
                        for k4 in range(4):
                            kc = half * 4 + k4
                            p.op("pe", lambda e, kc=kc, k4=k4, tb=tb, bank=bank: e.transpose(
                                out=psb[bank][:, k4 * 128:(k4 + 1) * 128], in_=xt[bi][:, kc, tb * 128:(tb + 1) * 128], identity=ident[:]),
                                reads=[("xt", bi, kc), "ident"], writes=[("ps", bank)])
                        if half == 0:
                            p.op("dve", lambda e, tbl=tbl, bank=bank: e.tensor_copy(out=ytok[:, tbl, 0:512], in_=psb[bank][:]),
                                 reads=[("ps", bank)], writes=[("ytok", tbl, 0)])
                        else:
                            p.op("act", lambda e, tbl=tbl, bank=bank: e.copy(out=ytok[:, tbl, 512:1024], in_=psb[bank][:]),
                                 reads=[("ps", bank)], writes=[("ytok", tbl, 1)])
                r0 = t0 // 128 + th * 2
                p.op("pool", lambda e, r0=r0: e.dma_start(out=yout[:, r0:r0 + 2, :], in_=ytok),
                     reads=[("ytok", tbl, h) for tbl in range(2) for h in range(2)], dma="yst")

        layer_consts_kv(0)
        layer_consts_mx(0)
        def wr_range(t):
            if t["kind"] == "mx":
                return (t["src"], t["lo"], t["lo"] + TQ)
            if t["kind"] == "ffn":
                return (1 - t["src"], t["c0"], t["c0"] + t["W"])
            return None

        def can_prefetch(i):
            if i + 1 >= len(sched):
                return False
            w = wr_range(sched[i])
            n = sched[i + 1]
            return not (w is not None and w[0] == n["src"] and w[1] < n["lo"] + n["ncol"] and n["lo"] < w[2])

        for i, t in enumerate(sched):
            drain_casts(2)
            emit_load(i)
            if can_prefetch(i):
                emit_load(i + 1)
            prologue(i)
            prologue_ht(i)
            {"kv": kv_tile, "mx": mx_tile, "ffn": ffn_tile, "fin": fin_tile}[t["kind"]](i)

        assert state["wseq"] == len(wlist), (state["wseq"], len(wlist))
        p.emit()
    return nc


def rope_tables(positions):
    half = HD // 2
    freq = (np.float32(10000.0) ** (-np.arange(0, half, 2, dtype=np.float32) / np.float32(half))).astype(np.float32)
    pos = np.asarray(positions)
    row = (pos // 64).astype(np.float32)
    col = (pos % 64).astype(np.float32)
    cos = np.zeros((128, len(pos)), np.float32)
    sin = np.zeros((128, len(pos)), np.float32)
    for pidx in range(128):
        d = pidx % 64
        axis = row if d < 32 else col
        j = d % 32
        i = j % 16
        ang = (axis * freq[i]).astype(np.float32)
        cos[pidx] = np.cos(ang)
        s = np.sin(ang)
        sin[pidx] = -s if j < 16 else s
    return cos, sin


def rot_matrix():
    r = np.zeros((128, 128), np.float32)
    for pidx in range(128):
        partner = pidx + 16 if (pidx % 32) < 16 else pidx - 16
        r[partner, pidx] = 1.0
    return r


def img(w):
    K, N = w.shape
    return np.ascontiguousarray(w.reshape(K // 128, 128, N).transpose(1, 0, 2)).reshape(128, -1)


def prep_weights(inp, cfg):
    L = cfg.L
    wmain = np.empty((L * NG_MAIN, 128, 4096), np.float32)
    wkv = np.empty((L, 128, 2048), np.float32)
    sgw = np.empty((L, 128, 1024), np.float32)
    sgg = np.empty((L, 128, 512), np.float32)
    sgb = np.empty((L, 128, 512), np.float32)
    pp = np.zeros((128, cfg.npp), np.float32)
    qperm = np.concatenate([np.concatenate([np.arange(c * 64, c * 64 + 64), np.arange((4 + c) * 64, (4 + c) * 64 + 64)])
                            for c in range(4)])
    for l in range(L):
        w_in = np.asarray(inp["w_in"][l])
        b = l * NG_MAIN
        wmain[b + G_Q] = img(w_in[:, 0:512][:, qperm])
        wkv[l] = img(w_in[:, 512:768])
        wmain[b + G_U] = img(w_in[:, 768:1280])
        wmain[b + G_VS] = img(w_in[:, 1280:1792])
        wmain[b + G_GA0] = img(w_in[:, 1792:2304])
        wmain[b + G_GA1] = img(w_in[:, 2304:2816])
        wmain[b + G_GB0] = img(w_in[:, 2816:3328])
        wmain[b + G_GB1] = img(w_in[:, 3328:3840])
        wmain[b + G_WA] = img(np.asarray(inp["w_branch_a"][l])[qperm, :])
        wmain[b + G_WB] = img(np.asarray(inp["w_branch_b"][l]))
        wm = np.asarray(inp["w_mix_out"][l])
        wmain[b + G_MIX] = img(wm[:, 0:512])
        wmain[b + G_MIX + 1] = img(wm[:, 512:1024])
        wu = np.asarray(inp["w_up"][l])
        for i in range(8):
            cols = np.concatenate([np.arange(2 * i * 128, (2 * i + 2) * 128), 2048 + np.arange(2 * i * 128, (2 * i + 2) * 128)])
            wmain[b + G_UP + i] = img(wu[:, cols])
        wd = np.asarray(inp["w_down"][l])
        for i in range(4):
            wmain[b + G_DN + i] = img(wd[:, i * 256:(i + 1) * 256])
        sw = np.asarray(inp["sg_w"][l])
        sgw[l] = np.ascontiguousarray(sw.transpose(2, 0, 1)).reshape(128, 1024)
        sgg[l] = np.broadcast_to(np.asarray(inp["sg_norm_g"][l])[None, :], (128, 512))
        sbias = np.asarray(inp["sg_b"][l])
        t = np.empty((128, 4, 128), np.float32)
        for c in range(4):
            t[0:64, c, :] = sbias[2 * c][None, :]
            t[64:128, c, :] = sbias[2 * c + 1][None, :]
        sgb[l] = t.reshape(128, 512)
        o = l * PP_L
        pp[:, o:o + 8] = np.asarray(inp["attn_norm_g"][l]).reshape(8, 128).T
        pp[:, o + 8:o + 16] = np.asarray(inp["ffn_norm_g"][l]).reshape(8, 128).T
        pp[:, o + 16] = np.tile(np.asarray(inp["q_norm_g"][l]), 2)
        pp[:, o + 17] = np.tile(np.asarray(inp["k_norm_g"][l]), 2)
        cw = np.asarray(inp["conv_w"][l])
        cb = np.asarray(inp["conv_b"][l])
        cv = np.stack([cw[0], cw[1], cw[2], cb], axis=-1).reshape(32, 128, 4).transpose(1, 0, 2)
        pp[:, o + 18:o + 18 + 128] = cv.reshape(128, 128)
    o = L * PP_L
    pp[:, o:o + 8] = np.asarray(inp["final_norm_g"]).reshape(8, 128).T
    return dict(wmain=wmain, wkv=wkv, sgw=sgw, sgg=sgg, sgb=sgb), pp


def core_inputs(shared, pp_base, cfg, xa_list, xb_list, a_is_one_seq):
    L = cfg.L
    x = np.concatenate([np.asarray(a) for a in xa_list] + [np.asarray(b) for b in xb_list], axis=0).astype(np.float32)
    pos = []
    if a_is_one_seq:
        pos.append(np.arange(cfg.a_units * cfg.unit))
    else:
        for _ in range(cfg.a_units):
            pos.append(np.arange(cfg.unit))
    for _ in range(cfg.n_b):
        pos.append(np.arange(cfg.unit))
    cos, sin = rope_tables(np.concatenate(pos))
    pp = pp_base.copy()
    o = L * PP_L
    pp[:, o + 8] = 1.0 if a_is_one_seq else 0.0
    for uq in range(4):
        for uk in range(4):
            pp[:, o + 9 + uq * 4 + uk] = 0.0 if (a_is_one_seq or uq == uk) else -30000.0
    m = dict(shared)
    m.update(x=np.ascontiguousarray(x), pp=pp, ropec=cos, ropes=sin,
             ident=np.eye(128, dtype=np.float32), rotm=rot_matrix())
    return m


_NC_CACHE = {}


def kernel(**inputs):
    cfg = Cfg()
    shared, pp_base = prep_weights(inputs, cfg)
    xp = np.asarray(inputs["x_prompt"])
    xs = np.asarray(inputs["x_sample"])
    in_maps = []
    for c in range(8):
        if c < 4:
            in_maps.append(core_inputs(shared, pp_base, cfg, [xp[c]], [xs[2 * c], xs[2 * c + 1]], True))
        else:
            b = 8 + 6 * (c - 4)
            in_maps.append(core_inputs(shared, pp_base, cfg, [xs[b + i] for i in range(4)], [xs[b + 4], xs[b + 5]], False))
    nc = build(cfg)
    res = run_bass_kernel_spmd(nc, in_maps, core_ids=list(range(8)))
    yp = np.empty(xp.shape, np.float32)
    ys = np.empty(xs.shape, np.float32)
    U = cfg.unit
    for c in range(8):
        y = np.asarray(res.results[c]["y"])
        if c < 4:
            yp[c] = y[0:4 * U]
            ys[2 * c] = y[4 * U:5 * U]
            ys[2 * c + 1] = y[5 * U:6 * U]
        else:
            b = 8 + 6 * (c - 4)
            for i in range(6):
                ys[b + i] = y[i * U:(i + 1) * U]
    return (yp, ys)
```

```python
import contextlib
import math
import types
import numpy as np
import concourse.bass as bass
import concourse.mybir as mybir
from concourse.bass_utils import run_bass_kernel_spmd

F32 = mybir.dt.float32
BF16 = mybir.dt.bfloat16
AF = mybir.ActivationFunctionType
ALU = mybir.AluOpType

ENGS = ("pe", "act", "dve", "pool", "sp")
SEM_EPOCH = 30000

D = 1024
KC = 8
HD = 64
EPS = 1e-6
NG_MAIN = 23
G_VS, G_U, G_Q, G_GA0, G_GB0, G_WA, G_WB, G_GA1, G_GB1, G_MIX, G_UP, G_DN = 0, 1, 2, 3, 4, 5, 6, 7, 8, 9, 11, 19
PP_L = 8 + 8 + 1 + 1 + 128


class Prog:
    def __init__(self, nc):
        self.nc = nc
        self.ops = []
        self.last_w = {}
        self.readers = {}
        self.eng_count = {e: 0 for e in ENGS}

    @staticmethod
    def _freeze(fn):
        if fn.__closure__ is None:
            return fn
        cells = []
        for c in fn.__closure__:
            try:
                cells.append(types.CellType(c.cell_contents))
            except ValueError:
                cells.append(c)
        return types.FunctionType(fn.__code__, fn.__globals__, fn.__name__, fn.__defaults__, tuple(cells))

    def _key(self, oid):
        o = self.ops[oid]
        return ("dma", o["dma"]) if o["dma"] is not None else ("eng", o["eng"])

    def op(self, eng, fn, reads=(), writes=(), dma=None):
        oid = len(self.ops)
        deps = {}

        def add(d):
            k = self._key(d)
            if deps.get(k, -1) < d:
                deps[k] = d
        for r in reads:
            if r in self.last_w:
                add(self.last_w[r])
        for w in writes:
            if w in self.last_w:
                add(self.last_w[w])
            for rd in self.readers.get(w, {}).values():
                add(rd)
        for w in writes:
            self.last_w[w] = oid
            self.readers[w] = {}
        self.ops.append(dict(eng=eng, fn=self._freeze(fn), deps=set(deps.values()), dma=dma, lidx=self.eng_count[eng]))
        k = self._key(oid)
        for r in reads:
            self.readers.setdefault(r, {})[k] = oid
        self.eng_count[eng] += 1
        return oid

    @staticmethod
    def _skippable(do, o):
        if do["dma"] is not None or o["dma"] is not None or do["eng"] != o["eng"]:
            return False
        if o["eng"] == "pe":
            return True
        return o["lidx"] - do["lidx"] >= 3

    def emit(self):
        nc = self.nc
        ops = self.ops
        need_inc = [False] * len(ops)
        for o in ops:
            for d in o["deps"]:
                if not self._skippable(ops[d], o):
                    need_inc[d] = True
        dma_keys = []
        for i, o in enumerate(ops):
            if o["dma"] is not None:
                need_inc[i] = True
                if o["dma"] not in dma_keys:
                    dma_keys.append(o["dma"])
        n_inc_eng = {e: 0 for e in ENGS}
        for i, o in enumerate(ops):
            if o["dma"] is None and need_inc[i]:
                n_inc_eng[o["eng"]] += 1
        sem_names = []
        for e in ENGS:
            for k in range((n_inc_eng[e] + SEM_EPOCH - 1) // SEM_EPOCH):
                sem_names.append(("eng", e, k))
        for k in dma_keys:
            sem_names.append(("dma", k))
        self.n_sems = len(sem_names)
        stack = contextlib.ExitStack()
        sems = {}
        for i, sn in enumerate(sem_names):
            sems[sn] = stack.enter_context(nc.semaphore("s%d" % i))
        cnt_eng = {e: 0 for e in ENGS}
        cnt_dma = {k: 0 for k in dma_keys}
        for i, o in enumerate(ops):
            if not need_inc[i]:
                o["sem"] = None
            elif o["dma"] is not None:
                cnt_dma[o["dma"]] += 16
                o["sem"] = (("dma", o["dma"]), cnt_dma[o["dma"]])
            else:
                n = cnt_eng[o["eng"]]
                cnt_eng[o["eng"]] += 1
                o["sem"] = (("eng", o["eng"], n // SEM_EPOCH), n % SEM_EPOCH + 1)
        per_eng = {e: [] for e in ENGS}
        for i, o in enumerate(ops):
            per_eng[o["eng"]].append(i)
        final_dma = dict(cnt_dma)

        def run_engine(e, eh):
            waited = {}
            maxep = {}
            for i in per_eng[e]:
                o = ops[i]
                w = {}
                for d in o["deps"]:
                    do = ops[d]
                    if do["sem"] is None or self._skippable(do, o):
                        continue
                    sn, v = do["sem"]
                    if waited.get(sn, 0) >= v:
                        continue
                    if sn[0] == "eng" and maxep.get(sn[1], -1) > sn[2]:
                        continue
                    if w.get(sn, 0) < v:
                        w[sn] = v
                for sn, v in w.items():
                    eh.wait_ge(sems[sn], v)
                    waited[sn] = v
                    if sn[0] == "eng":
                        maxep[sn[1]] = max(maxep.get(sn[1], -1), sn[2])
                ins = o["fn"](eh)
                if o["sem"] is not None:
                    sn, v = o["sem"]
                    ins.then_inc(sems[sn], 16 if o["dma"] is not None else 1)
            if e == "sp":
                for k, v in final_dma.items():
                    if v > 0:
                        eh.wait_ge(sems[("dma", k)], v)

        with nc.Block() as block:
            @block.tensor
            def _(eh):
                run_engine("pe", eh)

            @block.scalar
            def _(eh):
                run_engine("act", eh)

            @block.vector
            def _(eh):
                run_engine("dve", eh)

            @block.gpsimd
            def _(eh):
                run_engine("pool", eh)

            @block.sync
            def _(eh):
                run_engine("sp", eh)
        stack.close()


def ffn_widths(unit):
    n = -(-unit // 510)
    base = unit // n
    rem = unit - base * n
    return [base + (1 if i < rem else 0) for i in range(n)]


class Cfg:
    def __init__(self, L=4, unit=2048, a_units=4, n_b=2, nring=6):
        self.L = L
        self.unit = unit
        self.a_units = a_units
        self.n_b = n_b
        self.nring = nring
        self.seqs = [(0, a_units)]
        off = a_units * unit
        for _ in range(n_b):
            self.seqs.append((off, 1))
            off += unit
        self.ntok = off
        self.maxkeys = a_units * unit
        self.npp = L * PP_L + 8 + 1 + 16


def build(cfg):
    nc = bass.Bass("TRN2", target_bir_lowering=False)
    L, NT, UNIT = cfg.L, cfg.ntok, cfg.unit
    TQ = 512
    NKB = cfg.maxkeys // 128

    def din(name, shape, dt=F32):
        return nc.dram_tensor(name, shape, dt, kind="ExternalInput").ap()

    x_d = din("x", [NT, D])
    wmain_d = din("wmain", [L * NG_MAIN, 128, 4096])
    wkv_d = din("wkv", [L, 128, 2048])
    sgw_d = din("sgw", [L, 128, 1024])
    sgg_d = din("sgg", [L, 128, 512])
    sgb_d = din("sgb", [L, 128, 512])
    pp_d = din("pp", [128, cfg.npp])
    cos_d = din("ropec", [128, NT])
    sin_d = din("ropes", [128, NT])
    ident_d = din("ident", [128, 128])
    rotm_d = din("rotm", [128, 128])
    y_d = nc.dram_tensor("y", [NT, D], F32, kind="ExternalOutput").ap()

    def dint(name, shape, dt):
        return nc.dram_tensor(name, shape, dt, kind="Internal").ap()

    wmain_b = dint("wmain_b", [L * NG_MAIN, 128, 4096], BF16)
    wkv_b = dint("wkv_b", [L, 128, 2048], BF16)
    sgw_b = dint("sgw_b", [L, 128, 1024], BF16)
    xT = [dint("xTa", [KC, 128, NT], F32), dint("xTb", [KC, 128, NT], F32)]
    xTv = [t.rearrange("k p t -> p k t") for t in xT]

    st = contextlib.ExitStack()
    with st:
        def sb(name, shape, dt):
            return st.enter_context(nc.sbuf_tensor("sb_" + name, shape, dt))

        ring = [sb("ring%d" % i, [128, 4096], BF16) for i in range(cfg.nring)]
        wkv_s = sb("wkv_s", [128, 2048], BF16)
        sgw_s = sb("sgw_s", [128, 1024], BF16)
        sgg_s = sb("sgg_s", [128, 512], F32)
        sgb_s = sb("sgb_s", [128, 512], F32)
        pp = sb("pp", [128, cfg.npp], F32)
        ident = sb("ident", [128, 128], F32)
        rot_f = sb("rot_f", [128, 128], F32)
        rot_b = sb("rot_b", [128, 128], BF16)
        ones_b = sb("ones_b", [128, 128], BF16)
        blk_b = sb("blk_b", [128, 128], BF16)
        KT = sb("KT", [128, max(cfg.maxkeys, 8192)], BF16)
        actT = KT[:, 0:8192].rearrange("p (j n) -> p j n", j=16)
        VA = sb("VA", [128, NKB, 192], BF16)
        xt = [sb("xt%d" % i, [128, KC, 514], F32) for i in range(2)]
        hT = sb("hT", [128, KC, 514], BF16)
        sqT = sb("sqT", [128, KC, 514], BF16)
        rstd = sb("rstd", [128, 514], F32)
        cosb = [sb("cos%d" % i, [128, 512], F32) for i in range(2)]
        sinb = [sb("sin%d" % i, [128, 512], F32) for i in range(2)]
        bbuf = sb("bbuf", [128, 16, 512], BF16)
        qT = bbuf[:, 0:4, :]
        attnT = bbuf[:, 4:8, :]
        uspT = bbuf[:, 8:12, :]
        vsn = bbuf[:, 12:16, :]
        fbuf = sb("fbuf", [128, 2048], F32)
        u_g = fbuf[:].rearrange("p (c n) -> p c n", c=4)
        ytok = fbuf[:].rearrange("p (b d) -> p b d", b=2)
        NTMP = 6
        tmp = [sb("tmp%d" % i, [128, 512], F32) for i in range(NTMP)]
        tb16 = [sb("tb16_%d" % i, [128, 512], BF16) for i in range(2)]
        NPT = 4
        pT = [sb("pT%d" % i, [128, 1024], BF16) for i in range(NPT)]
        small = sb("small", [128, 16], F32)
        psall = st.enter_context(nc.psum_tensor("psall", [128, 8 * 512], F32))
        psb = [psall[:, i * 512:(i + 1) * 512] for i in range(8)]

        p = Prog(nc)
        state = dict(tmp=0, pt=0, tb=0, wseq=0, wload=0)

        def T():
            i = state["tmp"] % NTMP
            state["tmp"] += 1
            return tmp[i], "tmp%d" % i

        def TB():
            i = state["tb"] % 2
            state["tb"] += 1
            return tb16[i], "tb16_%d" % i

        wlist = []

        def ring_load(seq_idx):
            if seq_idx >= len(wlist):
                return
            slot = seq_idx % cfg.nring
            gi = wlist[seq_idx]
            p.op("sp", lambda e, slot=slot, gi=gi: e.dma_start(out=ring[slot][:], in_=wmain_b[gi]),
                 reads=[("wready", gi // NG_MAIN)], writes=[("ring", slot)], dma=("ring", slot))

        class WG:
            def __init__(self):
                self.idx = state["wseq"]
                state["wseq"] += 1
                self.slot = self.idx % cfg.nring
                self.t = ring[self.slot]
                self.res = ("ring", self.slot)

            def done(self):
                ring_load(self.idx + cfg.nring)

        def sched_groups():
            out = []
            for l in range(L):
                for (soff, nu) in cfg.seqs:
                    ntile = nu * UNIT // TQ
                    for _ in range(ntile):
                        out += [l * NG_MAIN + g for g in range(0, G_UP)]
                    for _u in range(nu):
                        for _w in ffn_widths(UNIT):
                            out += [l * NG_MAIN + g for g in range(G_UP, NG_MAIN)]
            return out

        wlist.extend(sched_groups())

        p.op("sp", lambda e: e.dma_start(out=pp[:], in_=pp_d), writes=["pp"], dma="c0")
        p.op("sp", lambda e: e.dma_start(out=ident[:], in_=ident_d), writes=["ident"], dma="c1")
        p.op("sp", lambda e: e.dma_start(out=rot_f[:], in_=rotm_d), writes=["rot_f"], dma="c2")
        p.op("dve", lambda e: e.tensor_copy(out=rot_b[:], in_=rot_f[:]), reads=["rot_f"], writes=["rot_b"])
        p.op("dve", lambda e: e.memset(ones_b[:], 1.0), writes=["ones_b"])
        p.op("dve", lambda e: e.memset(blk_b[:], 0.0), writes=["blk_b"])
        p.op("dve", lambda e: e.memset(blk_b[0:64, 0:64], 1.0), reads=["blk_b"], writes=["blk_b0"])
        p.op("dve", lambda e: e.memset(blk_b[64:128, 64:128], 1.0), reads=["blk_b"], writes=["blk_b1"])
        p.op("pool", lambda e: e.memset(VA[:, :, 64:128], 1.0), writes=["VA1"])
        BLK = ["blk_b", "blk_b0", "blk_b1"]
        cast_q = []

        def cast_ops(l):
            ops_ = []
            ops_.append(lambda: p.op("pool", lambda e: e.dma_start(out=wkv_b[l], in_=wkv_d[l]), writes=[("wcast", l, -1)], dma=("cast", l)))
            ops_.append(lambda: p.op("pool", lambda e: e.dma_start(out=sgw_b[l], in_=sgw_d[l]), writes=[("wcast", l, -2)], dma=("cast", l)))
            for g in range(NG_MAIN):
                gi = l * NG_MAIN + g
                ops_.append(lambda gi=gi, g=g: p.op("pool", lambda e: e.dma_start(out=wmain_b[gi], in_=wmain_d[gi]),
                                                    writes=[("wcast", l, g)], dma=("cast", l)))
            ops_.append(lambda: p.op("pool", lambda e: e.memset(small[:, 12 + l % 4:13 + l % 4], 0.0),
                                     reads=[("wcast", l, g) for g in range(-2, NG_MAIN)], writes=[("wready", l)]))
            return ops_

        for f in cast_ops(0):
            f()
        for l in range(1, L):
            cast_q.extend(cast_ops(l))

        def drain_casts(n):
            for _ in range(min(n, len(cast_q))):
                cast_q.pop(0)()

        xin = x_d.rearrange("(n p) d -> p n d", p=128)
        HS = 256
        for t0 in range(0, NT, HS):
            bi = (t0 // HS) % 2
            p.op("sp", lambda e, t0=t0: e.dma_start(out=ytok, in_=xin[:, t0 // 128:t0 // 128 + 2, :]),
                 writes=["ytok"], dma="xin")
            for kc in range(KC):
                bank = 1 + (kc % 4)
                for tb in range(2):
                    p.op("pe", lambda e, kc=kc, tb=tb, bank=bank: e.transpose(
                        out=psb[bank][:, tb * 128:(tb + 1) * 128], in_=ytok[:, tb, kc * 128:(kc + 1) * 128],
                        identity=ident[:]), reads=["ytok", "ident"], writes=[("ps", bank)])
                if kc % 2 == 0:
                    p.op("dve", lambda e, kc=kc, bank=bank, bi=bi: e.tensor_copy(out=xt[bi][:, kc, 0:HS], in_=psb[bank][:, 0:HS]),
                         reads=[("ps", bank)], writes=[("xt", bi, kc)])
                else:
                    p.op("act", lambda e, kc=kc, bank=bank, bi=bi: e.copy(out=xt[bi][:, kc, 0:HS], in_=psb[bank][:, 0:HS]),
                         reads=[("ps", bank)], writes=[("xt", bi, kc)])
            p.op("sp", lambda e, t0=t0, bi=bi: e.dma_start(out=xTv[0][:, :, t0:t0 + HS], in_=xt[bi][:, :, 0:HS]),
                 reads=[("xt", bi, kc) for kc in range(KC)], writes=[("xT", 0, t0 // TQ)], dma=("xst", bi))

        for i in range(cfg.nring):
            ring_load(i)
        state["wload"] = cfg.nring

        def pcol(i):
            return pp[:, i:i + 1]

        def load_x(src, c0, ncol, bi, dst0=0):
            p.op("pool", lambda e: e.dma_start(out=xt[bi][:, :, dst0:dst0 + ncol], in_=xTv[src][:, :, c0:c0 + ncol]),
                 reads=[("xT", src, t) for t in range(c0 // TQ, (c0 + ncol - 1) // TQ + 1)],
                 writes=[("xt", bi, kc) for kc in range(KC)], dma=("xld", bi))

        XT = lambda bi: [("xt", bi, kc) for kc in range(KC)]
        HT = [("hT", kc) for kc in range(KC)]

        sched = []
        for l in range(L):
            src = l % 2
            for si, (soff, nu) in enumerate(cfg.seqs):
                ntile = nu * UNIT // TQ
                for j in range(ntile):
                    sched.append(dict(kind="kv", l=l, si=si, j=j, src=src, lo=soff + j * TQ, ncol=TQ, dst0=0, rope=soff + j * TQ,
                                      ntile=ntile, nu=nu, soff=soff))
                for j in range(ntile):
                    sched.append(dict(kind="mx", l=l, si=si, j=j, src=src, lo=soff + j * TQ, ncol=TQ, dst0=0, rope=soff + j * TQ,
                                      ntile=ntile, nu=nu, soff=soff))
                widths = ffn_widths(UNIT)
                for u in range(nu):
                    woff = 0
                    for wi, W in enumerate(widths):
                        c0 = soff + u * UNIT + woff
                        woff += W
                        left_edge = wi == 0
                        right_edge = wi == len(widths) - 1
                        seq_left = left_edge and u == 0
                        seq_right = right_edge and u == nu - 1
                        lo = c0 - 1 + (1 if seq_left else 0)
                        hi = c0 + W + 1 - (1 if seq_right else 0)
                        sched.append(dict(kind="ffn", l=l, si=si, src=src, lo=lo, ncol=hi - lo, dst0=(1 if seq_left else 0), rope=None,
                                          c0=c0, W=W, left_edge=left_edge, right_edge=right_edge, seq_left=seq_left, seq_right=seq_right))
        for t0 in range(0, NT, TQ):
            sched.append(dict(kind="fin", l=L, src=L % 2, lo=t0, ncol=TQ, dst0=0, rope=None))
        last_kv = {}
        last_mx = {}
        for i, t in enumerate(sched):
            if t["kind"] == "kv":
                last_kv[t["l"]] = i
            if t["kind"] == "mx":
                last_mx[t["l"]] = i
        loaded = [False] * len(sched)
        prologued = [False] * len(sched)

        def emit_load(i):
            if i >= len(sched) or loaded[i]:
                return
            loaded[i] = True
            t = sched[i]
            bi = i % 2
            if t["kind"] == "ffn":
                NW = t["W"] + 2
                if t["seq_left"]:
                    p.op("pool", lambda e: e.memset(xt[bi][:, :, 0:1], 0.0), writes=XT(bi))
                if t["seq_right"]:
                    p.op("pool", lambda e: e.memset(xt[bi][:, :, NW - 1:NW], 0.0), writes=XT(bi))
            load_x(t["src"], t["lo"], t["ncol"], bi, dst0=t["dst0"])
            if t["rope"] is not None:
                c0 = t["rope"]
                p.op("pool", lambda e: e.dma_start(out=cosb[bi][:], in_=cos_d[:, c0:c0 + TQ]), writes=[("cos", bi)], dma=("cos", bi))
                p.op("pool", lambda e: e.dma_start(out=sinb[bi][:], in_=sin_d[:, c0:c0 + TQ]), writes=[("sin", bi)], dma=("sin", bi))

        def layer_consts_kv(l):
            if l < L:
                p.op("sp", lambda e: e.dma_start(out=wkv_s[:], in_=wkv_b[l]), reads=[("wready", l)], writes=["wkv_s"], dma="lw0")

        def layer_consts_mx(l):
            if l < L:
                p.op("sp", lambda e: e.dma_start(out=sgw_s[:], in_=sgw_b[l]), reads=[("wready", l)], writes=["sgw_s"], dma="lw1")
                p.op("sp", lambda e: e.dma_start(out=sgg_s[:], in_=sgg_d[l]), writes=["sgg_s"], dma="lw2")
                p.op("sp", lambda e: e.dma_start(out=sgb_s[:], in_=sgb_d[l]), writes=["sgb_s"], dma="lw3")

        ht_done = [False] * len(sched)

        def prologue_ht(i):
            if i >= len(sched) or ht_done[i] or not prologued[i]:
                return
            ht_done[i] = True
            t = sched[i]
            bi = i % 2
            if t["kind"] == "ffn":
                ncol = t["W"] + 2
                gbase = t["l"] * PP_L + 8
            else:
                ncol = TQ
                gbase = t["l"] * PP_L
            for kc in range(KC):
                p.op("dve", lambda e, kc=kc: e.scalar_tensor_tensor(
                    out=hT[:, kc, 0:ncol], in0=xt[bi][:, kc, 0:ncol], scalar=pcol(gbase + kc), in1=rstd[:, 0:ncol],
                    op0=ALU.mult, op1=ALU.mult), reads=[("xt", bi, kc), "rstd", "pp"], writes=[("hT", kc)])

        def prologue(i, alt=False, defer_ht=False):
            if i >= len(sched) or prologued[i]:
                return

            def sq_ap(kc, n):
                return bbuf[:, kc, 0:n] if alt else sqT[:, kc, 0:n]

            def sq_res(kc):
                if not alt:
                    return [("sqT", kc)]
                return [("qT", kc)] if kc < 4 else [("attnT", kc - 4, 0), ("attnT", kc - 4, 1)]
            prologued[i] = True
            t = sched[i]
            bi = i % 2
            l = t["l"]
            ncol = t["ncol"] + t["dst0"] + (1 if t.get("seq_right") else 0) if t["kind"] == "ffn" else TQ
            if t["kind"] == "ffn":
                ncol = t["W"] + 2
                gbase = l * PP_L + 8
                if t["left_edge"] and not t["seq_left"]:
                    p.op("dve", lambda e: e.tensor_scalar(out=xt[bi][:, :, 0:1], in0=xt[bi][:, :, 0:1],
                                                          scalar1=pcol(L * PP_L + 8), scalar2=None, op0=ALU.mult),
                         reads=XT(bi) + ["pp"], writes=XT(bi))
                if t["right_edge"] and not t["seq_right"]:
                    p.op("dve", lambda e: e.tensor_scalar(out=xt[bi][:, :, ncol - 1:ncol], in0=xt[bi][:, :, ncol - 1:ncol],
                                                          scalar1=pcol(L * PP_L + 8), scalar2=None, op0=ALU.mult),
                         reads=XT(bi) + ["pp"], writes=XT(bi))
            elif t["kind"] == "fin":
                gbase = L * PP_L
            else:
                gbase = l * PP_L
            for kc in range(KC):
                p.op("act", lambda e, kc=kc: e.activation(out=sq_ap(kc, ncol), in_=xt[bi][:, kc, 0:ncol], func=AF.Square),
                     reads=[("xt", bi, kc)], writes=sq_res(kc))
            for kc in range(KC):
                p.op("pe", lambda e, kc=kc: e.matmul(psb[0][:, 0:ncol], lhsT=ones_b[:], rhs=sq_ap(kc, ncol),
                                                     start=(kc == 0), stop=(kc == KC - 1)),
                     reads=sq_res(kc) + ["ones_b"], writes=[("ps", 0)])
            p.op("act", lambda e: e.activation(out=rstd[:, 0:ncol], in_=psb[0][:, 0:ncol], func=AF.Ln, bias=EPS, scale=1.0 / D),
                 reads=[("ps", 0)], writes=["rstd"])
            p.op("act", lambda e: e.activation(out=rstd[:, 0:ncol], in_=rstd[:, 0:ncol], func=AF.Exp, scale=-0.5),
                 reads=["rstd"], writes=["rstd"])
            if t["kind"] == "fin":
                for kc in range(KC):
                    p.op("dve", lambda e, kc=kc: e.scalar_tensor_tensor(
                        out=xt[bi][:, kc, 0:TQ], in0=xt[bi][:, kc, 0:TQ], scalar=pcol(gbase + kc), in1=rstd[:, 0:TQ],
                        op0=ALU.mult, op1=ALU.mult), reads=[("xt", bi, kc), "rstd", "pp"], writes=[("xt", bi, kc)])
            elif not defer_ht:
                prologue_ht(i)

        wkv3 = wkv_s[:].rearrange("p (k n) -> p k n", k=KC)
        sgw3 = sgw_s[:].rearrange("p (g n) -> p g n", g=8)
        sgb3 = sgb_s[:].rearrange("p (c n) -> p c n", c=4)

        def head_chain_a(src_bank, gcol):
            sq, sqr = TB()
            p.op("act", lambda e: e.activation(out=sq[:], in_=psb[src_bank][:], func=AF.Square),
                 reads=[("ps", src_bank)], writes=[sqr])
            p.op("pe", lambda e: e.matmul(psb[6][:], lhsT=blk_b[:], rhs=sq[:], start=True, stop=True),
                 reads=[sqr] + BLK, writes=[("ps", 6)])
            t1, t1r = T()
            p.op("act", lambda e: e.activation(out=t1[:], in_=psb[6][:], func=AF.Ln, bias=EPS, scale=1.0 / HD),
                 reads=[("ps", 6)], writes=[t1r])
            p.op("act", lambda e: e.activation(out=t1[:], in_=t1[:], func=AF.Exp, scale=-0.5), reads=[t1r], writes=[t1r])
            kn, knr = TB()
            p.op("dve", lambda e: e.scalar_tensor_tensor(out=kn[:], in0=psb[src_bank][:], scalar=pcol(gcol), in1=t1[:],
                                                         op0=ALU.mult, op1=ALU.mult),
                 reads=[("ps", src_bank), t1r, "pp"], writes=[knr])
            return kn, knr

        def head_chain_b(kn, knr, cs, out_ap, out_res):
            p.op("pe", lambda e: e.matmul(psb[7][:], lhsT=rot_b[:], rhs=kn[:], start=True, stop=True),
                 reads=[knr, "rot_b"], writes=[("ps", 7)])
            t2, t2r = T()
            p.op("dve", lambda e: e.tensor_tensor(out=t2[:], in0=kn[:], in1=cosb[cs][:], op=ALU.mult),
                 reads=[knr, ("cos", cs)], writes=[t2r])
            t3, t3r = T()
            p.op("dve", lambda e: e.tensor_tensor(out=t3[:], in0=psb[7][:], in1=sinb[cs][:], op=ALU.mult),
                 reads=[("ps", 7), ("sin", cs)], writes=[t3r])
            p.op("dve", lambda e: e.tensor_tensor(out=out_ap, in0=t2[:], in1=t3[:], op=ALU.add),
                 reads=[t2r, t3r], writes=[out_res])

        def proj_fm(wg, mc, bank):
            w3 = wg.t[:].rearrange("p (k n) -> p k n", k=KC)
            for kc in range(KC):
                p.op("pe", lambda e, kc=kc: e.matmul(psb[bank][:], lhsT=w3[:, kc, mc * 128:(mc + 1) * 128],
                                                     rhs=hT[:, kc, 0:TQ], start=(kc == 0), stop=(kc == KC - 1)),
                     reads=[("hT", kc), wg.res], writes=[("ps", bank)])

        kvproj_done = [False] * len(sched)

        def kv_proj(i):
            if kvproj_done[i]:
                return
            kvproj_done[i] = True
            kb_, vb_ = (1, 2) if i % 2 == 0 else (3, 4)
            for kc in range(KC):
                p.op("pe", lambda e, kc=kc: e.matmul(psb[kb_][:], lhsT=wkv3[:, kc, 0:128], rhs=hT[:, kc, 0:TQ],
                                                     start=(kc == 0), stop=(kc == KC - 1)),
                     reads=[("hT", kc), "wkv_s"], writes=[("ps", kb_)])
            for tb in range(4):
                for kc in range(KC):
                    p.op("pe", lambda e, kc=kc, tb=tb: e.matmul(
                        psb[vb_][:, tb * 128:(tb + 1) * 128], lhsT=hT[:, kc, tb * 128:(tb + 1) * 128],
                        rhs=wkv3[:, kc, 128:256], start=(kc == 0), stop=(kc == KC - 1)),
                        reads=[("hT", kc), "wkv_s"], writes=[("ps", vb_)])

        def kv_tile(i):
            t = sched[i]
            bi = i % 2
            l, j = t["l"], t["j"]
            ppl = l * PP_L
            kb_, vb_ = (1, 2) if i % 2 == 0 else (3, 4)
            kv_proj(i)
            if i + 1 < len(sched) and sched[i + 1]["kind"] == "kv" and loaded[i + 1]:
                prologue(i + 1)
                kv_proj(i + 1)
            kn, knr = head_chain_a(kb_, ppl + 17)
            head_chain_b(kn, knr, bi, KT[:, j * TQ:(j + 1) * TQ], ("KT", j))
            kb0 = j * 4
            v4 = psb[vb_][:].rearrange("p (b g d) -> p b g d", b=4, g=2)
            p.op("dve", lambda e: e.tensor_copy(out=VA[:, kb0:kb0 + 4, 0:64], in_=v4[:, :, 0, :]),
                 reads=[("ps", vb_)], writes=[("VA0", j)])
            p.op("dve", lambda e: e.tensor_copy(out=VA[:, kb0:kb0 + 4, 128:192], in_=v4[:, :, 1, :]),
                 reads=[("ps", vb_)], writes=[("VA2", j)])
            if i == last_kv[l]:
                layer_consts_kv(l + 1)

        def mx_tile(i):
            t = sched[i]
            bi = i % 2
            l, j, nu, soff = t["l"], t["j"], t["nu"], t["soff"]
            src = t["src"]
            c0 = t["lo"]
            ppl = l * PP_L
            nkb = nu * UNIT // 128
            uq = j // (UNIT // TQ)
            wv = WG()
            wu = WG()
            wq = WG()
            ubank = [5, 6, 7, 0]
            w3 = wv.t[:].rearrange("p (k n) -> p k n", k=KC)
            for tb in range(4):
                for kc in range(KC):
                    p.op("pe", lambda e, kc=kc, tb=tb: e.matmul(
                        psb[1 + tb][:], lhsT=hT[:, kc, tb * 128:(tb + 1) * 128], rhs=w3[:, kc, :],
                        start=(kc == 0), stop=(kc == KC - 1)),
                        reads=[("hT", kc), wv.res], writes=[("ps", 1 + tb)])
            wv.done()
            for c in range(4):
                proj_fm(wu, c, ubank[c])
            wu.done()
            vsg = []
            for tb in range(4):
                t1, t1r = T()
                vsg.append((t1, t1r))
                p.op("act", lambda e, tb=tb, t1=t1: e.activation(out=t1[:], in_=psb[1 + tb][:], func=AF.Gelu_apprx_tanh),
                     reads=[("ps", 1 + tb)], writes=[t1r])
            for c in range(4):
                p.op("act", lambda e, c=c: e.activation(out=u_g[:, c, :], in_=psb[ubank[c]][:], func=AF.Gelu_apprx_tanh),
                     reads=[("ps", ubank[c])], writes=[("u_g", c)])
            for c in range(4):
                proj_fm(wq, c, 1 + c)
            wq.done()
            for tb in range(4):
                t1, t1r = vsg[tb]
                junk, junkr = TB()
                p.op("act", lambda e, t1=t1, junk=junk, tb=tb: e.activation(out=junk[:], in_=t1[:], func=AF.Square,
                                                                             accum_out=small[:, tb:tb + 1]),
                     reads=[t1r], writes=[junkr, ("small", tb)])
            for tb in range(4):
                p.op("act", lambda e, tb=tb: e.activation(out=small[:, 4 + tb:5 + tb], in_=small[:, tb:tb + 1], func=AF.Ln,
                                                          bias=EPS, scale=1.0 / 512),
                     reads=[("small", tb)], writes=[("small", 4 + tb)])
            for tb in range(4):
                p.op("act", lambda e, tb=tb: e.activation(out=small[:, 8 + tb:9 + tb], in_=small[:, 4 + tb:5 + tb], func=AF.Exp,
                                                          scale=-0.5),
                     reads=[("small", 4 + tb)], writes=[("small", 8 + tb)])
            for tb in range(4):
                t1, t1r = vsg[tb]
                p.op("dve", lambda e, tb=tb, t1=t1: e.scalar_tensor_tensor(
                    out=vsn[:, tb, :], in0=t1[:], scalar=small[:, 8 + tb:9 + tb], in1=sgg_s[:],
                    op0=ALU.mult, op1=ALU.mult), reads=[t1r, ("small", 8 + tb), "sgg_s"], writes=[("vsn", tb)])
            qsq = []
            for c in range(4):
                p.op("act", lambda e, c=c: e.activation(out=attnT[:, c, :], in_=psb[1 + c][:], func=AF.Square),
                     reads=[("ps", 1 + c)], writes=[("attnT", c, 0), ("attnT", c, 1)])
            rs = []
            for c in range(4):
                kb_ = 5 + (c % 2)
                p.op("pe", lambda e, c=c, kb_=kb_: e.matmul(psb[kb_][:], lhsT=blk_b[:], rhs=attnT[:, c, :], start=True, stop=True),
                     reads=[("attnT", c, 0), ("attnT", c, 1)] + BLK, writes=[("ps", kb_)])
                t1, t1r = T()
                rs.append((t1, t1r))
                p.op("act", lambda e, t1=t1, kb_=kb_: e.activation(out=t1[:], in_=psb[kb_][:], func=AF.Ln, bias=EPS, scale=1.0 / HD),
                     reads=[("ps", kb_)], writes=[t1r])
            for c in range(4):
                t1, t1r = rs[c]
                p.op("act", lambda e, t1=t1: e.activation(out=t1[:], in_=t1[:], func=AF.Exp, scale=-0.5), reads=[t1r], writes=[t1r])
            for c in range(4):
                t1, t1r = rs[c]
                p.op("dve", lambda e, c=c, t1=t1: e.scalar_tensor_tensor(out=attnT[:, c, :], in0=psb[1 + c][:], scalar=pcol(ppl + 16),
                                                                         in1=t1[:], op0=ALU.mult, op1=ALU.mult),
                     reads=[("ps", 1 + c), t1r, "pp"], writes=[("attnT", c, 0), ("attnT", c, 1)])
            for c in range(4):
                rb = 7 if c % 2 == 0 else 0
                p.op("pe", lambda e, c=c, rb=rb: e.matmul(psb[rb][:], lhsT=rot_b[:], rhs=attnT[:, c, :], start=True, stop=True),
                     reads=[("attnT", c, 0), ("attnT", c, 1), "rot_b"], writes=[("ps", rb)])
                t2, t2r = T()
                p.op("dve", lambda e, c=c, t2=t2: e.tensor_tensor(out=t2[:], in0=attnT[:, c, :], in1=cosb[bi][:], op=ALU.mult),
                     reads=[("attnT", c, 0), ("attnT", c, 1), ("cos", bi)], writes=[t2r])
                t3, t3r = T()
                p.op("dve", lambda e, rb=rb, t3=t3: e.tensor_tensor(out=t3[:], in0=psb[rb][:], in1=sinb[bi][:], op=ALU.mult),
                     reads=[("ps", rb), ("sin", bi)], writes=[t3r])
                p.op("dve", lambda e, c=c, t2=t2, t3=t3: e.tensor_tensor(out=qT[:, c, :], in0=t2[:], in1=t3[:], op=ALU.add),
                     reads=[t2r, t3r], writes=[("qT", c)])
            for tb in range(4):
                for c in range(4):
                    bank = 1 + c
                    for gg in range(2):
                        g = 2 * c + gg
                        p.op("pe", lambda e, tb=tb, g=g, gg=gg, bank=bank: e.matmul(
                            psb[bank][gg * 64:(gg + 1) * 64, tb * 128:(tb + 1) * 128],
                            lhsT=vsn[:, tb, g * 64:(g + 1) * 64], rhs=sgw3[:, g, :], start=True, stop=True,
                            tile_position=(0, gg * 64)),
                            reads=[("vsn", tb), "sgw_s"], writes=[("ps", bank)])
            for c in range(4):
                bank = 1 + c
                t1, t1r = T()
                p.op("dve", lambda e, c=c, bank=bank, t1=t1: e.tensor_tensor(
                    out=t1[:].rearrange("p (b n) -> p b n", b=4), in0=psb[bank][:].rearrange("p (b n) -> p b n", b=4),
                    in1=sgb3[:, c:c + 1, :].broadcast_to([128, 4, 128]), op=ALU.add),
                    reads=[("ps", bank), "sgb_s"], writes=[t1r])
                p.op("dve", lambda e, c=c, t1=t1: e.tensor_tensor(out=uspT[:, c, :], in0=t1[:], in1=u_g[:, c, :], op=ALU.mult),
                     reads=[t1r, ("u_g", c)], writes=[("uspT", c)])
            for c in range(4):
                ob = 4 if c % 2 == 0 else 6

                def S(kb, c=c):
                    sb0 = 2 * (kb % 2)
                    for hh in range(2):
                        r0 = hh * 64
                        p.op("pe", lambda e: e.matmul(psb[sb0 + hh][:], lhsT=KT[r0:r0 + 64, kb * 128:(kb + 1) * 128],
                                                      rhs=qT[r0:r0 + 64, c, :], start=True, stop=True),
                             reads=[("KT", kb // 4), ("qT", c)], writes=[("ps", sb0 + hh)])

                def E(kb):
                    sb0 = 2 * (kb % 2)
                    pi = state["pt"] % NPT
                    state["pt"] += 1
                    uk = (kb * 128) // UNIT
                    bias = pcol(L * PP_L + 9 + uq * 4 + uk) if nu > 1 else 0.0
                    p.op("act", lambda e: e.activation(out=pT[pi][:], in_=psall[:, sb0 * 512:(sb0 + 2) * 512], func=AF.Exp,
                                                       bias=bias, scale=0.125),
                         reads=[("ps", sb0), ("ps", sb0 + 1), "pp"], writes=[("pT", pi)])
                    return pi

                def PV(kb, pi, ob=ob):
                    for hh in range(2):
                        vcol = 64 * hh
                        p.op("pe", lambda e: e.matmul(psb[ob + hh][:], lhsT=VA[:, kb, vcol:vcol + 128],
                                                      rhs=pT[pi][:, hh * 512:(hh + 1) * 512],
                                                      start=(kb == 0), stop=(kb == nkb - 1)),
                             reads=[("pT", pi), ("VA0", kb // 4), ("VA2", kb // 4), "VA1"], writes=[("ps", ob + hh)])

                S(0)
                if nkb > 1:
                    S(1)
                for kb in range(nkb):
                    pi = E(kb)
                    if kb + 2 < nkb:
                        S(kb + 2)
                    PV(kb, pi)
                for hh in range(2):
                    r0 = hh * 64
                    d0 = 64 - r0
                    obank = ob + hh
                    t1, t1r = T()
                    if c == 3:
                        p.op("act", lambda e: e.activation(out=t1[d0:d0 + 64, :], in_=psb[obank][d0:d0 + 64, :], func=AF.Ln),
                             reads=[("ps", obank)], writes=[t1r])
                        p.op("act", lambda e: e.activation(out=t1[d0:d0 + 64, :], in_=t1[d0:d0 + 64, :], func=AF.Exp, scale=-1.0),
                             reads=[t1r], writes=[t1r])
                    else:
                        p.op("dve", lambda e: e.reciprocal(out=t1[d0:d0 + 64, :], in_=psb[obank][d0:d0 + 64, :]),
                             reads=[("ps", obank)], writes=[t1r])
                    p.op("dve", lambda e: e.tensor_tensor(
                        out=attnT[r0:r0 + 64, c, :], in0=psb[obank][r0:r0 + 64, :], in1=t1[d0:d0 + 64, :], op=ALU.mult),
                        reads=[("ps", obank), t1r], writes=[("attnT", c, hh)])
            wga0 = WG()
            wgb0 = WG()
            wa = WG()
            wb = WG()
            wa3 = wa.t[:].rearrange("p (k n) -> p k n", k=4)
            wb3 = wb.t[:].rearrange("p (k n) -> p k n", k=4)
            wga, wgb = wga0, wgb0
            for m in range(8):
                if m == 4:
                    wga0.done()
                    wgb0.done()
                    wga = WG()
                    wgb = WG()
                banks = [1, 2, 3, 4] if m % 2 == 0 else [5, 6, 7, 0]
                proj_fm(wga, m % 4, banks[0])
                proj_fm(wgb, m % 4, banks[1])
                for kc in range(4):
                    p.op("pe", lambda e, kc=kc, m=m, b=banks[2]: e.matmul(
                        psb[b][:], lhsT=wa3[:, kc, m * 128:(m + 1) * 128], rhs=attnT[:, kc, :],
                        start=(kc == 0), stop=(kc == 3)),
                        reads=[("attnT", kc, 0), ("attnT", kc, 1), wa.res], writes=[("ps", banks[2])])
                for kc in range(4):
                    p.op("pe", lambda e, kc=kc, m=m, b=banks[3]: e.matmul(
                        psb[b][:], lhsT=wb3[:, kc, m * 128:(m + 1) * 128], rhs=uspT[:, kc, :],
                        start=(kc == 0), stop=(kc == 3)),
                        reads=[("uspT", kc), wb.res], writes=[("ps", banks[3])])
                sa, sar = T()
                p.op("act", lambda e, sa=sa, b=banks[0]: e.activation(out=sa[:], in_=psb[b][:], func=AF.Sigmoid),
                     reads=[("ps", banks[0])], writes=[sar])
                sg_, sgr = T()
                p.op("act", lambda e, sg_=sg_, b=banks[1]: e.activation(out=sg_[:], in_=psb[b][:], func=AF.Sigmoid),
                     reads=[("ps", banks[1])], writes=[sgr])
                p.op("dve", lambda e, sa=sa, b=banks[2]: e.tensor_tensor(out=sa[:], in0=sa[:], in1=psb[b][:], op=ALU.mult),
                     reads=[sar, ("ps", banks[2])], writes=[sar])
                p.op("dve", lambda e, sg_=sg_, b=banks[3]: e.tensor_tensor(out=sg_[:], in0=sg_[:], in1=psb[b][:], op=ALU.mult),
                     reads=[sgr, ("ps", banks[3])], writes=[sgr])
                p.op("dve", lambda e, sa=sa, sg_=sg_, m=m: e.tensor_tensor(out=sqT[:, m, 0:TQ], in0=sa[:], in1=sg_[:], op=ALU.add),
                     reads=[sar, sgr], writes=[("sqT", m)])
            wga.done()
            wgb.done()
            wa.done()
            wb.done()
            for half in range(2):
                if half == 1 and i + 1 < len(sched) and sched[i + 1]["kind"] in ("mx", "ffn") and loaded[i + 1]:
                    prologue(i + 1, alt=True, defer_ht=True)
                wg = WG()
                w3 = wg.t[:].rearrange("p (k n) -> p k n", k=KC)
                for mm in range(4):
                    m = half * 4 + mm
                    bank = 1 + (m % 2)
                    for kc in range(KC):
                        p.op("pe", lambda e, kc=kc, mm=mm, bank=bank: e.matmul(
                            psb[bank][:], lhsT=w3[:, kc, mm * 128:(mm + 1) * 128], rhs=sqT[:, kc, 0:TQ],
                            start=(kc == 0), stop=(kc == KC - 1)),
                            reads=[("sqT", kc), wg.res], writes=[("ps", bank)])
                    p.op("dve", lambda e, m=m, bank=bank: e.tensor_tensor(
                        out=xt[bi][:, m, 0:TQ], in0=xt[bi][:, m, 0:TQ], in1=psb[bank][:], op=ALU.add),
                        reads=[("xt", bi, m), ("ps", bank)], writes=[("xt", bi, m)])
                wg.done()
            p.op("pool", lambda e: e.dma_start(out=xTv[src][:, :, c0:c0 + TQ], in_=xt[bi][:, :, 0:TQ]),
                 reads=XT(bi), writes=[("xT", src, c0 // TQ)], dma=("xst", bi))
            prologue_ht(i + 1)
            if i == last_mx[l]:
                layer_consts_mx(l + 1)

        def ffn_tile(i):
            t = sched[i]
            bi = i % 2
            l, W, c0 = t["l"], t["W"], t["c0"]
            dst = 1 - t["src"]
            ppl = l * PP_L
            NW = W + 2
            for gi in range(8):
                wg = WG()
                w3 = wg.t[:].rearrange("p (k n) -> p k n", k=KC)
                for jj in range(2):
                    jf = 2 * gi + jj
                    gb_, vb_ = (1, 2) if jf % 2 == 0 else (3, 4)
                    for (bank, col) in ((gb_, jj), (vb_, 2 + jj)):
                        for kc in range(KC):
                            p.op("pe", lambda e, kc=kc, bank=bank, col=col: e.matmul(
                                psb[bank][:, 0:NW], lhsT=w3[:, kc, col * 128:(col + 1) * 128], rhs=hT[:, kc, 0:NW],
                                start=(kc == 0), stop=(kc == KC - 1)),
                                reads=[("hT", kc), wg.res], writes=[("ps", bank)])
                    res = []
                    for (bank, ch) in ((gb_, jf), (vb_, 16 + jf)):
                        cb = ppl + 18 + ch * 4
                        t1, t1r = T()
                        p.op("act", lambda e, t1=t1, bank=bank, cb=cb: e.activation(
                            out=t1[:, 0:W], in_=psb[bank][:, 1:W + 1], func=AF.Identity, bias=pcol(cb + 3), scale=pcol(cb + 1)),
                            reads=[("ps", bank), "pp"], writes=[t1r])
                        p.op("dve", lambda e, t1=t1, bank=bank, cb=cb: e.scalar_tensor_tensor(
                            out=t1[:, 0:W], in0=psb[bank][:, 0:W], scalar=pcol(cb + 0), in1=t1[:, 0:W],
                            op0=ALU.mult, op1=ALU.add), reads=[("ps", bank), t1r, "pp"], writes=[t1r])
                        p.op("dve", lambda e, t1=t1, bank=bank, cb=cb: e.scalar_tensor_tensor(
                            out=t1[:, 0:W], in0=psb[bank][:, 2:W + 2], scalar=pcol(cb + 2), in1=t1[:, 0:W],
                            op0=ALU.mult, op1=ALU.add), reads=[("ps", bank), t1r, "pp"], writes=[t1r])
                        res.append((t1, t1r))
                    (tg, tgr), (tv, tvr) = res
                    p.op("act", lambda e, tg=tg: e.activation(out=tg[:, 0:W], in_=tg[:, 0:W], func=AF.Gelu_apprx_tanh),
                         reads=[tgr], writes=[tgr])
                    p.op("dve", lambda e, tg=tg, tv=tv, jf=jf: e.tensor_tensor(
                        out=actT[:, jf, 0:W], in0=tg[:, 0:W], in1=tv[:, 0:W], op=ALU.mult),
                        reads=[tgr, tvr], writes=[("actT", jf)])
                wg.done()
            for gi in range(4):
                if gi == 1 and i + 1 < len(sched) and sched[i + 1]["kind"] == "ffn" and loaded[i + 1]:
                    prologue(i + 1, defer_ht=True)
                wg = WG()
                w3 = wg.t[:].rearrange("p (k n) -> p k n", k=16)
                for mm in range(2):
                    m = gi * 2 + mm
                    bank = 5 + (m % 2)
                    for kc in range(16):
                        p.op("pe", lambda e, kc=kc, mm=mm, bank=bank: e.matmul(
                            psb[bank][:, 0:W], lhsT=w3[:, kc, mm * 128:(mm + 1) * 128], rhs=actT[:, kc, 0:W],
                            start=(kc == 0), stop=(kc == 15)),
                            reads=[("actT", kc), wg.res], writes=[("ps", bank)])
                    p.op("dve", lambda e, m=m, bank=bank: e.tensor_tensor(
                        out=xt[bi][:, m, 1:W + 1], in0=xt[bi][:, m, 1:W + 1], in1=psb[bank][:, 0:W], op=ALU.add),
                        reads=[("xt", bi, m), ("ps", bank)], writes=[("xt", bi, m)])
                wg.done()
            tl = list(range(c0 // TQ, (c0 + W - 1) // TQ + 1))
            p.op("pool", lambda e: e.dma_start(out=xTv[dst][:, :, c0:c0 + W], in_=xt[bi][:, :, 1:W + 1]),
                 reads=XT(bi), writes=[("xT", dst, tt) for tt in tl], dma=("xst", bi))
            prologue_ht(i + 1)

        yout = y_d.rearrange("(n p) d -> p n d", p=128)

        def fin_tile(i):
            t = sched[i]
            bi = i % 2
            t0 = t["lo"]
            for th in range(2):
                for tbl in range(2):
                    tb = th * 2 + tbl
                    for half in range(2):
                        bank = 1 + ((tbl * 2 + half) % 4)
                        for k4 in range(4):
                            kc = half * 4 + k4
                            p.op("pe", lambda e, kc=kc, k4=k4, tb=tb, bank=bank: e.transpose(
                                out=psb[bank][:, k4 * 128:(k4 + 1) * 128], in_=xt[bi][:, kc, tb * 128:(tb + 1) * 128], identity=ident[:]),
                                reads=[("xt", bi, kc), "ident"], writes=[("ps", bank)])
                        if half == 0:
                            p.op("dve", lambda e, tbl=tbl, bank=bank: e.tensor_copy(out=ytok[:, tbl, 0:512], in_=psb[bank][:]),
                                 reads=[("ps", bank)], writes=[("ytok", tbl, 0)])
                        else:
                            p.op("act", lambda e, tbl=tbl, bank=bank: e.copy(out=ytok[:, tbl, 512:1024], in_=psb[bank][:]),
                                 reads=[("ps", bank)], writes=[("ytok", tbl, 1)])
                r0 = t0 // 128 + th * 2
                p.op("pool", lambda e, r0=r0: e.dma_start(out=yout[:, r0:r0 + 2, :], in_=ytok),
                     reads=[("ytok", tbl, h) for tbl in range(2) for h in range(2)], dma="yst")

        layer_consts_kv(0)
        layer_consts_mx(0)
        def wr_range(t):
            if t["kind"] == "mx":
                return (t["src"], t["lo"], t["lo"] + TQ)
            if t["kind"] == "ffn":
                return (1 - t["src"], t["c0"], t["c0"] + t["W"])
            return None

        def can_prefetch(i):
            if i + 1 >= len(sched):
                return False
            w = wr_range(sched[i])
            n = sched[i + 1]
            return not (w is not None and w[0] == n["src"] and w[1] < n["lo"] + n["ncol"] and n["lo"] < w[2])

        for i, t in enumerate(sched):
            drain_casts(2)
            emit_load(i)
            if can_prefetch(i):
                emit_load(i + 1)
            prologue(i)
            prologue_ht(i)
            {"kv": kv_tile, "mx": mx_tile, "ffn": ffn_tile, "fin": fin_tile}[t["kind"]](i)

        assert state["wseq"] == len(wlist), (state["wseq"], len(wlist))
        p.emit()
    return nc


def rope_tables(positions):
    half = HD // 2
    freq = (np.float32(10000.0) ** (-np.arange(0, half, 2, dtype=np.float32) / np.float32(half))).astype(np.float32)
    pos = np.asarray(positions)
    row = (pos // 64).astype(np.float32)
    col = (pos % 64).astype(np.float32)
    cos = np.zeros((128, len(pos)), np.float32)
    sin = np.zeros((128, len(pos)), np.float32)
    for pidx in range(128):
        d = pidx % 64
        axis = row if d < 32 else col
        j = d % 32
        i = j % 16
        ang = (axis * freq[i]).astype(np.float32)
        cos[pidx] = np.cos(ang)
        s = np.sin(ang)
        sin[pidx] = -s if j < 16 else s
    return cos, sin


def rot_matrix():
    r = np.zeros((128, 128), np.float32)
    for pidx in range(128):
        partner = pidx + 16 if (pidx % 32) < 16 else pidx - 16
        r[partner, pidx] = 1.0
    return r


def img(w):
    K, N = w.shape
    return np.ascontiguousarray(w.reshape(K // 128, 128, N).transpose(1, 0, 2)).reshape(128, -1)


def prep_weights(inp, cfg):
    L = cfg.L
    wmain = np.empty((L * NG_MAIN, 128, 4096), np.float32)
    wkv = np.empty((L, 128, 2048), np.float32)
    sgw = np.empty((L, 128, 1024), np.float32)
    sgg = np.empty((L, 128, 512), np.float32)
    sgb = np.empty((L, 128, 512), np.float32)
    pp = np.zeros((128, cfg.npp), np.float32)
    qperm = np.concatenate([np.concatenate([np.arange(c * 64, c * 64 + 64), np.arange((4 + c) * 64, (4 + c) * 64 + 64)])
                            for c in range(4)])
    for l in range(L):
        w_in = np.asarray(inp["w_in"][l])
        b = l * NG_MAIN
        wmain[b + G_Q] = img(w_in[:, 0:512][:, qperm])
        wkv[l] = img(w_in[:, 512:768])
        wmain[b + G_U] = img(w_in[:, 768:1280])
        wmain[b + G_VS] = img(w_in[:, 1280:1792])
        wmain[b + G_GA0] = img(w_in[:, 1792:2304])
        wmain[b + G_GA1] = img(w_in[:, 2304:2816])
        wmain[b + G_GB0] = img(w_in[:, 2816:3328])
        wmain[b + G_GB1] = img(w_in[:, 3328:3840])
        wmain[b + G_WA] = img(np.asarray(inp["w_branch_a"][l])[qperm, :])
        wmain[b + G_WB] = img(np.asarray(inp["w_branch_b"][l]))
        wm = np.asarray(inp["w_mix_out"][l])
        wmain[b + G_MIX] = img(wm[:, 0:512])
        wmain[b + G_MIX + 1] = img(wm[:, 512:1024])
        wu = np.asarray(inp["w_up"][l])
        for i in range(8):
            cols = np.concatenate([np.arange(2 * i * 128, (2 * i + 2) * 128), 2048 + np.arange(2 * i * 128, (2 * i + 2) * 128)])
            wmain[b + G_UP + i] = img(wu[:, cols])
        wd = np.asarray(inp["w_down"][l])
        for i in range(4):
            wmain[b + G_DN + i] = img(wd[:, i * 256:(i + 1) * 256])
        sw = np.asarray(inp["sg_w"][l])
        sgw[l] = np.ascontiguousarray(sw.transpose(2, 0, 1)).reshape(128, 1024)
        sgg[l] = np.broadcast_to(np.asarray(inp["sg_norm_g"][l])[None, :], (128, 512))
        sbias = np.asarray(inp["sg_b"][l])
        t = np.empty((128, 4, 128), np.float32)
        for c in range(4):
            t[0:64, c, :] = sbias[2 * c][None, :]
            t[64:128, c, :] = sbias[2 * c + 1][None, :]
        sgb[l] = t.reshape(128, 512)
        o = l * PP_L
        pp[:, o:o + 8] = np.asarray(inp["attn_norm_g"][l]).reshape(8, 128).T
        pp[:, o + 8:o + 16] = np.asarray(inp["ffn_norm_g"][l]).reshape(8, 128).T
        pp[:, o + 16] = np.tile(np.asarray(inp["q_norm_g"][l]), 2)
        pp[:, o + 17] = np.tile(np.asarray(inp["k_norm_g"][l]), 2)
        cw = np.asarray(inp["conv_w"][l])
        cb = np.asarray(inp["conv_b"][l])
        cv = np.stack([cw[0], cw[1], cw[2], cb], axis=-1).reshape(32, 128, 4).transpose(1, 0, 2)
        pp[:, o + 18:o + 18 + 128] = cv.reshape(128, 128)
    o = L * PP_L
    pp[:, o:o + 8] = np.asarray(inp["final_norm_g"]).reshape(8, 128).T
    return dict(wmain=wmain, wkv=wkv, sgw=sgw, sgg=sgg, sgb=sgb), pp


def core_inputs(shared, pp_base, cfg, xa_list, xb_list, a_is_one_seq):
    L = cfg.L
    x = np.concatenate([np.asarray(a) for a in xa_list] + [np.asarray(b) for b in xb_list], axis=0).astype(np.float32)
    pos = []
    if a_is_one_seq:
        pos.append(np.arange(cfg.a_units * cfg.unit))
    else:
        for _ in range(cfg.a_units):
            pos.append(np.arange(cfg.unit))
    for _ in range(cfg.n_b):
        pos.append(np.arange(cfg.unit))
    cos, sin = rope_tables(np.concatenate(pos))
    pp = pp_base.copy()
    o = L * PP_L
    pp[:, o + 8] = 1.0 if a_is_one_seq else 0.0
    for uq in range(4):
        for uk in range(4):
            pp[:, o + 9 + uq * 4 + uk] = 0.0 if (a_is_one_seq or uq == uk) else -30000.0
    m = dict(shared)
    m.update(x=np.ascontiguousarray(x), pp=pp, ropec=cos, ropes=sin,
             ident=np.eye(128, dtype=np.float32), rotm=rot_matrix())
    return m


_NC_CACHE = {}


def kernel(**inputs):
    cfg = Cfg()
    shared, pp_base = prep_weights(inputs, cfg)
    xp = np.asarray(inputs["x_prompt"])
    xs = np.asarray(inputs["x_sample"])
    in_maps = []
    for c in range(8):
        if c < 4:
            in_maps.append(core_inputs(shared, pp_base, cfg, [xp[c]], [xs[2 * c], xs[2 * c + 1]], True))
        else:
            b = 8 + 6 * (c - 4)
            in_maps.append(core_inputs(shared, pp_base, cfg, [xs[b + i] for i in range(4)], [xs[b + 4], xs[b + 5]], False))
    nc = build(cfg)
    res = run_bass_kernel_spmd(nc, in_maps, core_ids=list(range(8)))
    yp = np.empty(xp.shape, np.float32)
    ys = np.empty(xs.shape, np.float32)
    U = cfg.unit
    for c in range(8):
        y = np.asarray(res.results[c]["y"])
        if c < 4:
            yp[c] = y[0:4 * U]
            ys[2 * c] = y[4 * U:5 * U]
            ys[2 * c + 1] = y[5 * U:6 * U]
        else:
            b = 8 + 6 * (c - 4)
            for i in range(6):
                ys[b + i] = y[i * U:(i + 1) * U]
    return (yp, ys)
```

```python
import contextlib
import math
import types
import numpy as np
import concourse.bass as bass
import concourse.mybir as mybir
from concourse.bass_utils import run_bass_kernel_spmd

F32 = mybir.dt.float32
BF16 = mybir.dt.bfloat16
AF = mybir.ActivationFunctionType
ALU = mybir.AluOpType

ENGS = ("pe", "act", "dve", "pool", "sp")
SEM_EPOCH = 30000

D = 1024
KC = 8
HD = 64
EPS = 1e-6
NG_MAIN = 23
G_VS, G_U, G_Q, G_GA0, G_GB0, G_WA, G_WB, G_GA1, G_GB1, G_MIX, G_UP, G_DN = 0, 1, 2, 3, 4, 5, 6, 7, 8, 9, 11, 19
PP_L = 8 + 8 + 1 + 1 + 128


class Prog:
    def __init__(self, nc):
        self.nc = nc
        self.ops = []
        self.last_w = {}
        self.readers = {}
        self.eng_count = {e: 0 for e in ENGS}

    @staticmethod
    def _freeze(fn):
        if fn.__closure__ is None:
            return fn
        cells = []
        for c in fn.__closure__:
            try:
                cells.append(types.CellType(c.cell_contents))
            except ValueError:
                cells.append(c)
        return types.FunctionType(fn.__code__, fn.__globals__, fn.__name__, fn.__defaults__, tuple(cells))

    def _key(self, oid):
        o = self.ops[oid]
        return ("dma", o["dma"]) if o["dma"] is not None else ("eng", o["eng"])

    def op(self, eng, fn, reads=(), writes=(), dma=None):
        oid = len(self.ops)
        deps = {}

        def add(d):
            k = self._key(d)
            if deps.get(k, -1) < d:
                deps[k] = d
        for r in reads:
            if r in self.last_w:
                add(self.last_w[r])
        for w in writes:
            if w in self.last_w:
                add(self.last_w[w])
            for rd in self.readers.get(w, {}).values():
                add(rd)
        for w in writes:
            self.last_w[w] = oid
            self.readers[w] = {}
        self.ops.append(dict(eng=eng, fn=self._freeze(fn), deps=set(deps.values()), dma=dma, lidx=self.eng_count[eng]))
        k = self._key(oid)
        for r in reads:
            self.readers.setdefault(r, {})[k] = oid
        self.eng_count[eng] += 1
        return oid

    @staticmethod
    def _skippable(do, o):
        if do["dma"] is not None or o["dma"] is not None or do["eng"] != o["eng"]:
            return False
        if o["eng"] == "pe":
            return True
        return o["lidx"] - do["lidx"] >= 3

    def emit(self):
        nc = self.nc
        ops = self.ops
        need_inc = [False] * len(ops)
        for o in ops:
            for d in o["deps"]:
                if not self._skippable(ops[d], o):
                    need_inc[d] = True
        dma_keys = []
        for i, o in enumerate(ops):
            if o["dma"] is not None:
                need_inc[i] = True
                if o["dma"] not in dma_keys:
                    dma_keys.append(o["dma"])
        n_inc_eng = {e: 0 for e in ENGS}
        for i, o in enumerate(ops):
            if o["dma"] is None and need_inc[i]:
                n_inc_eng[o["eng"]] += 1
        sem_names = []
        for e in ENGS:
            for k in range((n_inc_eng[e] + SEM_EPOCH - 1) // SEM_EPOCH):
                sem_names.append(("eng", e, k))
        for k in dma_keys:
            sem_names.append(("dma", k))
        self.n_sems = len(sem_names)
        stack = contextlib.ExitStack()
        sems = {}
        for i, sn in enumerate(sem_names):
            sems[sn] = stack.enter_context(nc.semaphore("s%d" % i))
        cnt_eng = {e: 0 for e in ENGS}
        cnt_dma = {k: 0 for k in dma_keys}
        for i, o in enumerate(ops):
            if not need_inc[i]:
                o["sem"] = None
            elif o["dma"] is not None:
                cnt_dma[o["dma"]] += 16
                o["sem"] = (("dma", o["dma"]), cnt_dma[o["dma"]])
            else:
                n = cnt_eng[o["eng"]]
                cnt_eng[o["eng"]] += 1
                o["sem"] = (("eng", o["eng"], n // SEM_EPOCH), n % SEM_EPOCH + 1)
        per_eng = {e: [] for e in ENGS}
        for i, o in enumerate(ops):
            per_eng[o["eng"]].append(i)
        final_dma = dict(cnt_dma)

        def run_engine(e, eh):
            waited = {}
            maxep = {}
            for i in per_eng[e]:
                o = ops[i]
                w = {}
                for d in o["deps"]:
                    do = ops[d]
                    if do["sem"] is None or self._skippable(do, o):
                        continue
                    sn, v = do["sem"]
                    if waited.get(sn, 0) >= v:
                        continue
                    if sn[0] == "eng" and maxep.get(sn[1], -1) > sn[2]:
                        continue
                    if w.get(sn, 0) < v:
                        w[sn] = v
                for sn, v in w.items():
                    eh.wait_ge(sems[sn], v)
                    waited[sn] = v
                    if sn[0] == "eng":
                        maxep[sn[1]] = max(maxep.get(sn[1], -1), sn[2])
                ins = o["fn"](eh)
                if o["sem"] is not None:
                    sn, v = o["sem"]
                    ins.then_inc(sems[sn], 16 if o["dma"] is not None else 1)
            if e == "sp":
                for k, v in final_dma.items():
                    if v > 0:
                        eh.wait_ge(sems[("dma", k)], v)

        with nc.Block() as block:
            @block.tensor
            def _(eh):
                run_engine("pe", eh)

            @block.scalar
            def _(eh):
                run_engine("act", eh)

            @block.vector
            def _(eh):
                run_engine("dve", eh)

            @block.gpsimd
            def _(eh):
                run_engine("pool", eh)

            @block.sync
            def _(eh):
                run_engine("sp", eh)
        stack.close()


def ffn_widths(unit):
    n = -(-unit // 510)
    base = unit // n
    rem = unit - base * n
    return [base + (1 if i < rem else 0) for i in range(n)]


class Cfg:
    def __init__(self, L=4, unit=2048, a_units=4, n_b=2, nring=6):
        self.L = L
        self.unit = unit
        self.a_units = a_units
        self.n_b = n_b
        self.nring = nring
        self.seqs = [(0, a_units)]
        off = a_units * unit
        for _ in range(n_b):
            self.seqs.append((off, 1))
            off += unit
        self.ntok = off
        self.maxkeys = a_units * unit
        self.npp = L * PP_L + 8 + 1 + 16


def build(cfg):
    nc = bass.Bass("TRN2", target_bir_lowering=False)
    L, NT, UNIT = cfg.L, cfg.ntok, cfg.unit
    TQ = 512
    NKB = cfg.maxkeys // 128

    def din(name, shape, dt=F32):
        return nc.dram_tensor(name, shape, dt, kind="ExternalInput").ap()

    x_d = din("x", [NT, D])
    wmain_d = din("wmain", [L * NG_MAIN, 128, 4096])
    wkv_d = din("wkv", [L, 128, 2048])
    sgw_d = din("sgw", [L, 128, 1024])
    sgg_d = din("sgg", [L, 128, 512])
    sgb_d = din("sgb", [L, 128, 512])
    pp_d = din("pp", [128, cfg.npp])
    cos_d = din("ropec", [128, NT])
    sin_d = din("ropes", [128, NT])
    ident_d = din("ident", [128, 128])
    rotm_d = din("rotm", [128, 128])
    y_d = nc.dram_tensor("y", [NT, D], F32, kind="ExternalOutput").ap()

    def dint(name, shape, dt):
        return nc.dram_tensor(name, shape, dt, kind="Internal").ap()

    wmain_b = dint("wmain_b", [L * NG_MAIN, 128, 4096], BF16)
    wkv_b = dint("wkv_b", [L, 128, 2048], BF16)
    sgw_b = dint("sgw_b", [L, 128, 1024], BF16)
    xT = [dint("xTa", [KC, 128, NT], F32), dint("xTb", [KC, 128, NT], F32)]
    xTv = [t.rearrange("k p t -> p k t") for t in xT]

    st = contextlib.ExitStack()
    with st:
        def sb(name, shape, dt):
            return st.enter_context(nc.sbuf_tensor("sb_" + name, shape, dt))

        ring = [sb("ring%d" % i, [128, 4096], BF16) for i in range(cfg.nring)]
        wkv_s = sb("wkv_s", [128, 2048], BF16)
        sgw_s = sb("sgw_s", [128, 1024], BF16)
        sgg_s = sb("sgg_s", [128, 512], F32)
        sgb_s = sb("sgb_s", [128, 512], F32)
        pp = sb("pp", [128, cfg.npp], F32)
        ident = sb("ident", [128, 128], F32)
        rot_f = sb("rot_f", [128, 128], F32)
        rot_b = sb("rot_b", [128, 128], BF16)
        ones_b = sb("ones_b", [128, 128], BF16)
        blk_b = sb("blk_b", [128, 128], BF16)
        KT = sb("KT", [128, max(cfg.maxkeys, 8192)], BF16)
        actT = KT[:, 0:8192].rearrange("p (j n) -> p j n", j=16)
        VA = sb("VA", [128, NKB, 192], BF16)
        xt = [sb("xt%d" % i, [128, KC, 514], F32) for i in range(2)]
        hT = sb("hT", [128, KC, 514], BF16)
        sqT = sb("sqT", [128, KC, 514], BF16)
        rstd = sb("rstd", [128, 514], F32)
        cosb = [sb("cos%d" % i, [128, 512], F32) for i in range(2)]
        sinb = [sb("sin%d" % i, [128, 512], F32) for i in range(2)]
        bbuf = sb("bbuf", [128, 16, 512], BF16)
        qT = bbuf[:, 0:4, :]
        attnT = bbuf[:, 4:8, :]
        uspT = bbuf[:, 8:12, :]
        vsn = bbuf[:, 12:16, :]
        fbuf = sb("fbuf", [128, 2048], F32)
        u_g = fbuf[:].rearrange("p (c n) -> p c n", c=4)
        ytok = fbuf[:].rearrange("p (b d) -> p b d", b=2)
        NTMP = 6
        tmp = [sb("tmp%d" % i, [128, 512], F32) for i in range(NTMP)]
        tb16 = [sb("tb16_%d" % i, [128, 512], BF16) for i in range(2)]
        NPT = 4
        pT = [sb("pT%d" % i, [128, 1024], BF16) for i in range(NPT)]
        small = sb("small", [128, 16], F32)
        psall = st.enter_context(nc.psum_tensor("psall", [128, 8 * 512], F32))
        psb = [psall[:, i * 512:(i + 1) * 512] for i in range(8)]

        p = Prog(nc)
        state = dict(tmp=0, pt=0, tb=0, wseq=0, wload=0)

        def T():
            i = state["tmp"] % NTMP
            state["tmp"] += 1
            return tmp[i], "tmp%d" % i

        def TB():
            i = state["tb"] % 2
            state["tb"] += 1
            return tb16[i], "tb16_%d" % i

        wlist = []

        def ring_load(seq_idx):
            if seq_idx >= len(wlist):
                return
            slot = seq_idx % cfg.nring
            gi = wlist[seq_idx]
            p.op("sp", lambda e, slot=slot, gi=gi: e.dma_start(out=ring[slot][:], in_=wmain_b[gi]),
                 reads=[("wready", gi // NG_MAIN)], writes=[("ring", slot)], dma=("ring", slot))

        class WG:
            def __init__(self):
                self.idx = state["wseq"]
                state["wseq"] += 1
                self.slot = self.idx % cfg.nring
                self.t = ring[self.slot]
                self.res = ("ring", self.slot)

            def done(self):
                ring_load(self.idx + cfg.nring)

        def sched_groups():
            out = []
            for l in range(L):
                for (soff, nu) in cfg.seqs:
                    ntile = nu * UNIT // TQ
                    for _ in range(ntile):
                        out += [l * NG_MAIN + g for g in range(0, G_UP)]
                    for _u in range(nu):
                        for _w in ffn_widths(UNIT):
                            out += [l * NG_MAIN + g for g in range(G_UP, NG_MAIN)]
            return out

        wlist.extend(sched_groups())

        p.op("sp", lambda e: e.dma_start(out=pp[:], in_=pp_d), writes=["pp"], dma="c0")
        p.op("sp", lambda e: e.dma_start(out=ident[:], in_=ident_d), writes=["ident"], dma="c1")
        p.op("sp", lambda e: e.dma_start(out=rot_f[:], in_=rotm_d), writes=["rot_f"], dma="c2")
        p.op("dve", lambda e: e.tensor_copy(out=rot_b[:], in_=rot_f[:]), reads=["rot_f"], writes=["rot_b"])
        p.op("dve", lambda e: e.memset(ones_b[:], 1.0), writes=["ones_b"])
        p.op("dve", lambda e: e.memset(blk_b[:], 0.0), writes=["blk_b"])
        p.op("dve", lambda e: e.memset(blk_b[0:64, 0:64], 1.0), reads=["blk_b"], writes=["blk_b0"])
        p.op("dve", lambda e: e.memset(blk_b[64:128, 64:128], 1.0), reads=["blk_b"], writes=["blk_b1"])
        p.op("pool", lambda e: e.memset(VA[:, :, 64:128], 1.0), writes=["VA1"])
        BLK = ["blk_b", "blk_b0", "blk_b1"]
        cast_q = []

        def cast_ops(l):
            ops_ = []
            ops_.append(lambda: p.op("pool", lambda e: e.dma_start(out=wkv_b[l], in_=wkv_d[l]), writes=[("wcast", l, -1)], dma=("cast", l)))
            ops_.append(lambda: p.op("pool", lambda e: e.dma_start(out=sgw_b[l], in_=sgw_d[l]), writes=[("wcast", l, -2)], dma=("cast", l)))
            for g in range(NG_MAIN):
                gi = l * NG_MAIN + g
                ops_.append(lambda gi=gi, g=g: p.op("pool", lambda e: e.dma_start(out=wmain_b[gi], in_=wmain_d[gi]),
                                                    writes=[("wcast", l, g)], dma=("cast", l)))
            ops_.append(lambda: p.op("pool", lambda e: e.memset(small[:, 12 + l % 4:13 + l % 4], 0.0),
                                     reads=[("wcast", l, g) for g in range(-2, NG_MAIN)], writes=[("wready", l)]))
            return ops_

        for f in cast_ops(0):
            f()
        for l in range(1, L):
            cast_q.extend(cast_ops(l))

        def drain_casts(n):
            for _ in range(min(n, len(cast_q))):
                cast_q.pop(0)()

        xin = x_d.rearrange("(n p) d -> p n d", p=128)
        HS = 256
        for t0 in range(0, NT, HS):
            bi = (t0 // HS) % 2
            p.op("sp", lambda e, t0=t0: e.dma_start(out=ytok, in_=xin[:, t0 // 128:t0 // 128 + 2, :]),
                 writes=["ytok"], dma="xin")
            for kc in range(KC):
                bank = 1 + (kc % 4)
                for tb in range(2):
                    p.op("pe", lambda e, kc=kc, tb=tb, bank=bank: e.transpose(
                        out=psb[bank][:, tb * 128:(tb + 1) * 128], in_=ytok[:, tb, kc * 128:(kc + 1) * 128],
                        identity=ident[:]), reads=["ytok", "ident"], writes=[("ps", bank)])
                if kc % 2 == 0:
                    p.op("dve", lambda e, kc=kc, bank=bank, bi=bi: e.tensor_copy(out=xt[bi][:, kc, 0:HS], in_=psb[bank][:, 0:HS]),
                         reads=[("ps", bank)], writes=[("xt", bi, kc)])
                else:
                    p.op("act", lambda e, kc=kc, bank=bank, bi=bi: e.copy(out=xt[bi][:, kc, 0:HS], in_=psb[bank][:, 0:HS]),
                         reads=[("ps", bank)], writes=[("xt", bi, kc)])
            p.op("sp", lambda e, t0=t0, bi=bi: e.dma_start(out=xTv[0][:, :, t0:t0 + HS], in_=xt[bi][:, :, 0:HS]),
                 reads=[("xt", bi, kc) for kc in range(KC)], writes=[("xT", 0, t0 // TQ)], dma=("xst", bi))

        for i in range(cfg.nring):
            ring_load(i)
        state["wload"] = cfg.nring

        def pcol(i):
            return pp[:, i:i + 1]

        def load_x(src, c0, ncol, bi, dst0=0):
            p.op("pool", lambda e: e.dma_start(out=xt[bi][:, :, dst0:dst0 + ncol], in_=xTv[src][:, :, c0:c0 + ncol]),
                 reads=[("xT", src, t) for t in range(c0 // TQ, (c0 + ncol - 1) // TQ + 1)],
                 writes=[("xt", bi, kc) for kc in range(KC)], dma=("xld", bi))

        XT = lambda bi: [("xt", bi, kc) for kc in range(KC)]
        HT = [("hT", kc) for kc in range(KC)]

        sched = []
        for l in range(L):
            src = l % 2
            for si, (soff, nu) in enumerate(cfg.seqs):
                ntile = nu * UNIT // TQ
                for j in range(ntile):
                    sched.append(dict(kind="kv", l=l, si=si, j=j, src=src, lo=soff + j * TQ, ncol=TQ, dst0=0, rope=soff + j * TQ,
                                      ntile=ntile, nu=nu, soff=soff))
                for j in range(ntile):
                    sched.append(dict(kind="mx", l=l, si=si, j=j, src=src, lo=soff + j * TQ, ncol=TQ, dst0=0, rope=soff + j * TQ,
                                      ntile=ntile, nu=nu, soff=soff))
                widths = ffn_widths(UNIT)
                for u in range(nu):
                    woff = 0
                    for wi, W in enumerate(widths):
                        c0 = soff + u * UNIT + woff
                        woff += W
                        left_edge = wi == 0
                        right_edge = wi == len(widths) - 1
                        seq_left = left_edge and u == 0
                        seq_right = right_edge and u == nu - 1
                        lo = c0 - 1 + (1 if seq_left else 0)
                        hi = c0 + W + 1 - (1 if seq_right else 0)
                        sched.append(dict(kind="ffn", l=l, si=si, src=src, lo=lo, ncol=hi - lo, dst0=(1 if seq_left else 0), rope=None,
                                          c0=c0, W=W, left_edge=left_edge, right_edge=right_edge, seq_left=seq_left, seq_right=seq_right))
        for t0 in range(0, NT, TQ):
            sched.append(dict(kind="fin", l=L, src=L % 2, lo=t0, ncol=TQ, dst0=0, rope=None))
        last_kv = {}
        last_mx = {}
        for i, t in enumerate(sched):
            if t["kind"] == "kv":
                last_kv[t["l"]] = i
            if t["kind"] == "mx":
                last_mx[t["l"]] = i
        loaded = [False] * len(sched)
        prologued = [False] * len(sched)

        def emit_load(i):
            if i >= len(sched) or loaded[i]:
                return
            loaded[i] = True
            t = sched[i]
            bi = i % 2
            if t["kind"] == "ffn":
                NW = t["W"] + 2
                if t["seq_left"]:
                    p.op("pool", lambda e: e.memset(xt[bi][:, :, 0:1], 0.0), writes=XT(bi))
                if t["seq_right"]:
                    p.op("pool", lambda e: e.memset(xt[bi][:, :, NW - 1:NW], 0.0), writes=XT(bi))
            load_x(t["src"], t["lo"], t["ncol"], bi, dst0=t["dst0"])
            if t["rope"] is not None:
                c0 = t["rope"]
                p.op("pool", lambda e: e.dma_start(out=cosb[bi][:], in_=cos_d[:, c0:c0 + TQ]), writes=[("cos", bi)], dma=("cos", bi))
                p.op("pool", lambda e: e.dma_start(out=sinb[bi][:], in_=sin_d[:, c0:c0 + TQ]), writes=[("sin", bi)], dma=("sin", bi))

        def layer_consts_kv(l):
            if l < L:
                p.op("sp", lambda e: e.dma_start(out=wkv_s[:], in_=wkv_b[l]), reads=[("wready", l)], writes=["wkv_s"], dma="lw0")

        def layer_consts_mx(l):
            if l < L:
                p.op("sp", lambda e: e.dma_start(out=sgw_s[:], in_=sgw_b[l]), reads=[("wready", l)], writes=["sgw_s"], dma="lw1")
                p.op("sp", lambda e: e.dma_start(out=sgg_s[:], in_=sgg_d[l]), writes=["sgg_s"], dma="lw2")
                p.op("sp", lambda e: e.dma_start(out=sgb_s[:], in_=sgb_d[l]), writes=["sgb_s"], dma="lw3")

        ht_done = [False] * len(sched)

        def prologue_ht(i):
            if i >= len(sched) or ht_done[i] or not prologued[i]:
                return
            ht_done[i] = True
            t = sched[i]
            bi = i % 2
            if t["kind"] == "ffn":
                ncol = t["W"] + 2
                gbase = t["l"] * PP_L + 8
            else:
                ncol = TQ
                gbase = t["l"] * PP_L
            for kc in range(KC):
                p.op("dve", lambda e, kc=kc: e.scalar_tensor_tensor(
                    out=hT[:, kc, 0:ncol], in0=xt[bi][:, kc, 0:ncol], scalar=pcol(gbase + kc), in1=rstd[:, 0:ncol],
                    op0=ALU.mult, op1=ALU.mult), reads=[("xt", bi, kc), "rstd", "pp"], writes=[("hT", kc)])

        def prologue(i, alt=False, defer_ht=False):
            if i >= len(sched) or prologued[i]:
                return

            def sq_ap(kc, n):
                return bbuf[:, kc, 0:n] if alt else sqT[:, kc, 0:n]

            def sq_res(kc):
                if not alt:
                    return [("sqT", kc)]
                return [("qT", kc)] if kc < 4 else [("attnT", kc - 4, 0), ("attnT", kc - 4, 1)]
            prologued[i] = True
            t = sched[i]
            bi = i % 2
            l = t["l"]
            ncol = t["ncol"] + t["dst0"] + (1 if t.get("seq_right") else 0) if t["kind"] == "ffn" else TQ
            if t["kind"] == "ffn":
                ncol = t["W"] + 2
                gbase = l * PP_L + 8
                if t["left_edge"] and not t["seq_left"]:
                    p.op("dve", lambda e: e.tensor_scalar(out=xt[bi][:, :, 0:1], in0=xt[bi][:, :, 0:1],
                                                          scalar1=pcol(L * PP_L + 8), scalar2=None, op0=ALU.mult),
                         reads=XT(bi) + ["pp"], writes=XT(bi))
                if t["right_edge"] and not t["seq_right"]:
                    p.op("dve", lambda e: e.tensor_scalar(out=xt[bi][:, :, ncol - 1:ncol], in0=xt[bi][:, :, ncol - 1:ncol],
                                                          scalar1=pcol(L * PP_L + 8), scalar2=None, op0=ALU.mult),
                         reads=XT(bi) + ["pp"], writes=XT(bi))
            elif t["kind"] == "fin":
                gbase = L * PP_L
            else:
                gbase = l * PP_L
            for kc in range(KC):
                p.op("act", lambda e, kc=kc: e.activation(out=sq_ap(kc, ncol), in_=xt[bi][:, kc, 0:ncol], func=AF.Square),
                     reads=[("xt", bi, kc)], writes=sq_res(kc))
            for kc in range(KC):
                p.op("pe", lambda e, kc=kc: e.matmul(psb[0][:, 0:ncol], lhsT=ones_b[:], rhs=sq_ap(kc, ncol),
                                                     start=(kc == 0), stop=(kc == KC - 1)),
                     reads=sq_res(kc) + ["ones_b"], writes=[("ps", 0)])
            p.op("act", lambda e: e.activation(out=rstd[:, 0:ncol], in_=psb[0][:, 0:ncol], func=AF.Ln, bias=EPS, scale=1.0 / D),
                 reads=[("ps", 0)], writes=["rstd"])
            p.op("act", lambda e: e.activation(out=rstd[:, 0:ncol], in_=rstd[:, 0:ncol], func=AF.Exp, scale=-0.5),
                 reads=["rstd"], writes=["rstd"])
            if t["kind"] == "fin":
                for kc in range(KC):
                    p.op("dve", lambda e, kc=kc: e.scalar_tensor_tensor(
                        out=xt[bi][:, kc, 0:TQ], in0=xt[bi][:, kc, 0:TQ], scalar=pcol(gbase + kc), in1=rstd[:, 0:TQ],
                        op0=ALU.mult, op1=ALU.mult), reads=[("xt", bi, kc), "rstd", "pp"], writes=[("xt", bi, kc)])
            elif not defer_ht:
                prologue_ht(i)

        wkv3 = wkv_s[:].rearrange("p (k n) -> p k n", k=KC)
        sgw3 = sgw_s[:].rearrange("p (g n) -> p g n", g=8)
        sgb3 = sgb_s[:].rearrange("p (c n) -> p c n", c=4)

        def head_chain_a(src_bank, gcol):
            sq, sqr = TB()
            p.op("act", lambda e: e.activation(out=sq[:], in_=psb[src_bank][:], func=AF.Square),
                 reads=[("ps", src_bank)], writes=[sqr])
            p.op("pe", lambda e: e.matmul(psb[6][:], lhsT=blk_b[:], rhs=sq[:], start=True, stop=True),
                 reads=[sqr] + BLK, writes=[("ps", 6)])
            t1, t1r = T()
            p.op("act", lambda e: e.activation(out=t1[:], in_=psb[6][:], func=AF.Ln, bias=EPS, scale=1.0 / HD),
                 reads=[("ps", 6)], writes=[t1r])
            p.op("act", lambda e: e.activation(out=t1[:], in_=t1[:], func=AF.Exp, scale=-0.5), reads=[t1r], writes=[t1r])
            kn, knr = TB()
            p.op("dve", lambda e: e.scalar_tensor_tensor(out=kn[:], in0=psb[src_bank][:], scalar=pcol(gcol), in1=t1[:],
                                                         op0=ALU.mult, op1=ALU.mult),
                 reads=[("ps", src_bank), t1r, "pp"], writes=[knr])
            return kn, knr

        def head_chain_b(kn, knr, cs, out_ap, out_res):
            p.op("pe", lambda e: e.matmul(psb[7][:], lhsT=rot_b[:], rhs=kn[:], start=True, stop=True),
                 reads=[knr, "rot_b"], writes=[("ps", 7)])
            t2, t2r = T()
            p.op("dve", lambda e: e.tensor_tensor(out=t2[:], in0=kn[:], in1=cosb[cs][:], op=ALU.mult),
                 reads=[knr, ("cos", cs)], writes=[t2r])
            t3, t3r = T()
            p.op("dve", lambda e: e.tensor_tensor(out=t3[:], in0=psb[7][:], in1=sinb[cs][:], op=ALU.mult),
                 reads=[("ps", 7), ("sin", cs)], writes=[t3r])
            p.op("dve", lambda e: e.tensor_tensor(out=out_ap, in0=t2[:], in1=t3[:], op=ALU.add),
                 reads=[t2r, t3r], writes=[out_res])

        def proj_fm(wg, mc, bank):
            w3 = wg.t[:].rearrange("p (k n) -> p k n", k=KC)
            for kc in range(KC):
                p.op("pe", lambda e, kc=kc: e.matmul(psb[bank][:], lhsT=w3[:, kc, mc * 128:(mc + 1) * 128],
                                                     rhs=hT[:, kc, 0:TQ], start=(kc == 0), stop=(kc == KC - 1)),
                     reads=[("hT", kc), wg.res], writes=[("ps", bank)])

        def kv_tile(i):
            t = sched[i]
            bi = i % 2
            l, j = t["l"], t["j"]
            ppl = l * PP_L
            for kc in range(KC):
                p.op("pe", lambda e, kc=kc: e.matmul(psb[1][:], lhsT=wkv3[:, kc, 0:128], rhs=hT[:, kc, 0:TQ],
                                                     start=(kc == 0), stop=(kc == KC - 1)),
                     reads=[("hT", kc), "wkv_s"], writes=[("ps", 1)])
            for tb in range(4):
                for kc in range(KC):
                    p.op("pe", lambda e, kc=kc, tb=tb: e.matmul(
                        psb[2][:, tb * 128:(tb + 1) * 128], lhsT=hT[:, kc, tb * 128:(tb + 1) * 128],
                        rhs=wkv3[:, kc, 128:256], start=(kc == 0), stop=(kc == KC - 1)),
                        reads=[("hT", kc), "wkv_s"], writes=[("ps", 2)])
            kn, knr = head_chain_a(1, ppl + 17)
            if i + 1 < len(sched) and sched[i + 1]["kind"] == "kv" and loaded[i + 1]:
                prologue(i + 1)
            head_chain_b(kn, knr, bi, KT[:, j * TQ:(j + 1) * TQ], ("KT", j))
            kb0 = j * 4
            v4 = psb[2][:].rearrange("p (b g d) -> p b g d", b=4, g=2)
            p.op("dve", lambda e: e.tensor_copy(out=VA[:, kb0:kb0 + 4, 0:64], in_=v4[:, :, 0, :]),
                 reads=[("ps", 2)], writes=[("VA0", j)])
            p.op("dve", lambda e: e.tensor_copy(out=VA[:, kb0:kb0 + 4, 128:192], in_=v4[:, :, 1, :]),
                 reads=[("ps", 2)], writes=[("VA2", j)])
            if i == last_kv[l]:
                layer_consts_kv(l + 1)

        def mx_tile(i):
            t = sched[i]
            bi = i % 2
            l, j, nu, soff = t["l"], t["j"], t["nu"], t["soff"]
            src = t["src"]
            c0 = t["lo"]
            ppl = l * PP_L
            nkb = nu * UNIT // 128
            uq = j // (UNIT // TQ)
            wv = WG()
            wu = WG()
            wq = WG()
            ubank = [5, 6, 7, 0]
            w3 = wv.t[:].rearrange("p (k n) -> p k n", k=KC)
            for tb in range(4):
                for kc in range(KC):
                    p.op("pe", lambda e, kc=kc, tb=tb: e.matmul(
                        psb[1 + tb][:], lhsT=hT[:, kc, tb * 128:(tb + 1) * 128], rhs=w3[:, kc, :],
                        start=(kc == 0), stop=(kc == KC - 1)),
                        reads=[("hT", kc), wv.res], writes=[("ps", 1 + tb)])
            wv.done()
            for c in range(4):
                proj_fm(wu, c, ubank[c])
            wu.done()
            vsg = []
            for tb in range(4):
                t1, t1r = T()
                vsg.append((t1, t1r))
                p.op("act", lambda e, tb=tb, t1=t1: e.activation(out=t1[:], in_=psb[1 + tb][:], func=AF.Gelu_apprx_tanh),
                     reads=[("ps", 1 + tb)], writes=[t1r])
            for c in range(4):
                p.op("act", lambda e, c=c: e.activation(out=u_g[:, c, :], in_=psb[ubank[c]][:], func=AF.Gelu_apprx_tanh),
                     reads=[("ps", ubank[c])], writes=[("u_g", c)])
            for c in range(4):
                proj_fm(wq, c, 1 + c)
            wq.done()
            for tb in range(4):
                t1, t1r = vsg[tb]
                junk, junkr = TB()
                p.op("act", lambda e, t1=t1, junk=junk, tb=tb: e.activation(out=junk[:], in_=t1[:], func=AF.Square,
                                                                             accum_out=small[:, tb:tb + 1]),
                     reads=[t1r], writes=[junkr, ("small", tb)])
            for tb in range(4):
                p.op("act", lambda e, tb=tb: e.activation(out=small[:, 4 + tb:5 + tb], in_=small[:, tb:tb + 1], func=AF.Ln,
                                                          bias=EPS, scale=1.0 / 512),
                     reads=[("small", tb)], writes=[("small", 4 + tb)])
            for tb in range(4):
                p.op("act", lambda e, tb=tb: e.activation(out=small[:, 8 + tb:9 + tb], in_=small[:, 4 + tb:5 + tb], func=AF.Exp,
                                                          scale=-0.5),
                     reads=[("small", 4 + tb)], writes=[("small", 8 + tb)])
            for tb in range(4):
                t1, t1r = vsg[tb]
                p.op("dve", lambda e, tb=tb, t1=t1: e.scalar_tensor_tensor(
                    out=vsn[:, tb, :], in0=t1[:], scalar=small[:, 8 + tb:9 + tb], in1=sgg_s[:],
                    op0=ALU.mult, op1=ALU.mult), reads=[t1r, ("small", 8 + tb), "sgg_s"], writes=[("vsn", tb)])
            qsq = []
            for c in range(4):
                p.op("act", lambda e, c=c: e.activation(out=attnT[:, c, :], in_=psb[1 + c][:], func=AF.Square),
                     reads=[("ps", 1 + c)], writes=[("attnT", c, 0), ("attnT", c, 1)])
            rs = []
            for c in range(4):
                kb_ = 5 + (c % 2)
                p.op("pe", lambda e, c=c, kb_=kb_: e.matmul(psb[kb_][:], lhsT=blk_b[:], rhs=attnT[:, c, :], start=True, stop=True),
                     reads=[("attnT", c, 0), ("attnT", c, 1)] + BLK, writes=[("ps", kb_)])
                t1, t1r = T()
                rs.append((t1, t1r))
                p.op("act", lambda e, t1=t1, kb_=kb_: e.activation(out=t1[:], in_=psb[kb_][:], func=AF.Ln, bias=EPS, scale=1.0 / HD),
                     reads=[("ps", kb_)], writes=[t1r])
            for c in range(4):
                t1, t1r = rs[c]
                p.op("act", lambda e, t1=t1: e.activation(out=t1[:], in_=t1[:], func=AF.Exp, scale=-0.5), reads=[t1r], writes=[t1r])
            for c in range(4):
                t1, t1r = rs[c]
                p.op("dve", lambda e, c=c, t1=t1: e.scalar_tensor_tensor(out=attnT[:, c, :], in0=psb[1 + c][:], scalar=pcol(ppl + 16),
                                                                         in1=t1[:], op0=ALU.mult, op1=ALU.mult),
                     reads=[("ps", 1 + c), t1r, "pp"], writes=[("attnT", c, 0), ("attnT", c, 1)])
            for c in range(4):
                rb = 7 if c % 2 == 0 else 0
                p.op("pe", lambda e, c=c, rb=rb: e.matmul(psb[rb][:], lhsT=rot_b[:], rhs=attnT[:, c, :], start=True, stop=True),
                     reads=[("attnT", c, 0), ("attnT", c, 1), "rot_b"], writes=[("ps", rb)])
                t2, t2r = T()
                p.op("dve", lambda e, c=c, t2=t2: e.tensor_tensor(out=t2[:], in0=attnT[:, c, :], in1=cosb[bi][:], op=ALU.mult),
                     reads=[("attnT", c, 0), ("attnT", c, 1), ("cos", bi)], writes=[t2r])
                t3, t3r = T()
                p.op("dve", lambda e, rb=rb, t3=t3: e.tensor_tensor(out=t3[:], in0=psb[rb][:], in1=sinb[bi][:], op=ALU.mult),
                     reads=[("ps", rb), ("sin", bi)], writes=[t3r])
                p.op("dve", lambda e, c=c, t2=t2, t3=t3: e.tensor_tensor(out=qT[:, c, :], in0=t2[:], in1=t3[:], op=ALU.add),
                     reads=[t2r, t3r], writes=[("qT", c)])
            for tb in range(4):
                for c in range(4):
                    bank = 1 + c
                    for gg in range(2):
                        g = 2 * c + gg
                        p.op("pe", lambda e, tb=tb, g=g, gg=gg, bank=bank: e.matmul(
                            psb[bank][gg * 64:(gg + 1) * 64, tb * 128:(tb + 1) * 128],
                            lhsT=vsn[:, tb, g * 64:(g + 1) * 64], rhs=sgw3[:, g, :], start=True, stop=True,
                            tile_position=(0, gg * 64)),
                            reads=[("vsn", tb), "sgw_s"], writes=[("ps", bank)])
            for c in range(4):
                bank = 1 + c
                t1, t1r = T()
                p.op("dve", lambda e, c=c, bank=bank, t1=t1: e.tensor_tensor(
                    out=t1[:].rearrange("p (b n) -> p b n", b=4), in0=psb[bank][:].rearrange("p (b n) -> p b n", b=4),
                    in1=sgb3[:, c:c + 1, :].broadcast_to([128, 4, 128]), op=ALU.add),
                    reads=[("ps", bank), "sgb_s"], writes=[t1r])
                p.op("dve", lambda e, c=c, t1=t1: e.tensor_tensor(out=uspT[:, c, :], in0=t1[:], in1=u_g[:, c, :], op=ALU.mult),
                     reads=[t1r, ("u_g", c)], writes=[("uspT", c)])
            steps = [(c, kb) for c in range(4) for kb in range(nkb)]

            def S(n):
                c, kb = steps[n]
                sb0 = 2 * (n % 2)
                for hh in range(2):
                    r0 = hh * 64
                    p.op("pe", lambda e: e.matmul(psb[sb0 + hh][:], lhsT=KT[r0:r0 + 64, kb * 128:(kb + 1) * 128],
                                                  rhs=qT[r0:r0 + 64, c, :], start=True, stop=True),
                         reads=[("KT", kb // 4), ("qT", c)], writes=[("ps", sb0 + hh)])

            def E(n):
                c, kb = steps[n]
                sb0 = 2 * (n % 2)
                pi = state["pt"] % NPT
                state["pt"] += 1
                uk = (kb * 128) // UNIT
                bias = pcol(L * PP_L + 9 + uq * 4 + uk) if nu > 1 else 0.0
                p.op("act", lambda e: e.activation(out=pT[pi][:], in_=psall[:, sb0 * 512:(sb0 + 2) * 512], func=AF.Exp,
                                                   bias=bias, scale=0.125),
                     reads=[("ps", sb0), ("ps", sb0 + 1), "pp"], writes=[("pT", pi)])
                return pi

            def PV(n, pi):
                c, kb = steps[n]
                ob = 4 if c % 2 == 0 else 6
                for hh in range(2):
                    vcol = 64 * hh
                    p.op("pe", lambda e: e.matmul(psb[ob + hh][:], lhsT=VA[:, kb, vcol:vcol + 128],
                                                  rhs=pT[pi][:, hh * 512:(hh + 1) * 512],
                                                  start=(kb == 0), stop=(kb == nkb - 1)),
                         reads=[("pT", pi), ("VA0", kb // 4), ("VA2", kb // 4), "VA1"], writes=[("ps", ob + hh)])

            def NORM(c):
                ob = 4 if c % 2 == 0 else 6
                for hh in range(2):
                    r0 = hh * 64
                    d0 = 64 - r0
                    obank = ob + hh
                    t1, t1r = T()
                    if c == 3:
                        p.op("act", lambda e: e.activation(out=t1[d0:d0 + 64, :], in_=psb[obank][d0:d0 + 64, :], func=AF.Ln),
                             reads=[("ps", obank)], writes=[t1r])
                        p.op("act", lambda e: e.activation(out=t1[d0:d0 + 64, :], in_=t1[d0:d0 + 64, :], func=AF.Exp, scale=-1.0),
                             reads=[t1r], writes=[t1r])
                    else:
                        p.op("dve", lambda e: e.reciprocal(out=t1[d0:d0 + 64, :], in_=psb[obank][d0:d0 + 64, :]),
                             reads=[("ps", obank)], writes=[t1r])
                    p.op("dve", lambda e: e.tensor_tensor(
                        out=attnT[r0:r0 + 64, c, :], in0=psb[obank][r0:r0 + 64, :], in1=t1[d0:d0 + 64, :], op=ALU.mult),
                        reads=[("ps", obank), t1r], writes=[("attnT", c, hh)])

            S(0)
            S(1)
            for n in range(len(steps)):
                pi = E(n)
                if n + 2 < len(steps):
                    S(n + 2)
                PV(n, pi)
                if steps[n][1] == nkb - 1:
                    NORM(steps[n][0])
            wga0 = WG()
            wgb0 = WG()
            wa = WG()
            wb = WG()
            wa3 = wa.t[:].rearrange("p (k n) -> p k n", k=4)
            wb3 = wb.t[:].rearrange("p (k n) -> p k n", k=4)
            wga, wgb = wga0, wgb0
            for m in range(8):
                if m == 4:
                    wga0.done()
                    wgb0.done()
                    wga = WG()
                    wgb = WG()
                banks = [1, 2, 3, 4] if m % 2 == 0 else [5, 6, 7, 0]
                proj_fm(wga, m % 4, banks[0])
                proj_fm(wgb, m % 4, banks[1])
                for kc in range(4):
                    p.op("pe", lambda e, kc=kc, m=m, b=banks[2]: e.matmul(
                        psb[b][:], lhsT=wa3[:, kc, m * 128:(m + 1) * 128], rhs=attnT[:, kc, :],
                        start=(kc == 0), stop=(kc == 3)),
                        reads=[("attnT", kc, 0), ("attnT", kc, 1), wa.res], writes=[("ps", banks[2])])
                for kc in range(4):
                    p.op("pe", lambda e, kc=kc, m=m, b=banks[3]: e.matmul(
                        psb[b][:], lhsT=wb3[:, kc, m * 128:(m + 1) * 128], rhs=uspT[:, kc, :],
                        start=(kc == 0), stop=(kc == 3)),
                        reads=[("uspT", kc), wb.res], writes=[("ps", banks[3])])
                sa, sar = T()
                p.op("act", lambda e, sa=sa, b=banks[0]: e.activation(out=sa[:], in_=psb[b][:], func=AF.Sigmoid),
                     reads=[("ps", banks[0])], writes=[sar])
                sg_, sgr = T()
                p.op("act", lambda e, sg_=sg_, b=banks[1]: e.activation(out=sg_[:], in_=psb[b][:], func=AF.Sigmoid),
                     reads=[("ps", banks[1])], writes=[sgr])
                p.op("dve", lambda e, sa=sa, b=banks[2]: e.tensor_tensor(out=sa[:], in0=sa[:], in1=psb[b][:], op=ALU.mult),
                     reads=[sar, ("ps", banks[2])], writes=[sar])
                p.op("dve", lambda e, sg_=sg_, b=banks[3]: e.tensor_tensor(out=sg_[:], in0=sg_[:], in1=psb[b][:], op=ALU.mult),
                     reads=[sgr, ("ps", banks[3])], writes=[sgr])
                p.op("dve", lambda e, sa=sa, sg_=sg_, m=m: e.tensor_tensor(out=sqT[:, m, 0:TQ], in0=sa[:], in1=sg_[:], op=ALU.add),
                     reads=[sar, sgr], writes=[("sqT", m)])
            wga.done()
            wgb.done()
            wa.done()
            wb.done()
            for half in range(2):
                if half == 1 and i + 1 < len(sched) and sched[i + 1]["kind"] in ("mx", "ffn") and loaded[i + 1]:
                    prologue(i + 1, alt=True, defer_ht=True)
                wg = WG()
                w3 = wg.t[:].rearrange("p (k n) -> p k n", k=KC)
                for mm in range(4):
                    m = half * 4 + mm
                    bank = 1 + (m % 2)
                    for kc in range(KC):
                        p.op("pe", lambda e, kc=kc, mm=mm, bank=bank: e.matmul(
                            psb[bank][:], lhsT=w3[:, kc, mm * 128:(mm + 1) * 128], rhs=sqT[:, kc, 0:TQ],
                            start=(kc == 0), stop=(kc == KC - 1)),
                            reads=[("sqT", kc), wg.res], writes=[("ps", bank)])
                    p.op("dve", lambda e, m=m, bank=bank: e.tensor_tensor(
                        out=xt[bi][:, m, 0:TQ], in0=xt[bi][:, m, 0:TQ], in1=psb[bank][:], op=ALU.add),
                        reads=[("xt", bi, m), ("ps", bank)], writes=[("xt", bi, m)])
                wg.done()
            p.op("pool", lambda e: e.dma_start(out=xTv[src][:, :, c0:c0 + TQ], in_=xt[bi][:, :, 0:TQ]),
                 reads=XT(bi), writes=[("xT", src, c0 // TQ)], dma=("xst", bi))
            prologue_ht(i + 1)
            if i == last_mx[l]:
                layer_consts_mx(l + 1)

        def ffn_tile(i):
            t = sched[i]
            bi = i % 2
            l, W, c0 = t["l"], t["W"], t["c0"]
            dst = 1 - t["src"]
            ppl = l * PP_L
            NW = W + 2
            for gi in range(8):
                wg = WG()
                w3 = wg.t[:].rearrange("p (k n) -> p k n", k=KC)
                for jj in range(2):
                    jf = 2 * gi + jj
                    gb_, vb_ = (1, 2) if jf % 2 == 0 else (3, 4)
                    for (bank, col) in ((gb_, jj), (vb_, 2 + jj)):
                        for kc in range(KC):
                            p.op("pe", lambda e, kc=kc, bank=bank, col=col: e.matmul(
                                psb[bank][:, 0:NW], lhsT=w3[:, kc, col * 128:(col + 1) * 128], rhs=hT[:, kc, 0:NW],
                                start=(kc == 0), stop=(kc == KC - 1)),
                                reads=[("hT", kc), wg.res], writes=[("ps", bank)])
                    res = []
                    for (bank, ch) in ((gb_, jf), (vb_, 16 + jf)):
                        cb = ppl + 18 + ch * 4
                        t1, t1r = T()
                        p.op("act", lambda e, t1=t1, bank=bank, cb=cb: e.activation(
                            out=t1[:, 0:W], in_=psb[bank][:, 1:W + 1], func=AF.Identity, bias=pcol(cb + 3), scale=pcol(cb + 1)),
                            reads=[("ps", bank), "pp"], writes=[t1r])
                        p.op("dve", lambda e, t1=t1, bank=bank, cb=cb: e.scalar_tensor_tensor(
                            out=t1[:, 0:W], in0=psb[bank][:, 0:W], scalar=pcol(cb + 0), in1=t1[:, 0:W],
                            op0=ALU.mult, op1=ALU.add), reads=[("ps", bank), t1r, "pp"], writes=[t1r])
                        p.op("dve", lambda e, t1=t1, bank=bank, cb=cb: e.scalar_tensor_tensor(
                            out=t1[:, 0:W], in0=psb[bank][:, 2:W + 2], scalar=pcol(cb + 2), in1=t1[:, 0:W],
                            op0=ALU.mult, op1=ALU.add), reads=[("ps", bank), t1r, "pp"], writes=[t1r])
                        res.append((t1, t1r))
                    (tg, tgr), (tv, tvr) = res
                    p.op("act", lambda e, tg=tg: e.activation(out=tg[:, 0:W], in_=tg[:, 0:W], func=AF.Gelu_apprx_tanh),
                         reads=[tgr], writes=[tgr])
                    p.op("dve", lambda e, tg=tg, tv=tv, jf=jf: e.tensor_tensor(
                        out=actT[:, jf, 0:W], in0=tg[:, 0:W], in1=tv[:, 0:W], op=ALU.mult),
                        reads=[tgr, tvr], writes=[("actT", jf)])
                wg.done()
            for gi in range(4):
                if gi == 1 and i + 1 < len(sched) and sched[i + 1]["kind"] == "ffn" and loaded[i + 1]:
                    prologue(i + 1, defer_ht=True)
                wg = WG()
                w3 = wg.t[:].rearrange("p (k n) -> p k n", k=16)
                for mm in range(2):
                    m = gi * 2 + mm
                    bank = 5 + (m % 2)
                    for kc in range(16):
                        p.op("pe", lambda e, kc=kc, mm=mm, bank=bank: e.matmul(
                            psb[bank][:, 0:W], lhsT=w3[:, kc, mm * 128:(mm + 1) * 128], rhs=actT[:, kc, 0:W],
                            start=(kc == 0), stop=(kc == 15)),
                            reads=[("actT", kc), wg.res], writes=[("ps", bank)])
                    p.op("dve", lambda e, m=m, bank=bank: e.tensor_tensor(
                        out=xt[bi][:, m, 1:W + 1], in0=xt[bi][:, m, 1:W + 1], in1=psb[bank][:, 0:W], op=ALU.add),
                        reads=[("xt", bi, m), ("ps", bank)], writes=[("xt", bi, m)])
                wg.done()
            tl = list(range(c0 // TQ, (c0 + W - 1) // TQ + 1))
            p.op("pool", lambda e: e.dma_start(out=xTv[dst][:, :, c0:c0 + W], in_=xt[bi][:, :, 1:W + 1]),
                 reads=XT(bi), writes=[("xT", dst, tt) for tt in tl], dma=("xst", bi))
            prologue_ht(i + 1)

        yout = y_d.rearrange("(n p) d -> p n d", p=128)

        def fin_tile(i):
            t = sched[i]
            bi = i % 2
            t0 = t["lo"]
            for th in range(2):
                for tbl in range(2):
                    tb = th * 2 + tbl
                    for half in range(2):
                        bank = 1 + ((tbl * 2 + half) % 4)
                        for k4 in range(4):
                            kc = half * 4 + k4
                            p.op("pe", lambda e, kc=kc, k4=k4, tb=tb, bank=bank: e.transpose(
                                out=psb[bank][:, k4 * 128:(k4 + 1) * 128], in_=xt[bi][:, kc, tb * 128:(tb + 1) * 128], identity=ident[:]),
                                reads=[("xt", bi, kc), "ident"], writes=[("ps", bank)])
                        if half == 0:
                            p.op("dve", lambda e, tbl=tbl, bank=bank: e.tensor_copy(out=ytok[:, tbl, 0:512], in_=psb[bank][:]),
                                 reads=[("ps", bank)], writes=[("ytok", tbl, 0)])
                        else:
                            p.op("act", lambda e, tbl=tbl, bank=bank: e.copy(out=ytok[:, tbl, 512:1024], in_=psb[bank][:]),
                                 reads=[("ps", bank)], writes=[("ytok", tbl, 1)])
                r0 = t0 // 128 + th * 2
                p.op("pool", lambda e, r0=r0: e.dma_start(out=yout[:, r0:r0 + 2, :], in_=ytok),
                     reads=[("ytok", tbl, h) for tbl in range(2) for h in range(2)], dma="yst")

        layer_consts_kv(0)
        layer_consts_mx(0)
        def wr_range(t):
            if t["kind"] == "mx":
                return (t["src"], t["lo"], t["lo"] + TQ)
            if t["kind"] == "ffn":
                return (1 - t["src"], t["c0"], t["c0"] + t["W"])
            return None

        def can_prefetch(i):
            if i + 1 >= len(sched):
                return False
            w = wr_range(sched[i])
            n = sched[i + 1]
            return not (w is not None and w[0] == n["src"] and w[1] < n["lo"] + n["ncol"] and n["lo"] < w[2])

        for i, t in enumerate(sched):
            drain_casts(2)
            emit_load(i)
            if can_prefetch(i):
                emit_load(i + 1)
            prologue(i)
            prologue_ht(i)
            {"kv": kv_tile, "mx": mx_tile, "ffn": ffn_tile, "fin": fin_tile}[t["kind"]](i)

        assert state["wseq"] == len(wlist), (state["wseq"], len(wlist))
        p.emit()
    return nc


def rope_tables(positions):
    half = HD // 2
    freq = (np.float32(10000.0) ** (-np.arange(0, half, 2, dtype=np.float32) / np.float32(half))).astype(np.float32)
    pos = np.asarray(positions)
    row = (pos // 64).astype(np.float32)
    col = (pos % 64).astype(np.float32)
    cos = np.zeros((128, len(pos)), np.float32)
    sin = np.zeros((128, len(pos)), np.float32)
    for pidx in range(128):
        d = pidx % 64
        axis = row if d < 32 else col
        j = d % 32
        i = j % 16
        ang = (axis * freq[i]).astype(np.float32)
        cos[pidx] = np.cos(ang)
        s = np.sin(ang)
        sin[pidx] = -s if j < 16 else s
    return cos, sin


def rot_matrix():
    r = np.zeros((128, 128), np.float32)
    for pidx in range(128):
        partner = pidx + 16 if (pidx % 32) < 16 else pidx - 16
        r[partner, pidx] = 1.0
    return r


def img(w):
    K, N = w.shape
    return np.ascontiguousarray(w.reshape(K // 128, 128, N).transpose(1, 0, 2)).reshape(128, -1)


def prep_weights(inp, cfg):
    L = cfg.L
    wmain = np.empty((L * NG_MAIN, 128, 4096), np.float32)
    wkv = np.empty((L, 128, 2048), np.float32)
    sgw = np.empty((L, 128, 1024), np.float32)
    sgg = np.empty((L, 128, 512), np.float32)
    sgb = np.empty((L, 128, 512), np.float32)
    pp = np.zeros((128, cfg.npp), np.float32)
    qperm = np.concatenate([np.concatenate([np.arange(c * 64, c * 64 + 64), np.arange((4 + c) * 64, (4 + c) * 64 + 64)])
                            for c in range(4)])
    for l in range(L):
        w_in = np.asarray(inp["w_in"][l])
        b = l * NG_MAIN
        wmain[b + G_Q] = img(w_in[:, 0:512][:, qperm])
        wkv[l] = img(w_in[:, 512:768])
        wmain[b + G_U] = img(w_in[:, 768:1280])
        wmain[b + G_VS] = img(w_in[:, 1280:1792])
        wmain[b + G_GA0] = img(w_in[:, 1792:2304])
        wmain[b + G_GA1] = img(w_in[:, 2304:2816])
        wmain[b + G_GB0] = img(w_in[:, 2816:3328])
        wmain[b + G_GB1] = img(w_in[:, 3328:3840])
        wmain[b + G_WA] = img(np.asarray(inp["w_branch_a"][l])[qperm, :])
        wmain[b + G_WB] = img(np.asarray(inp["w_branch_b"][l]))
        wm = np.asarray(inp["w_mix_out"][l])
        wmain[b + G_MIX] = img(wm[:, 0:512])
        wmain[b + G_MIX + 1] = img(wm[:, 512:1024])
        wu = np.asarray(inp["w_up"][l])
        for i in range(8):
            cols = np.concatenate([np.arange(2 * i * 128, (2 * i + 2) * 128), 2048 + np.arange(2 * i * 128, (2 * i + 2) * 128)])
            wmain[b + G_UP + i] = img(wu[:, cols])
        wd = np.asarray(inp["w_down"][l])
        for i in range(4):
            wmain[b + G_DN + i] = img(wd[:, i * 256:(i + 1) * 256])
        sw = np.asarray(inp["sg_w"][l])
        sgw[l] = np.ascontiguousarray(sw.transpose(2, 0, 1)).reshape(128, 1024)
        sgg[l] = np.broadcast_to(np.asarray(inp["sg_norm_g"][l])[None, :], (128, 512))
        sbias = np.asarray(inp["sg_b"][l])
        t = np.empty((128, 4, 128), np.float32)
        for c in range(4):
            t[0:64, c, :] = sbias[2 * c][None, :]
            t[64:128, c, :] = sbias[2 * c + 1][None, :]
        sgb[l] = t.reshape(128, 512)
        o = l * PP_L
        pp[:, o:o + 8] = np.asarray(inp["attn_norm_g"][l]).reshape(8, 128).T
        pp[:, o + 8:o + 16] = np.asarray(inp["ffn_norm_g"][l]).reshape(8, 128).T
        pp[:, o + 16] = np.tile(np.asarray(inp["q_norm_g"][l]), 2)
        pp[:, o + 17] = np.tile(np.asarray(inp["k_norm_g"][l]), 2)
        cw = np.asarray(inp["conv_w"][l])
        cb = np.asarray(inp["conv_b"][l])
        cv = np.stack([cw[0], cw[1], cw[2], cb], axis=-1).reshape(32, 128, 4).transpose(1, 0, 2)
        pp[:, o + 18:o + 18 + 128] = cv.reshape(128, 128)
    o = L * PP_L
    pp[:, o:o + 8] = np.asarray(inp["final_norm_g"]).reshape(8, 128).T
    return dict(wmain=wmain, wkv=wkv, sgw=sgw, sgg=sgg, sgb=sgb), pp


def core_inputs(shared, pp_base, cfg, xa_list, xb_list, a_is_one_seq):
    L = cfg.L
    x = np.concatenate([np.asarray(a) for a in xa_list] + [np.asarray(b) for b in xb_list], axis=0).astype(np.float32)
    pos = []
    if a_is_one_seq:
        pos.append(np.arange(cfg.a_units * cfg.unit))
    else:
        for _ in range(cfg.a_units):
            pos.append(np.arange(cfg.unit))
    for _ in range(cfg.n_b):
        pos.append(np.arange(cfg.unit))
    cos, sin = rope_tables(np.concatenate(pos))
    pp = pp_base.copy()
    o = L * PP_L
    pp[:, o + 8] = 1.0 if a_is_one_seq else 0.0
    for uq in range(4):
        for uk in range(4):
            pp[:, o + 9 + uq * 4 + uk] = 0.0 if (a_is_one_seq or uq == uk) else -30000.0
    m = dict(shared)
    m.update(x=np.ascontiguousarray(x), pp=pp, ropec=cos, ropes=sin,
             ident=np.eye(128, dtype=np.float32), rotm=rot_matrix())
    return m


_NC_CACHE = {}


def kernel(**inputs):
    cfg = Cfg()
    shared, pp_base = prep_weights(inputs, cfg)
    xp = np.asarray(inputs["x_prompt"])
    xs = np.asarray(inputs["x_sample"])
    in_maps = []
    for c in range(8):
        if c < 4:
            in_maps.append(core_inputs(shared, pp_base, cfg, [xp[c]], [xs[2 * c], xs[2 * c + 1]], True))
        else:
            b = 8 + 6 * (c - 4)
            in_maps.append(core_inputs(shared, pp_base, cfg, [xs[b + i] for i in range(4)], [xs[b + 4], xs[b + 5]], False))
    nc = build(cfg)
    res = run_bass_kernel_spmd(nc, in_maps, core_ids=list(range(8)))
    yp = np.empty(xp.shape, np.float32)
    ys = np.empty(xs.shape, np.float32)
    U = cfg.unit
    for c in range(8):
        y = np.asarray(res.results[c]["y"])
        if c < 4:
            yp[c] = y[0:4 * U]
            ys[2 * c] = y[4 * U:5 * U]
            ys[2 * c + 1] = y[5 * U:6 * U]
        else:
            b = 8 + 6 * (c - 4)
            for i in range(6):
                ys[b + i] = y[i * U:(i + 1) * U]
    return (yp, ys)
```
